# Optimizing a Trainium2 kernel written in Bass

```python
import jax, jax.numpy as jnp
from jax import lax
import numpy as np

D_MODEL = 1024
BATCH = 8
SEQ = 2048
DEPTH = 1

HEAD_DIM = 64
DIL_CONFIGS = ((128, 1), (512, 4), (2048, 16))
DIL_HEADS = 8
DIL_WIDTH = DIL_HEADS * HEAD_DIM
DIL_BLOCK = 64
NA_HEADS = 8
NA_WIDTH = NA_HEADS * HEAD_DIM
NA_ROWS_MAX = 8
NA_COLS = 16
GRID_W = 64
MEM_LEN = 256
MEM_HEADS = 4
MEM_HEAD_DIM = 128
MEM_WIDTH = MEM_HEADS * MEM_HEAD_DIM
ROPE_THETA = 500000.0
ROPE_DIM = HEAD_DIM // 4
N_BRANCH = 3
BRANCH_WIDTH = 512
EPS = 1e-6
NEG = -1e30

kernel_name = 'hybrid_dilated_neighbourhood_memory_block'


def _in_sizes():
    return ([DIL_WIDTH] * (3 * len(DIL_CONFIGS))
            + [NA_WIDTH] * 3
            + [MEM_WIDTH]
            + [BRANCH_WIDTH] * N_BRANCH
            + [N_BRANCH * D_MODEL])


def _rmsnorm(x, g):
    xf = x.astype(jnp.float32)
    y = xf * lax.rsqrt(jnp.mean(xf * xf, axis=-1, keepdims=True) + EPS)
    return (y * g.astype(jnp.float32)).astype(x.dtype)


def _heads(t, n_heads):
    b, s, w = t.shape
    return t.reshape(b, s, n_heads, w // n_heads).transpose(0, 2, 1, 3)


def _merge_heads(t):
    b, h, s, d = t.shape
    return t.transpose(0, 2, 1, 3).reshape(b, s, h * d)


def _rope_partial(t, pos):
    half = ROPE_DIM // 2
    inv = ROPE_THETA ** (-jnp.arange(half, dtype=jnp.float32) * 2.0 / ROPE_DIM)
    ang = pos[:, None] * inv[None, :]
    cos, sin = jnp.cos(ang), jnp.sin(ang)
    tf = t[..., :ROPE_DIM].astype(jnp.float32)
    t1, t2 = tf[..., :half], tf[..., half:]
    rot = jnp.concatenate([t1 * cos - t2 * sin, t2 * cos + t1 * sin], axis=-1).astype(t.dtype)
    return jnp.concatenate([rot, t[..., ROPE_DIM:]], axis=-1)


def _banded_attention(q, k, v, reach):
    n, L, hd = q.shape
    bq = DIL_BLOCK
    nb = -(-L // bq)
    lp = nb * bq
    qb = jnp.pad(q, ((0, 0), (0, lp - L), (0, 0))).reshape(n, nb, bq, hd)
    kpad = jnp.pad(k, ((0, 0), (bq, lp - L + bq), (0, 0)))
    vpad = jnp.pad(v, ((0, 0), (bq, lp - L + bq), (0, 0)))
    win = jnp.arange(nb)[:, None] * bq + jnp.arange(3 * bq)[None, :]
    kw = kpad[:, win]
    vw = vpad[:, win]
    kpos = win - bq
    qpos = jnp.arange(nb)[:, None] * bq + jnp.arange(bq)[None, :]
    valid = ((kpos[:, None, :] >= 0) & (kpos[:, None, :] < L)
             & (jnp.abs(qpos[:, :, None] - kpos[:, None, :]) <= reach))
    s = jnp.einsum('nbqd,nbkd->nbqk', qb, kw, preferred_element_type=jnp.float32) * (hd ** -0.5)
    s = jnp.where(valid[None], s, NEG)
    m = jnp.max(s, axis=-1, keepdims=True)
    p = jnp.exp(s - m)
    den = jnp.sum(p, axis=-1)
    out = jnp.einsum('nbqk,nbkd->nbqd', p, vw.astype(jnp.float32)) / den[..., None]
    lse = m[..., 0] + jnp.log(den)
    return out.reshape(n, lp, hd)[:, :L], lse.reshape(n, lp)[:, :L]


def _dilated_attention(q, k, v, dilation, reach):
    b, h, s, hd = q.shape
    mlen = s // dilation

    def fold(t):
        return t.reshape(b, h, mlen, dilation, hd).transpose(0, 1, 3, 2, 4).reshape(b * h * dilation, mlen, hd)

    out, lse = _banded_attention(fold(q), fold(k), fold(v), reach)
    out = out.reshape(b, h, dilation, mlen, hd).transpose(0, 1, 3, 2, 4).reshape(b, h, s, hd)
    lse = lse.reshape(b, h, dilation, mlen).transpose(0, 1, 3, 2).reshape(b, h, s)
    return out, lse


def _neighbourhood_attention(q, k, v, rpb):
    b, h, s, hd = q.shape
    rows = s // GRID_W
    kr = min(NA_ROWS_MAX, rows)
    q5 = q.reshape(b, h, rows, GRID_W, hd)
    k5 = k.reshape(b, h, rows, GRID_W, hd)
    v5 = v.reshape(b, h, rows, GRID_W, hd)
    r_ids = jnp.arange(rows)
    r_start = jnp.clip(r_ids - kr // 2, 0, rows - kr)
    row_idx = r_start[:, None] + jnp.arange(kr)[None, :]
    k_rows = k5[:, :, row_idx]
    v_rows = v5[:, :, row_idx]
    c_ids = jnp.arange(GRID_W)
    c_start = jnp.clip(c_ids - NA_COLS // 2, 0, GRID_W - NA_COLS)
    col_mask = (c_ids[None, :] >= c_start[:, None]) & (c_ids[None, :] < c_start[:, None] + NA_COLS)
    dr = row_idx - r_ids[:, None]
    dc = jnp.clip(c_ids[None, :] - c_ids[:, None], -(NA_COLS - 1), NA_COLS - 1)
    bias = rpb[:, dr + NA_ROWS_MAX - 1][..., dc + NA_COLS - 1]
    bias = bias.transpose(0, 1, 3, 2, 4).astype(jnp.float32)
    sc = jnp.einsum('bhrqd,bhrjkd->bhrqjk', q5, k_rows, preferred_element_type=jnp.float32) * (hd ** -0.5)
    sc = jnp.where(col_mask[:, None, :], sc + bias[None], NEG)
    p = jax.nn.softmax(sc, axis=(-2, -1))
    out = jnp.einsum('bhrqjk,bhrjkd->bhrqd', p, v_rows.astype(jnp.float32))
    return out.reshape(b, h, s, hd)


def _cross_attention(q, k, v):
    sc = jnp.einsum('bhqd,bhkd->bhqk', q, k, preferred_element_type=jnp.float32) * (q.shape[-1] ** -0.5)
    p = jax.nn.softmax(sc, axis=-1)
    return jnp.einsum('bhqk,bhkd->bhqd', p, v.astype(jnp.float32))


def setup_inputs(seed: int = 0) -> dict:
    key = jax.random.key(seed)
    ks = jax.random.split(key, 14)
    f32 = jnp.float32
    n_in = int(sum(_in_sizes()))

    def nrm(k, shape, scale):
        return jax.random.normal(k, shape, f32) * scale

    return {
        'x': nrm(ks[0], (BATCH, SEQ, D_MODEL), 1.0),
        'mem': nrm(ks[1], (BATCH, MEM_LEN, D_MODEL), 1.0),
        'pre_norm': 1.0 + nrm(ks[2], (DEPTH, D_MODEL), 0.05),
        'w_in': nrm(ks[3], (DEPTH, D_MODEL, n_in), D_MODEL ** -0.5),
        'merge_bias': nrm(ks[4], (DEPTH, N_BRANCH, D_MODEL), 0.1),
        'na_rpb': nrm(ks[5], (DEPTH, NA_HEADS, 2 * NA_ROWS_MAX - 1, 2 * NA_COLS - 1), 0.1),
        'mem_norm': 1.0 + nrm(ks[6], (DEPTH, D_MODEL), 0.05),
        'w_mem_kv': nrm(ks[7], (DEPTH, D_MODEL, 2 * MEM_WIDTH), D_MODEL ** -0.5),
        'w_branch_a': nrm(ks[8], (DEPTH, BRANCH_WIDTH, D_MODEL), BRANCH_WIDTH ** -0.5),
        'w_branch_b': nrm(ks[9], (DEPTH, BRANCH_WIDTH, D_MODEL), BRANCH_WIDTH ** -0.5),
        'w_branch_c': nrm(ks[10], (DEPTH, BRANCH_WIDTH, D_MODEL), BRANCH_WIDTH ** -0.5),
        'w_out': nrm(ks[11], (DEPTH, D_MODEL, D_MODEL), D_MODEL ** -0.5),
        'post_norm': 1.0 + nrm(ks[12], (DEPTH, D_MODEL), 0.05),
    }


def reference(x, mem, pre_norm, w_in, merge_bias, na_rpb, mem_norm, w_mem_kv,
              w_branch_a, w_branch_b, w_branch_c, w_out, post_norm):
    b, s, _ = x.shape
    pos = jnp.arange(s, dtype=jnp.float32)
    split_at = np.cumsum(_in_sizes())[:-1].tolist()
    n_dil = len(DIL_CONFIGS)
    off = 3 * n_dil
    for layer in range(DEPTH):
        h = _rmsnorm(x, pre_norm[layer])
        parts = jnp.split(h @ w_in[layer], split_at, axis=-1)

        outs, lses = [], []
        for g, (window, dilation) in enumerate(DIL_CONFIGS):
            q = _rope_partial(_heads(parts[3 * g], DIL_HEADS), pos)
            k = _rope_partial(_heads(parts[3 * g + 1], DIL_HEADS), pos)
            v = _heads(parts[3 * g + 2], DIL_HEADS)
            o, l = _dilated_attention(q, k, v, dilation, (window // 2) // dilation)
            outs.append(o)
            lses.append(l)
        wts = jax.nn.softmax(jnp.stack(lses, axis=0), axis=0)
        out_a = _merge_heads(jnp.sum(wts[..., None] * jnp.stack(outs, axis=0), axis=0).astype(x.dtype))

        out_b = _merge_heads(_neighbourhood_attention(
            _heads(parts[off], NA_HEADS), _heads(parts[off + 1], NA_HEADS),
            _heads(parts[off + 2], NA_HEADS), na_rpb[layer]).astype(x.dtype))

        kv_m = _rmsnorm(mem, mem_norm[layer]) @ w_mem_kv[layer]
        k_m, v_m = jnp.split(kv_m, 2, axis=-1)
        out_c = _merge_heads(_cross_attention(
            _heads(parts[off + 3], MEM_HEADS), _heads(k_m, MEM_HEADS),
            _heads(v_m, MEM_HEADS)).astype(x.dtype))

        g_a, g_b, g_c = parts[off + 4], parts[off + 5], parts[off + 6]
        gate_logits = parts[off + 7].reshape(b, s, N_BRANCH, D_MODEL) + merge_bias[layer]
        gates = jax.nn.sigmoid(gate_logits.astype(jnp.float32)).astype(x.dtype)
        y = (gates[:, :, 0] * ((out_a * jax.nn.silu(g_a)) @ w_branch_a[layer])
             + gates[:, :, 1] * ((out_b * jax.nn.silu(g_b)) @ w_branch_b[layer])
             + gates[:, :, 2] * ((out_c * jax.nn.silu(g_c)) @ w_branch_c[layer]))
        y = y @ w_out[layer]
        x = x + _rmsnorm(y, post_norm[layer])
    return x
```

```python
import numpy as np
from contextlib import ExitStack
import ml_dtypes
import concourse.bass as bass
import concourse.mybir as mybir
from concourse.bass_utils import run_bass_kernel_spmd

F32 = mybir.dt.float32
BF16 = mybir.dt.bfloat16
AF = mybir.ActivationFunctionType
ALU = mybir.AluOpType
AX = mybir.AxisListType

S = 2048
D = 1024
NIN = 11264
DILS = (1, 4, 16)
OFF_B = 4608
OFF_CQ = 6144
OFF_GA = 6656
OFF_GB = 7168
OFF_GC = 7680
OFF_ML = 8192
EPS = 1e-6
NW = 6


class Sched:
    def __init__(self, nc):
        self.nc = nc
        self.eng = dict(pe=nc.tensor, act=nc.scalar, dve=nc.vector, pool=nc.gpsimd, sp=nc.sync)
        self.sem = {}
        self.cnt = {}
        for e in ("pe", "act", "dve", "pool"):
            self.sem[e] = nc.semaphore("s_" + e).__enter__()
            self.cnt[e] = 0
        self.waited = {e: {} for e in self.eng}
        self.state = {}
        self.dsem = {}

    def _wait(self, e, tok):
        if tok is None:
            return
        name, sem, val, src = tok
        if src == e and e == "pe":
            return
        w = self.waited[e]
        if w.get(name, 0) >= val:
            return
        self.eng[e].wait_ge(sem, val)
        w[name] = val

    def deps(self, e, reads, writes):
        for k in reads:
            st = self.state.get(k)
            if st:
                self._wait(e, st[0])
                if isinstance(k, tuple) and k[0] == "ps":
                    for src, t in st[1].items():
                        if src != e:
                            self._wait(e, t)
        for k in writes:
            st = self.state.get(k)
            if st:
                self._wait(e, st[0])
                for t in st[1].values():
                    self._wait(e, t)

    def commit(self, tok, reads, writes):
        src = tok[3]
        for k in writes:
            self.state[k] = [tok, {}]
        for k in reads:
            if k in writes:
                continue
            st = self.state.setdefault(k, [None, {}])
            st[1][src] = tok

    def op(self, e, fn, reads=(), writes=()):
        self.deps(e, reads, writes)
        ins = fn(self.eng[e])
        self.cnt[e] += 1
        ins.then_inc(self.sem[e], 1)
        tok = ("s_" + e, self.sem[e], self.cnt[e], e)
        self.commit(tok, reads, writes)
        return tok

    def dma(self, e, out, in_, reads=(), writes=(), semkey=None):
        self.deps(e, reads, writes)
        if semkey not in self.dsem:
            self.dsem[semkey] = [self.nc.semaphore("d_" + semkey).__enter__(), 0]
        d = self.dsem[semkey]
        d[1] += 16
        self.eng[e].dma_start(out=out, in_=in_).then_inc(d[0], 16)
        tok = ("d_" + semkey, d[0], d[1], "dma_" + semkey)
        self.commit(tok, reads, writes)
        return tok

    def barrier(self, pe=True):
        toks = [("s_" + e, self.sem[e], self.cnt[e], e) for e in self.sem if self.cnt[e] > 0]
        toks += [("d_" + k, d[0], d[1], "dma_" + k) for k, d in self.dsem.items()]
        for e in self.eng:
            if e == "pe" and not pe:
                continue
            for t in toks:
                if t[3] == e and e == "pe":
                    continue
                self._wait(e, t)


class WQ:
    def __init__(self, sch, ring, specs, first=2):
        self.sch = sch
        self.ring = ring
        self.specs = specs
        self.issued = 0
        self.released = set()
        self.limit = first
        self.extra = ()
        self.try_issue()

    def unlimit(self, extra_reads=()):
        self.limit = None
        self.extra = tuple(extra_reads)
        self.try_issue()
        self.extra = ()

    def try_issue(self):
        n = len(self.ring)
        while self.issued < len(self.specs):
            j = self.issued
            if self.limit is not None and j >= self.limit:
                break
            if j >= n and (j - n) not in self.released:
                break
            slot = j % n
            src, nch = self.specs[j]
            ncols = src.shape[1]
            dst = self.ring[slot][:, 0:nch * ncols].rearrange("p (c n) -> p c n", c=nch)
            self.sch.dma("pool", out=dst, in_=src.rearrange("(c p) n -> p c n", p=128),
                         reads=list(self.extra), writes=[("ring", slot)], semkey="ring%d" % slot)
            self.issued += 1

    def get(self, j):
        assert j < self.issued, (j, self.issued)
        slot = j % len(self.ring)
        src, nch = self.specs[j]
        ncols = src.shape[1]
        v = self.ring[slot][:, 0:nch * ncols].rearrange("p (c n) -> p c n", c=nch)
        return v, ("ring", slot)

    def release(self, j):
        self.released.add(j)
        self.try_issue()


def na_groups():
    groups = []
    tiles = [("e", t, 128 * t, 0, 3, 6 - 2 * t, 1) for t in range(4)]
    groups.append((0, 1, tiles))
    for r0 in (4, 5, 12, 13, 20, 21):
        tiles = []
        for m in range(7):
            kr0 = r0 - 4 + 2 * m
            g_lo, g_hi = max(0, m - 3), min(3, m)
            w0 = 10 - 2 * m + 2 * g_lo
            if r0 % 2 == 0:
                tiles.append(("e", kr0 // 2, 64 * kr0, g_lo, g_hi, w0, 2))
            else:
                tiles.append(("o", (kr0 - 1) // 2, 64 * kr0, g_lo, g_hi, w0, 2))
        groups.append((r0, 2, tiles))
    tiles = [("e", t, 128 * t, 0, 3, 34 - 2 * t, 1) for t in range(12, 16)]
    groups.append((28, 1, tiles))
    return groups


def dil_superblocks(g):
    dil = DILS[g]
    L = S // dil
    nb = L // 128
    sbs = []
    if nb == 1:
        for r0 in range(0, dil, 4):
            units = []
            for rr in range(4):
                r = r0 + rr
                units.append((rr * 128, 128, r * L, [(r * nb + 0, 2, 0)]))
            sbs.append((r0 * L, 512, units))
        return sbs
    for r in range(dil):
        ulist = []
        for qb in range(-1, nb):
            lo, hi = 0, 128
            if qb == -1:
                lo = 64
            if qb == nb - 1:
                hi = 64
            tl = []
            if qb >= 0:
                tl.append((r * nb + qb, 0, lo))
            if qb + 1 < nb:
                tl.append((r * nb + qb + 1, 1, lo))
            ulist.append((r * L + 128 * qb + 64 + lo, hi - lo, tl))
        csz = 5 if nb == 4 else 4
        for i in range(0, len(ulist), csz):
            ch = ulist[i:i + csz]
            pos0 = ch[0][0]
            units = [(u[0] - pos0, u[1], u[0], u[2]) for u in ch]
            width = ch[-1][0] + ch[-1][1] - pos0
            sbs.append((pos0, width, units))
    return sbs


def build(debug=False, phases=("A", "C", "B"), agroups=(0, 1, 2), astop=9):
    nc = bass.Bass("TRN2", target_bir_lowering=False)

    def din(name, shape, dt=F32):
        return nc.dram_tensor(name, list(shape), dt, kind="ExternalInput").ap()

    x = din("x", [S, D])
    mem = din("mem", [256, D])
    w_in = din("w_in", [D, NIN])
    w_mem = din("w_mem", [D, 1024])
    w_br = [din("w_br%d" % b, [512, D]) for b in range(3)]
    w_out = din("w_out", [D, D])
    g_pre = din("g_pre", [128, D])
    g_mem = din("g_mem", [128, D])
    g_post = din("g_post", [128, D])
    mbias = din("mbias", [128, 24])
    rpbg = din("rpbg", [128, 8, 14 * 64])
    c_ident = din("c_ident", [128, 128], BF16)
    c_perm = din("c_perm", [128, 128], BF16)
    c_ropeC = din("c_ropeC", [128, S])
    c_ropeS = din("c_ropeS", [128, S])
    c_mask = din("c_mask", [128, 3, 2, 2, 128], BF16)
    c_colmask = din("c_colmask", [128, 14 * 64], BF16)
    out = nc.dram_tensor("out", [S, D], F32, kind="ExternalOutput").ap()
    dbg = {}
    if debug:
        dbg["hT"] = nc.dram_tensor("dbg_hT", [128, 8, S], BF16, kind="ExternalOutput").ap()
        dbg["za"] = nc.dram_tensor("dbg_za", [128, 4, S], BF16, kind="ExternalOutput").ap()
        dbg["zb"] = nc.dram_tensor("dbg_zb", [128, 4, S], BF16, kind="ExternalOutput").ap()
        dbg["zc"] = nc.dram_tensor("dbg_zc", [128, 4, S], BF16, kind="ExternalOutput").ap()
        dbg["yT"] = nc.dram_tensor("dbg_yT", [128, 8, S], BF16, kind="ExternalOutput").ap()

    sch = Sched(nc)

    uniq = [0]

    def sb(name, shape, dt):
        uniq[0] += 1
        return nc.sbuf_tensor("%s_%d" % (name, uniq[0]), list(shape), dt)

    hT = sb("hT", [128, 8, S], BF16).__enter__()
    za = sb("za", [128, 4, S], BF16).__enter__()
    ring = [sb("ring%d" % i, [128, 4096], BF16).__enter__() for i in range(NW)]
    memT = sb("memT", [128, 8, 256], BF16).__enter__()
    ident = sb("ident", [128, 128], BF16).__enter__()
    ones = sb("ones", [128, 128], BF16).__enter__()
    zeros = sb("zeros", [128, 256], BF16).__enter__()
    epsT = sb("epsT", [128, 1], F32).__enter__()
    mb_sb = sb("mb_sb", [128, 24], F32).__enter__()
    psall = nc.psum_tensor("psall", [128, 8 * 512], F32).__enter__()

    class _Bank:
        def __init__(self, i):
            self.i = i

        def __getitem__(self, key):
            return psall[:, self.i * 512:(self.i + 1) * 512][key]
    banks = [_Bank(i) for i in range(8)]

    def bkey(i):
        return ("ps", i)

    specs = []
    idx = {}

    def add(name, ap, nch):
        idx[name] = len(specs)
        specs.append((ap, nch))

    if "A" in phases:
        for hf in range(2):
            for g in agroups:
                for s, sn in ((2, "v"), (0, "q"), (1, "k")):
                    c0 = 512 * (3 * g + s) + 256 * hf
                    add("A%d%d%s" % (hf, g, sn), w_in[:, c0:c0 + 256], 8)
            add("GA%d" % hf, w_in[:, OFF_GA + 256 * hf:OFF_GA + 256 * hf + 256], 8)
    if "C" in phases:
        add("KM", w_mem[:, 0:512], 8)
        add("VM", w_mem[:, 512:1024], 8)
        add("CQ", w_in[:, OFF_CQ:OFF_CQ + 512], 8)
        add("GC", w_in[:, OFF_GC:OFF_GC + 512], 8)
    if "B" in phases:
        for hf in range(2):
            for s, sn in ((2, "v"), (0, "q"), (1, "k")):
                c0 = OFF_B + 512 * s + 256 * hf
                add("B%d%s" % (hf, sn), w_in[:, c0:c0 + 256], 8)
            add("GB%d" % hf, w_in[:, OFF_GB + 256 * hf:OFF_GB + 256 * hf + 256], 8)
    for nh in range(2):
        for b in range(3):
            c0 = OFF_ML + 1024 * b + 512 * nh
            add("ML%d%d" % (nh, b), w_in[:, c0:c0 + 512], 8)
    for nh in range(2):
        add("WO%d" % nh, w_out[:, 512 * nh:512 * nh + 512], 8)

    sch.dma("sp", out=ident[:, :], in_=c_ident[:, :], writes=["ident"], semkey="const1")
    sch.dma("sp", out=mb_sb[:, :], in_=mbias[:, :], writes=["mb"], semkey="const2")
    sch.op("dve", lambda e: e.memset(ones[:, :], 1.0), writes=["ones"])
    sch.op("dve", lambda e: e.memset(zeros[:, :], 0.0), writes=["zeros"])
    sch.op("dve", lambda e: e.memset(epsT[:, :], EPS), writes=["eps"])

    wq = WQ(sch, ring, specs)

    def norm_tile(src_dram_rows, gbc, xs, sq, ss, rs, rstd, hb, skey, gkey="gbc", k2="", part=None, dq="sp"):
        if part in (None, -1):
            sch.dma(dq, out=xs[:, :], in_=src_dram_rows, writes=[skey + "xs"], semkey=skey + "xs")
        if part in (None, 0, 0.1):
            sch.op("act", lambda e: e.activation(out=sq[:, :], in_=xs[:, :], func=AF.Square,
                                                 accum_out=ss[:, 0:1]),
                   reads=[skey + "xs"], writes=["sq" + k2, "ss" + k2])
        if part in (None, 1):
            sch.op("act", lambda e: e.activation(out=rs[:, 0:1], in_=ss[:, 0:1], func=AF.Sqrt,
                                                 bias=epsT[:, 0:1], scale=1.0 / D),
                   reads=["ss" + k2, "eps"], writes=["rs" + k2])
            sch.op("dve", lambda e: e.reciprocal(out=rstd[:, 0:1], in_=rs[:, 0:1]), reads=["rs" + k2],
                   writes=["rstd" + k2])
            sch.op("dve", lambda e: e.scalar_tensor_tensor(out=hb[:, :], in0=xs[:, :], scalar=rstd[:, 0:1],
                                                           in1=gbc[:, :], op0=ALU.mult, op1=ALU.mult),
                   reads=[skey + "xs", "rstd" + k2, gkey], writes=[skey + "hb"])

    def transpose8(hb, hbkey, dst, dkey, bank_i, evac="act"):
        bT = banks[bank_i][:, :].bitcast(BF16)

        def f(pe):
            for c in range(8):
                m = pe.transpose(out=bT[:, c * 128:(c + 1) * 128], in_=hb[:, c * 128:(c + 1) * 128],
                                 identity=ident[:, :])
            return m
        sch.op("pe", f, reads=[hbkey, "ident"], writes=[bkey(bank_i)])
        if evac == "act":
            sch.op("act", lambda e: e.activation(out=dst, in_=bT.rearrange("p (c t) -> p c t", c=8), func=AF.Copy),
                   reads=[bkey(bank_i)], writes=[dkey])
        else:
            sch.op("dve", lambda e: e.tensor_copy(out=dst, in_=bT.rearrange("p (c t) -> p c t", c=8)),
                   reads=[bkey(bank_i)], writes=[dkey])

    pring = [0]

    def next_bank(lst):
        b = lst[pring[0] % len(lst)]
        pring[0] += 1
        return b

    def proj_fm(wv, wkey, col0, rhs_fn, bank_i, nchunk=8, ncols=128, extra_reads=()):
        def f(pe):
            for c in range(nchunk):
                r = rhs_fn(c)
                n_ = int(np.prod(r.shape[1:]))
                m = pe.matmul(banks[bank_i][0:ncols, 0:n_],
                              lhsT=wv[:, c, col0:col0 + ncols], rhs=r, start=(c == 0), stop=(c == nchunk - 1))
            return m
        return sch.op("pe", f, reads=[wkey] + list(extra_reads), writes=[bkey(bank_i)])

    def pipeline(jobs, lag):
        pend = []
        for s1, s2 in jobs:
            s1()
            pend.append(s2)
            if len(pend) > lag:
                pend.pop(0)()
        for f in pend:
            f()

    def pipeline_n(jobs, lags):
        ns = len(jobs[0])
        offs = [0]
        for l in lags:
            offs.append(offs[-1] + l)
        n = len(jobs)
        for it in range(n + offs[-1]):
            for s in range(ns):
                j = it - offs[s]
                if 0 <= j < n:
                    jobs[j][s]()

    vA_cm = sb("vA", [128, 16, 256], BF16)
    vA = vA_cm.__enter__()
    with ExitStack() as es_:
        gbc = es_.enter_context(sb("gbc", [128, D], F32))
        gbm = es_.enter_context(sb("gbm", [128, D], F32))
        NX = 6
        xss = [es_.enter_context(sb("xs", [128, D], F32)) for _ in range(NX)]
        sqs = [es_.enter_context(sb("sq", [128, D], F32)) for _ in range(3)]
        sss = [es_.enter_context(sb("ss", [128, 1], F32)) for _ in range(3)]
        rss = [es_.enter_context(sb("rs", [128, 1], F32)) for _ in range(3)]
        rstds = [es_.enter_context(sb("rstd", [128, 1], F32)) for _ in range(3)]
        hbs = [es_.enter_context(sb("hb", [128, D], BF16)) for _ in range(6)]

        early_v = ("A" in phases) and agroups[0] == 0

        def early_v_proj(kp):
            wvv, wvk = wq.get(idx["A00v"])
            bi = (0, 1, 2, 3)[kp % 4]

            def fv(pe):
                for u in range(2):
                    kt = 2 * kp + u
                    for c in range(8):
                        m = pe.matmul(banks[bi][:, u * 256:(u + 1) * 256], lhsT=hT[:, c, kt * 128:(kt + 1) * 128],
                                      rhs=wvv[:, c, :], start=(c == 0), stop=(c == 7))
                return m
            sch.op("pe", fv, reads=[wvk, ("hT", 2 * kp), ("hT", 2 * kp + 1)], writes=[bkey(bi)])
            sch.op("pool" if False else "dve", lambda e: e.tensor_copy(
                out=vA[:, 2 * kp:2 * kp + 2, :], in_=banks[bi][:, :].rearrange("p (u n) -> p u n", u=2)),
                reads=[bkey(bi)], writes=[("vA", kp)])

        def p0_job(tt):
            i2 = tt % 2
            i3 = tt % 3
            if tt < 16:
                src, g_, gk, dst, dk = x[tt * 128:(tt + 1) * 128, :], gbc, "gbc", hT[:, :, tt * 128:(tt + 1) * 128], ("hT", tt)
            else:
                mt = tt - 16
                src, g_, gk, dst, dk = mem[mt * 128:(mt + 1) * 128, :], gbm, "gbm", memT[:, :, mt * 128:(mt + 1) * 128], ("memT", mt)

            ix = tt % NX
            args = (src, g_, xss[ix], sqs[i3], sss[i3], rss[i3], rstds[i3], hbs[ix], "p0_%d" % ix)

            def sl():
                norm_tile(*args, gkey=gk, k2="_%d" % i3, part=-1, dq=("sp" if tt % 2 == 0 else "pool"))
                if tt == 1:
                    sch.dma("sp", out=gbc[:, :], in_=g_pre[:, :], writes=["gbc"], semkey="const3")
                if tt == 12:
                    sch.dma("sp", out=gbm[:, :], in_=g_mem[:, :], writes=["gbm"], semkey="const8")
                if tt == 15:
                    wq.unlimit(extra_reads=["p0_%dxs" % ix])

            def s0a():
                norm_tile(*args, gkey=gk, k2="_%d" % i3, part=0.1)

            def s0b():
                norm_tile(*args, gkey=gk, k2="_%d" % i3, part=0.2)

            def s1():
                norm_tile(*args, gkey=gk, k2="_%d" % i3, part=1)

            def s2():
                transpose8(hbs[ix], "p0_%dhb" % ix, dst, dk, 6 + i2, evac=("act" if tt % 2 == 0 else "dve"))
                if early_v and 3 <= tt < 18 and tt % 2 == 1:
                    early_v_proj((tt - 3) // 2)
            return sl, s0a, s0b, s1, s2
        pipeline_n([p0_job(tt) for tt in range(18)], [2, 1, 1, 1])
        sch.barrier()
    if debug:
        sch.dma("sp", out=dbg["hT"][:, :, :], in_=hT[:, :, :], semkey="dbg")
        sch.barrier()

    if "A" in phases:
        with ExitStack() as es_:
            accn = es_.enter_context(sb("accn", [128, 2, S], F32))
            accd = es_.enter_context(sb("accd", [128, 2, S], F32))
            qT = es_.enter_context(sb("qT", [128, 2, S], BF16))
            kT = es_.enter_context(sb("kT", [128, 2, S], BF16))
            ropeC = es_.enter_context(sb("ropeC", [128, S], F32))
            ropeS = es_.enter_context(sb("ropeS", [128, S], F32))
            perm = es_.enter_context(sb("perm", [128, 128], BF16))
            maskA = es_.enter_context(sb("maskA", [128, 3 * 512], BF16))
            NQB = 3
            qbs = [es_.enter_context(sb("qb", [128, 512], BF16)) for _ in range(NQB)]
            t1s = [es_.enter_context(sb("t1", [128, 512], F32)) for _ in range(3)]
            t2s = [es_.enter_context(sb("t2", [128, 512], F32)) for _ in range(3)]
            NPB = 8
            pAs = [es_.enter_context(sb("pA", [128, 512], BF16)) for _ in range(NPB)]
            pms = [es_.enter_context(sb("pm", [128, 512], BF16)) for _ in range(NPB)]
            sch.dma("sp", out=ropeC[:, :], in_=c_ropeC[:, :], writes=["ropeC"], semkey="const4")
            sch.dma("sp", out=ropeS[:, :], in_=c_ropeS[:, :], writes=["ropeS"], semkey="const5")
            sch.dma("sp", out=perm[:, :], in_=c_perm[:, :], writes=["perm"], semkey="const6")
            sch.dma("sp", out=maskA[:, :], in_=c_mask.rearrange("k a h t q -> k (a h t q)"), writes=["maskA"],
                    semkey="const7")
            WIDE = [0, 1, 2, 3, 4, 5]
            cq = [0]
            ct = [0]
            cu = [0]
            csb = [0]
            accq = []
            spc = [0]
            rc = [0]
            ROPEB = [2, 3, 4, 5, 6, 7, 0, 1]

            def rope_job(wv_, wk_, dstT, dn, jj, tb, dil):
                iq = cq[0] % NQB
                cq[0] += 1
                it = ct[0] % 3
                ct[0] += 1
                qb = qbs[iq]
                t1 = t1s[it]
                t2 = t2s[it]
                st = {}

                def s1():
                    bx = ROPEB[rc[0] % 8]
                    rc[0] += 1
                    st["bx"] = bx
                    proj_fm(wv_, wk_, jj * 128, lambda c: hT[:, c, tb * 512:(tb + 1) * 512], bx)
                    sch.op("act", lambda e: e.activation(out=qb[:, :], in_=banks[bx][:, :], func=AF.Copy),
                           reads=[bkey(bx)], writes=[("qb", iq)])

                def s2():
                    bx = st["bx"]
                    by = ROPEB[rc[0] % 8]
                    rc[0] += 1
                    sch.op("pe", lambda pe: pe.matmul(banks[by][:, :], lhsT=perm[:, :], rhs=qb[:, :],
                                                      start=True, stop=True),
                           reads=[("qb", iq), "perm"], writes=[bkey(by)])
                    sch.op("dve", lambda e: e.tensor_tensor(out=t1[:, :], in0=banks[bx][:, :],
                                                            in1=ropeC[:, tb * 512:(tb + 1) * 512], op=ALU.mult),
                           reads=[bkey(bx), "ropeC"], writes=[("t1", it)])
                    sch.op("dve", lambda e: e.tensor_tensor(out=t2[:, :], in0=banks[by][:, :],
                                                            in1=ropeS[:, tb * 512:(tb + 1) * 512], op=ALU.mult),
                           reads=[bkey(by), "ropeS"], writes=[("t2", it)])
                    npb = 512 // dil
                    dst = dstT[:, jj, tb * 512:(tb + 1) * 512]
                    i0, i1 = t1[:, :], t2[:, :]
                    sch.op(("pool" if (tb % 2 == 0) else "dve"),
                           lambda e: e.tensor_tensor(out=dst, in0=i0, in1=i1, op=ALU.add),
                           reads=[("t1", it), ("t2", it)], writes=[(dn, jj, tb)])
                return s1, s2

            def unit_job(g, jj, unit, acc_after, nb_, db_):
                (colofs, nq, qpos, tl) = unit
                ip = cu[0] % NPB
                if g == 0:
                    meng = "dve" if (cu[0] % 4) != 3 else "pool"
                else:
                    meng = "dve" if (cu[0] % 2) == 0 else "pool"
                cu[0] += 1
                pA = pAs[ip]
                pm = pms[ip]
                lo = tl[0][2]
                nt = len(tl)
                dil_ = DILS[g]
                L_ = S // dil_
                npb_ = 512 // dil_

                def tok_of(p):
                    return dil_ * (p % L_) + p // L_

                def tbs_of(p0, n):
                    t0_ = tok_of(p0)
                    return range(t0_ // 512, (t0_ + dil_ * (n - 1)) // 512 + 1)
                q0 = tok_of(qpos)
                qkeys = [("qT", jj, t) for t in tbs_of(qpos, nq)]
                for (kt_, mk_, lo__) in tl:
                    qkeys += [("kT", jj, t) for t in tbs_of(kt_ * 128, 128)]
                vkeys_ = sorted(set(("vA", kt_ // 2) for (kt_, mk_, lo__) in tl))

                def s1():
                    bs = 2 if spc[0] < 2 else (4 if spc[0] % 2 == 0 else 2)
                    spc[0] += 1

                    def fs(pe):
                        for ti, (kt, mk, lo_) in enumerate(tl):
                            for h in range(2):
                                kt0 = tok_of(kt * 128)
                                m = pe.matmul(banks[bs + h][:, ti * 128:ti * 128 + nq],
                                              lhsT=kT[h * 64:(h + 1) * 64, jj, kt0:kt0 + dil_ * 127 + 1:dil_],
                                              rhs=qT[h * 64:(h + 1) * 64, jj, q0:q0 + dil_ * (nq - 1) + 1:dil_],
                                              start=True, stop=True)
                        return m
                    sch.op("pe", fs, reads=qkeys, writes=[bkey(bs), bkey(bs + 1)])
                    sv = psall[:, bs * 512:(bs + 2) * 512].rearrange("p (h x) -> p h x", h=2)[:, :, 0:256]
                    sv = sv.rearrange("p h (t q) -> p h t q", t=2)[:, :, 0:nt, 0:nq]
                    pv = pA[:, :].rearrange("p (h t q) -> p h t q", h=2, t=2)[:, :, 0:nt, 0:nq]
                    pmv = pm[:, :].rearrange("p (h t q) -> p h t q", h=2, t=2)[:, :, 0:nt, 0:nq]
                    sch.op("act", lambda e: e.activation(out=pv, in_=sv, func=AF.Exp, scale=0.125),
                           reads=[bkey(bs), bkey(bs + 1)], writes=[("pA", ip)])
                    mk0 = tl[0][1]
                    mv = maskA[:, mk0 * 512:(mk0 + 1) * 512].rearrange("p (h t q) -> p h t q", h=2, t=2)
                    if mk0 == 1:
                        mv = mv[:, :, 1:1 + nt, lo:lo + nq]
                    else:
                        mv = mv[:, :, 0:nt, lo:lo + nq]
                    sch.op(meng, lambda e: e.tensor_tensor(out=pmv, in0=pv, in1=mv, op=ALU.mult),
                           reads=[("pA", ip), "maskA"], writes=[("pm", ip)])

                def s2():
                    def fpv(pe):
                        for ti, (kt, mk, lo_) in enumerate(tl):
                            for h in range(2):
                                pe.matmul(banks[nb_][h * 64:(h + 1) * 64, colofs:colofs + nq],
                                          lhsT=vA[:, kt, (2 * jj + h) * 64:(2 * jj + h + 1) * 64],
                                          rhs=pm[:, (h * 2 + ti) * 128:(h * 2 + ti) * 128 + nq],
                                          start=(ti == 0), stop=(ti == nt - 1))
                        for ti, (kt, mk, lo_) in enumerate(tl):
                            for h in range(2):
                                m = pe.matmul(banks[db_][h * 64:(h + 1) * 64, colofs:colofs + nq],
                                              lhsT=ones[:, 0:64],
                                              rhs=pm[:, (h * 2 + ti) * 128:(h * 2 + ti) * 128 + nq],
                                              start=(ti == 0), stop=(ti == nt - 1))
                        return m
                    keep = []
                    for ent in accq:
                        ent[0] += 1
                        if ent[0] >= 2 or ent[1] == nb_:
                            ent[2]()
                        else:
                            keep.append(ent)
                    accq[:] = keep
                    sch.op("pe", fpv, reads=[("pm", ip), "ones"] + vkeys_,
                           writes=[bkey(nb_), bkey(db_)])
                    if acc_after is not None:
                        accq.append([0, nb_, acc_after])
                return s1, s2

            def make_acc(g, jj, pos0, width, first, nb_, db_):
                dil = DILS[g]
                L = S // dil

                def f():
                    if dil == 1:
                        an = accn[:, jj, pos0:pos0 + width]
                        ad = accd[:, jj, pos0:pos0 + width]
                        bn = banks[nb_][:, 0:width]
                        bd = banks[db_][:, 0:width]
                    elif dil == 4:
                        r = pos0 // L
                        an = accn[:, jj, r::4]
                        ad = accd[:, jj, r::4]
                        bn = banks[nb_][:, 0:512]
                        bd = banks[db_][:, 0:512]
                    else:
                        r0 = pos0 // L
                        an = accn[:, jj, :].rearrange("p (m r) -> p r m", r=16)[:, r0:r0 + 4, :]
                        ad = accd[:, jj, :].rearrange("p (m r) -> p r m", r=16)[:, r0:r0 + 4, :]
                        bn = banks[nb_][:, :].rearrange("p (r m) -> p r m", r=4)
                        bd = banks[db_][:, :].rearrange("p (r m) -> p r m", r=4)
                    if first:
                        sch.op("act", lambda e: e.activation(out=an, in_=bn, func=AF.Copy),
                               reads=[bkey(nb_)], writes=[("accn", jj)])
                        sch.op("act", lambda e: e.activation(out=ad, in_=bd, func=AF.Copy),
                               reads=[bkey(db_)], writes=[("accd", jj)])
                    else:
                        sch.op("dve", lambda e: e.tensor_tensor(out=an, in0=bn, in1=an, op=ALU.add),
                               reads=[bkey(nb_), ("accn", jj)], writes=[("accn", jj)])
                        sch.op("dve", lambda e: e.tensor_tensor(out=ad, in0=bd, in1=ad, op=ALU.add),
                               reads=[bkey(db_), ("accd", jj)], writes=[("accd", jj)])
                return f

            def finalize_pair(hf, jj, wgv, wgk, tbs=(0, 1, 2, 3)):
                for ent in accq:
                    ent[2]()
                accq[:] = []
                for tb in tbs:
                    it = ct[0] % 3
                    ct[0] += 1
                    t1 = t1s[it]
                    t2 = t2s[it]
                    bg = next_bank([2, 3, 4, 5])
                    proj_fm(wgv, wgk, jj * 128, lambda c: hT[:, c, tb * 512:(tb + 1) * 512], bg)
                    sl = slice(tb * 512, (tb + 1) * 512)
                    sch.op("dve", lambda e: e.reciprocal(out=t1[:, :], in_=accd[:, jj, sl]),
                           reads=[("accd", jj), ("t1", it)], writes=[("t1", it)])
                    sch.op("pool", lambda e: e.tensor_tensor(out=t1[:, :], in0=t1[:, :], in1=accn[:, jj, sl],
                                                             op=ALU.mult),
                           reads=[("accn", jj), ("t1", it)], writes=[("t1", it)])
                    sch.op("act", lambda e: e.activation(out=t2[:, :], in_=banks[bg][:, :], func=AF.Silu),
                           reads=[bkey(bg), ("t2", it)], writes=[("t2", it)])
                    sch.op("pool", lambda e: e.tensor_tensor(out=za[:, 2 * hf + jj, sl], in0=t1[:, :],
                                                             in1=t2[:, :], op=ALU.mult),
                           reads=[("t1", it), ("t2", it)], writes=[("za", 2 * hf + jj, tb)])

            for hf in range(2):
                for g in agroups:
                    dil = DILS[g]
                    L = S // dil
                    wqv, wqk = wq.get(idx["A%d%dq" % (hf, g)])
                    wkv, wkk = wq.get(idx["A%d%dk" % (hf, g)])
                    wvv, wvk = wq.get(idx["A%d%dv" % (hf, g)])
                    nbt = L // 128
                    skipv = (hf == 0 and g == 0 and agroups[0] == 0)
                    for kp in range(0 if skipv else 8):
                        bi = next_bank(WIDE)

                        def fv(pe):
                            for u in range(2):
                                kt = 2 * kp + u
                                r, mb = kt // nbt, kt % nbt
                                st_ = dil * 128 * mb + r
                                for c in range(8):
                                    m = pe.matmul(banks[bi][:, u * 256:(u + 1) * 256],
                                                  lhsT=hT[:, c, st_:st_ + dil * 127 + 1:dil], rhs=wvv[:, c, :],
                                                  start=(c == 0), stop=(c == 7))
                            return m
                        sch.op("pe", fv, reads=[wvk], writes=[bkey(bi)])
                        sch.op("act", lambda e: e.activation(
                            out=vA[:, 2 * kp:2 * kp + 2, :], in_=banks[bi][:, :].rearrange("p (u n) -> p u n", u=2),
                            func=AF.Copy), reads=[bkey(bi)], writes=[("vA", kp)])
                    wq.release(idx["A%d%dv" % (hf, g)])
                    jobs = []
                    rc[0] = 0
                    for jj in range(2 if astop >= 2 else 0):
                        for (wv_, wk_, dstT, dn) in ((wqv, wqk, qT, "qT"), (wkv, wkk, kT, "kT")):
                            for tb in range(4):
                                jobs.append(rope_job(wv_, wk_, dstT, dn, jj, tb, dil))
                    jobs_r = jobs
                    csb[0] = 0
                    spc[0] = 0
                    sbs = dil_superblocks(g)
                    lastg = (g == agroups[-1])
                    fin_tb = {0: set(), 1: set()}
                    if lastg:
                        wgv, wgk = wq.get(idx["GA%d" % hf])
                    jobs = []
                    for jj in range(2 if astop >= 3 else 0):
                        for (pos0, width, units) in sbs:
                            nb_, db_ = ((6, 7), (0, 1))[csb[0] % 2]
                            csb[0] += 1
                            for ui, unit in enumerate(units):
                                acc = None
                                if ui == len(units) - 1:
                                    acc = make_acc(g, jj, pos0, width, g == agroups[0], nb_, db_)
                                jobs.append(unit_job(g, jj, unit, acc, nb_, db_))
                            if lastg and astop >= 4 and dil == 1:
                                done_hi = pos0 + width
                                tbs = tuple(t for t in range(4) if (t + 1) * 512 <= done_hi and t not in fin_tb[jj])
                                if tbs:
                                    fin_tb[jj].update(tbs)
                                    jobs.append((lambda: None, (lambda hf=hf, jj=jj, wgv=wgv, wgk=wgk, tbs=tbs:
                                                                finalize_pair(hf, jj, wgv, wgk, tbs))))
                        if lastg and astop >= 4 and dil != 1:
                            jobs.append((lambda: None, (lambda hf=hf, jj=jj, wgv=wgv, wgk=wgk: finalize_pair(hf, jj, wgv, wgk))))
                    jobs_u = jobs

                    def rel_qk(hf=hf, g=g):
                        wq.release(idx["A%d%dq" % (hf, g)])
                        wq.release(idx["A%d%dk" % (hf, g)])
                    nr = len(jobs_r)
                    if nr >= 4 and len(jobs_u) >= 8:
                        for i in range(nr - 2):
                            jobs_r[i][0]()
                            if i > 0:
                                jobs_r[i - 1][1]()
                        jobs_r[nr - 2][0]()
                        jobs_r[nr - 3][1]()
                        jobs_r[nr - 1][0]()
                        rel_qk()
                        jobs_u[0][0]()
                        jobs_r[nr - 2][1]()
                        jobs_u[1][0]()
                        jobs_r[nr - 1][1]()
                        jobs_u[2][0]()
                        pend = [jobs_u[0][1], jobs_u[1][1], jobs_u[2][1]]
                        for j_ in jobs_u[3:]:
                            j_[0]()
                            pend.append(j_[1])
                            if len(pend) > 6:
                                pend.pop(0)()
                        for f_ in pend:
                            f_()
                    else:
                        if jobs_r:
                            pipeline(jobs_r, 1)
                        rel_qk()
                        if jobs_u:
                            pipeline(jobs_u, 3)
                    for ent in accq:
                        ent[2]()
                    accq[:] = []
                wq.release(idx["GA%d" % hf])
            sch.barrier(pe=False)
    else:
        sch.op("dve", lambda e: e.memset(za[:, :, :], 0.0), writes=["za"])
        sch.barrier()

    vA_cm.__exit__(None, None, None)
    zb = sb("zb", [128, 4, S], BF16).__enter__()
    zc = sb("zc", [128, 4, S], BF16).__enter__()

    ebr_cm = sb("ebr", [128, 8, 14 * 64], BF16)
    ebr = ebr_cm.__enter__()
    if "C" in phases:
        with ExitStack() as es_:
            kmT = es_.enter_context(sb("kmT", [128, 4, 256], BF16))
            vm = es_.enter_context(sb("vm", [128, 2, 512], BF16))
            qcT = es_.enter_context(sb("qcT", [128, 4, S], BF16))
            sgc = es_.enter_context(sb("sgc", [128, 4, S], BF16))
            NPC = 5
            pcs = [es_.enter_context(sb("pc", [128, 1024], BF16)) for _ in range(NPC)]
            t1s = [es_.enter_context(sb("t1", [128, 512], F32)) for _ in range(2)]
            colm = es_.enter_context(sb("colm", [128, 14 * 64], BF16))
            rps = [es_.enter_context(sb("rp", [128, 14 * 64], F32)) for _ in range(2)]
            WIDE = [0, 1, 2, 3, 4, 5]
            kmv, kmk = wq.get(idx["KM"])
            for hp in range(2):
                bi = next_bank(WIDE)

                def fk(pe):
                    for u in range(2):
                        h = 2 * hp + u
                        for c in range(8):
                            m = pe.matmul(banks[bi][:, u * 256:(u + 1) * 256], lhsT=kmv[:, c, h * 128:(h + 1) * 128],
                                          rhs=memT[:, c, :], start=(c == 0), stop=(c == 7))
                    return m
                sch.op("pe", fk, reads=[kmk], writes=[bkey(bi)])
                sch.op("act", lambda e: e.activation(out=kmT[:, 2 * hp:2 * hp + 2, :],
                                                     in_=banks[bi][:, :].rearrange("p (u n) -> p u n", u=2),
                                                     func=AF.Copy), reads=[bkey(bi)], writes=[("kmT", hp)])
            wq.release(idx["KM"])
            vmv, vmk = wq.get(idx["VM"])
            for mt in range(2):
                bi = next_bank(WIDE)

                def fvm(pe):
                    for c in range(8):
                        m = pe.matmul(banks[bi][:, :], lhsT=memT[:, c, mt * 128:(mt + 1) * 128], rhs=vmv[:, c, :],
                                      start=(c == 0), stop=(c == 7))
                    return m
                sch.op("pe", fvm, reads=[vmk], writes=[bkey(bi)])
                sch.op("act", lambda e: e.activation(out=vm[:, mt, :], in_=banks[bi][:, :], func=AF.Copy),
                       reads=[bkey(bi)], writes=[("vm", mt)])
            wq.release(idx["VM"])
            def ebr_step(kk):
                if not ("B" in phases):
                    return
                if kk == 0:
                    sch.dma("sp", out=colm[:, :], in_=c_colmask[:, :], writes=["colm"], semkey="const9")
                if kk < 8:
                    rp = rps[kk % 2]
                    sch.dma("sp", out=rp[:, :], in_=rpbg[:, kk, :], writes=[("rp", kk % 2)], semkey="rp%d" % (kk % 2))
                if 1 <= kk <= 8:
                    h = kk - 1
                    rp = rps[h % 2]
                    sch.op("act", lambda e: e.activation(out=rp[:, :], in_=rp[:, :], func=AF.Exp),
                           reads=[("rp", h % 2)], writes=[("rp", h % 2)])
                    sch.op("pool", lambda e: e.tensor_tensor(
                        out=ebr[:, h, :].rearrange("p (w q) -> p w q", w=14),
                        in0=rp[:, :].rearrange("p (w q) -> p w q", w=14),
                        in1=colm[:, :].rearrange("p (w q) -> p w q", w=14), op=ALU.mult),
                        reads=[("rp", h % 2), "colm"], writes=[("ebr", h)])
            cqv, cqk = wq.get(idx["CQ"])
            gcv, gck = wq.get(idx["GC"])
            sc = float(128 ** -0.5)
            cu = [0]

            def c_job(h, tb):
                ip = cu[0] % NPC
                it = cu[0] % 2
                nb_, db_ = ((6, 7), (0, 1))[cu[0] % 2]
                cu[0] += 1
                pc = pcs[ip]
                t1 = t1s[it]
                sl = slice(tb * 512, (tb + 1) * 512)

                jidx = 4 * h + tb

                def s0():
                    if jidx >= 2 and jidx - 2 <= 8:
                        ebr_step(jidx - 2)
                    bi = 4
                    proj_fm(cqv, cqk, h * 128, lambda c: hT[:, c, sl], bi)
                    sch.op("act", lambda e: e.activation(out=qcT[:, h, sl], in_=banks[bi][:, :], func=AF.Copy),
                           reads=[bkey(bi)], writes=[("qcT", h, tb)])
                    bi2 = 5
                    proj_fm(gcv, gck, h * 128, lambda c: hT[:, c, sl], bi2)
                    sch.op("act", lambda e: e.activation(out=sgc[:, h, sl], in_=banks[bi2][:, :], func=AF.Silu),
                           reads=[bkey(bi2)], writes=[("sgc", h, tb)])

                def s1():
                    bs = 2

                    def fs(pe):
                        for mt in range(2):
                            m = pe.matmul(banks[bs + mt][:, :], lhsT=kmT[:, h, mt * 128:(mt + 1) * 128],
                                          rhs=qcT[:, h, sl], start=True, stop=True)
                        return m
                    sch.op("pe", fs, reads=[("kmT", h // 2), ("qcT", h, tb)], writes=[bkey(bs), bkey(bs + 1)])
                    sch.op("act", lambda e: e.activation(out=pc[:, :], in_=psall[:, bs * 512:(bs + 2) * 512],
                                                         func=AF.Exp, scale=sc),
                           reads=[bkey(bs), bkey(bs + 1)], writes=[("pc", ip)])

                def s2():
                    def fpv(pe):
                        for mt in range(2):
                            pe.matmul(banks[nb_][:, :], lhsT=vm[:, mt, h * 128:(h + 1) * 128],
                                      rhs=pc[:, mt * 512:(mt + 1) * 512], start=(mt == 0), stop=(mt == 1))
                        for mt in range(2):
                            m = pe.matmul(banks[db_][:, :], lhsT=ones[:, :], rhs=pc[:, mt * 512:(mt + 1) * 512],
                                          start=(mt == 0), stop=(mt == 1))
                        return m
                    sch.op("pe", fpv, reads=[("pc", ip), ("vm", 0), ("vm", 1), "ones"],
                           writes=[bkey(nb_), bkey(db_)])
                    sch.op("dve", lambda e: e.reciprocal(out=t1[:, :], in_=banks[db_][:, :]),
                           reads=[bkey(db_), ("t1", it)], writes=[("t1", it)])
                    sch.op("dve", lambda e: e.tensor_tensor(out=t1[:, :], in0=banks[nb_][:, :], in1=t1[:, :],
                                                            op=ALU.mult),
                           reads=[bkey(nb_), ("t1", it)], writes=[("t1", it)])
                    sch.op("pool", lambda e: e.tensor_tensor(out=zc[:, h, sl], in0=t1[:, :], in1=sgc[:, h, sl],
                                                             op=ALU.mult),
                           reads=[("t1", it), ("sgc", h, tb)], writes=[("zc", h, tb)])
                return s0, s1, s2
            pipeline_n([c_job(h, tb) for h in range(4) for tb in range(4)], [1, 3])
            wq.release(idx["CQ"])
            wq.release(idx["GC"])
            sch.barrier(pe=False)
    else:
        sch.op("dve", lambda e: e.memset(zc[:, :, :], 0.0), writes=["zc"])
        sch.barrier()

    if "B" in phases:
        with ExitStack() as es_:
            qT = es_.enter_context(sb("qT", [128, 2, S], BF16))
            kT = es_.enter_context(sb("kT", [128, 2, S], BF16))
            vE = es_.enter_context(sb("vE", [128, 16, 256], BF16))
            vO = es_.enter_context(sb("vO", [128, 16, 256], BF16))
            sgb = es_.enter_context(sb("sgb", [128, 2, S], BF16))
            NPB = 8
            pBs = [es_.enter_context(sb("pB", [128, 512], BF16)) for _ in range(NPB)]
            pns = [es_.enter_context(sb("pn", [128, 512], BF16)) for _ in range(NPB)]
            t1s = [es_.enter_context(sb("t1", [128, 256], F32)) for _ in range(2)]
            groups = na_groups()
            rcb = [0]

            def nextb():
                b_ = [2, 3, 4, 5, 6, 7, 0, 1][rcb[0] % 8]
                rcb[0] += 1
                return b_
            cu = [0]
            cg = [0]
            sc = 0.125
            vkeys = [("v", p, k_) for p in range(2) for k_ in range(8)]

            def tile_job(jj, pair, r0, step, tile, first, last, nb_, db_, fin, tidx=0):
                (par, vidx, ktok, g_lo, g_hi, w0, wstep) = tile
                ip = cu[0] % NPB
                meng = "pool" if tidx in (1, 2) else "dve"
                cu[0] += 1
                pB = pBs[ip]
                pn = pns[ip]
                ng = g_hi - g_lo + 1
                qkeys = [("qT", jj, t) for t in range(4)] + [("kT", jj, t) for t in range(4)]

                def s1():
                    bs = next_bank([2, 4])

                    def fs(pe):
                        for h in range(2):
                            qv = qT[h * 64:(h + 1) * 64, jj, :].rearrange("p (r q) -> p r q", q=64)
                            rr0 = r0 + step * g_lo
                            qv = qv[:, rr0:rr0 + step * (ng - 1) + 1:step, :]
                            m = pe.matmul(banks[bs + h][:, 64 * g_lo:64 * (g_hi + 1)],
                                          lhsT=kT[h * 64:(h + 1) * 64, jj, ktok:ktok + 128], rhs=qv,
                                          start=True, stop=True)
                        return m
                    sch.op("pe", fs, reads=qkeys, writes=[bkey(bs), bkey(bs + 1)])
                    sv = psall[:, bs * 512:(bs + 2) * 512].rearrange("p (h x) -> p h x", h=2)[:, :, 0:256]
                    sv = sv.rearrange("p h (g q) -> p h g q", g=4)[:, :, g_lo:g_hi + 1, :]
                    pv = pB[:, :].rearrange("p (h g q) -> p h g q", h=2, g=4)[:, :, g_lo:g_hi + 1, :]
                    pnv = pn[:, :].rearrange("p (h g q) -> p h g q", h=2, g=4)[:, :, g_lo:g_hi + 1, :]
                    sch.op("act", lambda e: e.activation(out=pv, in_=sv, func=AF.Exp, scale=sc),
                           reads=[bkey(bs), bkey(bs + 1)], writes=[("pB", ip)])
                    ev = ebr[:, 2 * pair:2 * pair + 2, :].rearrange("p h (w q) -> p h w q", w=14)
                    ev = ev[:, :, w0:w0 + wstep * (ng - 1) + 1:wstep, :]
                    sch.op(meng, lambda e: e.tensor_tensor(out=pnv, in0=pv, in1=ev, op=ALU.mult),
                           reads=[("pB", ip), ("ebr", 2 * pair), ("ebr", 2 * pair + 1)], writes=[("pn", ip)])

                def s2():
                    vt = vE if par == "e" else vO
                    def fpv(pe):
                        cs = slice(64 * g_lo, 64 * (g_hi + 1))
                        for h in range(2):
                            ps_ = slice(h * 256 + 64 * g_lo, h * 256 + 64 * (g_hi + 1))
                            pe.matmul(banks[nb_][h * 64:(h + 1) * 64, cs],
                                      lhsT=vt[:, vidx, (2 * jj + h) * 64:(2 * jj + h + 1) * 64], rhs=pn[:, ps_],
                                      start=first, stop=last)
                        for h in range(2):
                            ps_ = slice(h * 256 + 64 * g_lo, h * 256 + 64 * (g_hi + 1))
                            m = pe.matmul(banks[db_][h * 64:(h + 1) * 64, cs], lhsT=ones[:, 0:64], rhs=pn[:, ps_],
                                          start=first, stop=last)
                        return m
                    sch.op("pe", fpv, reads=[("pn", ip), "ones"] + vkeys, writes=[bkey(nb_), bkey(db_)])
                    if last:
                        fin()
                return s1, s2

            def make_fin(jj, pair, r0, step, nb_, db_):
                def f():
                    it = cg[0] % 2
                    cg[0] += 1
                    t1 = t1s[it]
                    sch.op("dve", lambda e: e.reciprocal(out=t1[:, :], in_=banks[db_][:, 0:256]),
                           reads=[bkey(db_), ("t1", it)], writes=[("t1", it)])
                    sch.op("dve", lambda e: e.tensor_tensor(out=t1[:, :], in0=banks[nb_][:, 0:256], in1=t1[:, :],
                                                            op=ALU.mult),
                           reads=[bkey(nb_), ("t1", it)], writes=[("t1", it)])
                    zv = zb[:, pair, :].rearrange("p (r q) -> p r q", q=64)[:, r0:r0 + 3 * step + 1:step, :]
                    gv = sgb[:, jj, :].rearrange("p (r q) -> p r q", q=64)[:, r0:r0 + 3 * step + 1:step, :]
                    sch.op("pool", lambda e: e.tensor_tensor(
                        out=zv, in0=t1[:, :].rearrange("p (g q) -> p g q", g=4), in1=gv, op=ALU.mult),
                        reads=[("t1", it)] + [("sgb", jj, t) for t in range(4)], writes=[("zb", pair, r0)])
                return f

            for hf in range(2):
                wqv, wqk = wq.get(idx["B%dq" % hf])
                wkv, wkk = wq.get(idx["B%dk" % hf])
                wvv, wvk = wq.get(idx["B%dv" % hf])
                for (vt, par) in ((vE, 0),):
                    for kp in range(8):
                        bi = nextb()
                        nu = 2 if (par == 0 or kp < 7) else 1

                        def fv(pe):
                            for u in range(nu):
                                t = 2 * kp + u
                                st_ = 128 * t + 64 * par
                                for c in range(8):
                                    m = pe.matmul(banks[bi][:, u * 256:(u + 1) * 256], lhsT=hT[:, c, st_:st_ + 128],
                                                  rhs=wvv[:, c, :], start=(c == 0), stop=(c == 7))
                            return m
                        sch.op("pe", fv, reads=[wvk], writes=[bkey(bi)])
                        sch.op("act", lambda e: e.activation(
                            out=vt[:, 2 * kp:2 * kp + nu, :],
                            in_=banks[bi][:, 0:nu * 256].rearrange("p (u n) -> p u n", u=nu), func=AF.Copy),
                            reads=[bkey(bi)], writes=[("v", par, kp)])
                wq.release(idx["B%dv" % hf])
                ev_keys = [("v", 0, kp_) for kp_ in range(8)]
                od_keys = [("v", 1, kp_) for kp_ in range(8)]
                sch.dma("sp", out=vO[0:64, 0:15, :], in_=vE[64:128, 0:15, :], reads=ev_keys, writes=od_keys,
                        semkey="vsh0")
                sch.dma("sp", out=vO[64:128, 0:15, :], in_=vE[0:64, 1:16, :], reads=ev_keys, writes=od_keys,
                        semkey="vsh1")
                for jj in range(2):
                    for (wv_, wk_, dstT, dn) in ((wqv, wqk, qT, "qT"), (wkv, wkk, kT, "kT")):
                        for tb in range(4):
                            bx = nextb()
                            proj_fm(wv_, wk_, jj * 128, lambda c: hT[:, c, tb * 512:(tb + 1) * 512], bx)
                            sch.op("act", lambda e: e.activation(out=dstT[:, jj, tb * 512:(tb + 1) * 512],
                                                                 in_=banks[bx][:, :], func=AF.Copy),
                                   reads=[bkey(bx)], writes=[(dn, jj, tb)])
                wq.release(idx["B%dq" % hf])
                wq.release(idx["B%dk" % hf])
                wgv, wgk = wq.get(idx["GB%d" % hf])
                for jj in range(2):
                    for tb in range(4):
                        bx = nextb()
                        proj_fm(wgv, wgk, jj * 128, lambda c: hT[:, c, tb * 512:(tb + 1) * 512], bx)
                        sch.op("act", lambda e: e.activation(out=sgb[:, jj, tb * 512:(tb + 1) * 512],
                                                             in_=banks[bx][:, :], func=AF.Silu),
                               reads=[bkey(bx)], writes=[("sgb", jj, tb)])
                wq.release(idx["GB%d" % hf])
                jobs = []
                for jj in range(2):
                    pair = 2 * hf + jj
                    for (r0, step, tiles) in groups:
                        nb_, db_ = ((6, 7), (0, 1))[cg[0] % 2] if False else ((6, 7), (0, 1))[len(jobs) % 2]
                        fin = make_fin(jj, pair, r0, step, nb_, db_)
                        gj = []
                        full = [t for t in tiles if t[3] == 0 and t[4] == 3]
                        tiles_o = [full[0]] + [t for t in tiles if t is not full[0]]
                        for ti, tile in enumerate(tiles_o):
                            gj.append(tile_job(jj, pair, r0, step, tile, ti == 0, ti == len(tiles_o) - 1, nb_, db_, fin, tidx=ti))
                        jobs.append(gj)
                flat = []
                for gi, gj in enumerate(jobs):
                    flat.extend(gj)
                pipeline(flat, 6)
            sch.barrier(pe=False)
    else:
        sch.op("dve", lambda e: e.memset(zb[:, :, :], 0.0), writes=["zb"])
        sch.barrier()

    ebr_cm.__exit__(None, None, None)
    if debug:
        sch.dma("sp", out=dbg["za"][:, :, :], in_=za[:, :, :], semkey="dbg")
        sch.dma("sp", out=dbg["zb"][:, :, :], in_=zb[:, :, :], semkey="dbg")
        sch.dma("sp", out=dbg["zc"][:, :, :], in_=zc[:, :, :], semkey="dbg")
        sch.barrier()

    yT = sb("yT", [128, 8, S], BF16).__enter__()
    zs = (za, zb, zc)
    with ExitStack() as es_:
        wbs = [es_.enter_context(sb("wbr", [128, 4, D], BF16)) for _ in range(3)]
        for b in range(3):
            for nh_ in range(2):
                sch.dma("pool", out=wbs[b][:, :, nh_ * 512:(nh_ + 1) * 512],
                        in_=w_br[b][:, nh_ * 512:(nh_ + 1) * 512].rearrange("(c p) n -> p c n", p=128),
                        writes=[("wbr", b, nh_)], semkey="wbr%d%d" % (b, nh_))
        sg0 = es_.enter_context(sb("sg0", [128, 512], F32))
        sg1 = es_.enter_context(sb("sg1", [128, 512], F32))
        sg2 = es_.enter_context(sb("sg2", [128, 512], F32))
        tt0 = es_.enter_context(sb("tt0", [128, 512], F32))
        tt1 = es_.enter_context(sb("tt1", [128, 512], F32))
        tt2 = es_.enter_context(sb("tt2", [128, 512], F32))
        uu = es_.enter_context(sb("uu", [128, 512], F32))
        sgs = (sg0, sg1, sg2)
        tts = (tt0, tt1, tt2)
        for nh in range(2):
            mlw = [wq.get(idx["ML%d%d" % (nh, b)]) for b in range(3)]
            wbw = [(wbs[b][:, :, nh * 512:(nh + 1) * 512], ("wbr", b, nh)) for b in range(3)]
            for nc4 in range(4):
                nch = 4 * nh + nc4
                for tb in range(4):
                    sl = slice(tb * 512, (tb + 1) * 512)
                    for b in range(3):
                        proj_fm(mlw[b][0], mlw[b][1], nc4 * 128, lambda c: hT[:, c, sl], b)
                        sch.op("act", lambda e: e.activation(out=sgs[b][:, :], in_=banks[b][:, :], func=AF.Sigmoid,
                                                             bias=mb_sb[:, b * 8 + nch:b * 8 + nch + 1]),
                               reads=[bkey(b), "mb"], writes=[("sg", b)])
                        if b == 0:
                            zk = [("za", c_, tb) for c_ in range(4)]
                        elif b == 2:
                            zk = [("zc", c_, tb) for c_ in range(4)]
                        else:
                            zk = [("zb", c_, r0_) for c_ in range(4) for r0_ in (0, 4, 5, 12, 13, 20, 21, 28)]
                        proj_fm(wbw[b][0], wbw[b][1], nc4 * 128, lambda c: zs[b][:, c, sl], 3 + b, nchunk=4,
                                extra_reads=zk)
                        sch.op("dve", lambda e: e.tensor_tensor(out=tts[b][:, :], in0=banks[3 + b][:, :],
                                                                in1=sgs[b][:, :], op=ALU.mult),
                               reads=[bkey(3 + b), ("sg", b)], writes=[("tt", b)])
                    sch.op("pool", lambda e: e.tensor_tensor(out=uu[:, :], in0=tt0[:, :], in1=tt1[:, :], op=ALU.add),
                           reads=[("tt", 0), ("tt", 1)], writes=["uu"])
                    sch.op("pool", lambda e: e.tensor_tensor(out=yT[:, nch, sl], in0=uu[:, :], in1=tt2[:, :],
                                                             op=ALU.add),
                           reads=["uu", ("tt", 2)], writes=[("yT", nch, tb)])
            for b in range(3):
                wq.release(idx["ML%d%d" % (nh, b)])
        sch.barrier(pe=False)
    if debug:
        sch.dma("sp", out=dbg["yT"][:, :, :], in_=yT[:, :, :], semkey="dbg")
        sch.barrier()

    with ExitStack() as es_:
        gbc = es_.enter_context(sb("gbc", [128, D], F32))
        xss = [es_.enter_context(sb("xs", [128, D], F32)) for _ in range(2)]
        osbs = [es_.enter_context(sb("osb", [128, D], F32)) for _ in range(2)]
        sq = es_.enter_context(sb("sq", [128, D], F32))
        sss = [es_.enter_context(sb("ss", [128, 1], F32)) for _ in range(2)]
        rss = [es_.enter_context(sb("rs", [128, 1], F32)) for _ in range(2)]
        rstds = [es_.enter_context(sb("rstd", [128, 1], F32)) for _ in range(2)]
        ons = [es_.enter_context(sb("on", [128, D], F32)) for _ in range(2)]
        ots = [es_.enter_context(sb("ot", [128, D], F32)) for _ in range(2)]
        sch.dma("sp", out=gbc[:, :], in_=g_post[:, :], writes=["gbc"], semkey="const10")
        wo = [wq.get(idx["WO%d" % nh]) for nh in range(2)]

        def o_job(tt):
            i = tt % 2
            xs, osb, ss, rs, rstd, on, ot = xss[i], osbs[i], sss[i], rss[i], rstds[i], ons[i], ots[i]

            def s1():
                sch.dma("pool", out=xs[:, :], in_=x[tt * 128:(tt + 1) * 128, :], writes=[("xs", i)],
                        semkey="fx%d" % i)
                for nh in range(2):
                    bi = 2 * (tt % 3) + nh

                    def fo(pe):
                        for c in range(8):
                            m = pe.matmul(banks[bi][:, :], lhsT=yT[:, c, tt * 128:(tt + 1) * 128],
                                          rhs=wo[nh][0][:, c, :], start=(c == 0), stop=(c == 7))
                        return m
                    sch.op("pe", fo, reads=[wo[nh][1]] + [("yT", c_, tt // 4) for c_ in range(8)], writes=[bkey(bi)])
                    sch.op("act", lambda e: e.activation(out=osb[:, nh * 512:(nh + 1) * 512], in_=banks[bi][:, :],
                                                         func=AF.Copy), reads=[bkey(bi)], writes=[("osb", i, nh)])
                sch.op("act", lambda e: e.activation(out=sq[:, :], in_=osb[:, :], func=AF.Square,
                                                     accum_out=ss[:, 0:1]),
                       reads=[("osb", i, 0), ("osb", i, 1)], writes=["sq", ("ss", i)])
                sch.op("act", lambda e: e.activation(out=rs[:, 0:1], in_=ss[:, 0:1], func=AF.Sqrt,
                                                     bias=epsT[:, 0:1], scale=1.0 / D),
                       reads=[("ss", i)], writes=[("rs", i)])
                sch.op("dve", lambda e: e.reciprocal(out=rstd[:, 0:1], in_=rs[:, 0:1]), reads=[("rs", i)],
                       writes=[("rstd", i)])

            def s2():
                sch.op("dve", lambda e: e.scalar_tensor_tensor(out=on[:, :], in0=osb[:, :], scalar=rstd[:, 0:1],
                                                               in1=gbc[:, :], op0=ALU.mult, op1=ALU.mult),
                       reads=[("osb", i, 0), ("osb", i, 1), ("rstd", i), "gbc"], writes=[("on", i)])
                sch.op("dve", lambda e: e.tensor_tensor(out=ot[:, :], in0=on[:, :], in1=xs[:, :], op=ALU.add),
                       reads=[("on", i), ("xs", i)], writes=[("ot", i)])
                sch.dma("sp", out=out[tt * 128:(tt + 1) * 128, :], in_=ot[:, :], reads=[("ot", i)],
                        semkey="out%d" % i)
            return s1, s2
        pipeline([o_job(tt) for tt in range(16)], 1)
        sch.barrier()
    return nc


def host_consts():
    bf = ml_dtypes.bfloat16
    c = {}
    c["c_ident"] = np.eye(128, dtype=np.float32).astype(bf)
    perm = np.zeros((128, 128), np.float32)
    for m in range(128):
        d = m % 64
        if d < 8:
            perm[m + 8, m] = 1.0
        elif d < 16:
            perm[m - 8, m] = 1.0
    c["c_perm"] = perm.astype(bf)
    pos = np.arange(S, dtype=np.float32)
    half = 8
    inv = (np.float32(500000.0) ** (-np.arange(half, dtype=np.float32) * np.float32(2.0) / np.float32(16))).astype(np.float32)
    ang = pos[None, :] * inv[:, None]
    cos, sin = np.cos(ang).astype(np.float32), np.sin(ang).astype(np.float32)
    C = np.ones((128, S), np.float32)
    Sg = np.zeros((128, S), np.float32)
    for p in range(128):
        d = p % 64
        if d < 8:
            C[p] = cos[d]
            Sg[p] = -sin[d]
        elif d < 16:
            C[p] = cos[d - 8]
            Sg[p] = sin[d - 8]
    c["c_ropeC"] = C
    c["c_ropeS"] = Sg
    k = np.arange(128)[:, None]
    q = np.arange(128)[None, :]
    mU = (k >= q).astype(np.float32)
    mL = (k <= q).astype(np.float32)
    mB = (np.abs(k - q) <= 64).astype(np.float32)
    mask = np.zeros((128, 3, 2, 2, 128), np.float32)
    for h in range(2):
        mask[:, 0, h, 0] = mU
        mask[:, 0, h, 1] = mL
        mask[:, 1, h, 0] = mU
        mask[:, 1, h, 1] = mL
        mask[:, 2, h, 0] = mB
        mask[:, 2, h, 1] = mB
    c["c_mask"] = mask.astype(bf)
    kc = np.arange(64)
    qc = np.arange(64)
    cstart = np.clip(qc - 8, 0, 48)
    cm = ((kc[:, None] >= cstart[None, :]) & (kc[:, None] < cstart[None, :] + 16)).astype(np.float32)
    c["c_colmask"] = np.tile(np.concatenate([cm, cm], axis=0), (1, 14)).astype(bf)
    return c


def gather_rpb(rpb):
    kc = np.arange(64)
    qc = np.arange(64)
    dc = np.clip(kc[:, None] - qc[None, :], -15, 15) + 15
    outp = np.zeros((2, 64, 8, 14, 64), np.float32)
    for w in range(14):
        d0 = 6 - w
        for i in range(2):
            outp[i, :, :, w, :] = np.transpose(rpb[:, d0 + i + 7][:, dc], (1, 0, 2))
    return np.ascontiguousarray(outp.reshape(128, 8, 14 * 64))


_NC_CACHE = {}


def make_in_maps(x, mem, pre_norm, w_in, merge_bias, na_rpb, mem_norm, w_mem_kv, w_branch_a, w_branch_b,
                 w_branch_c, w_out, post_norm, cores):
    f = lambda a: np.ascontiguousarray(np.asarray(a, dtype=np.float32))
    c = host_consts()
    shared = dict(c)
    shared["w_in"] = f(w_in[0])
    shared["w_mem"] = f(w_mem_kv[0])
    shared["w_br0"] = f(w_branch_a[0])
    shared["w_br1"] = f(w_branch_b[0])
    shared["w_br2"] = f(w_branch_c[0])
    shared["w_out"] = f(w_out[0])
    shared["g_pre"] = f(np.broadcast_to(np.asarray(pre_norm[0])[None, :], (128, D)))
    shared["g_mem"] = f(np.broadcast_to(np.asarray(mem_norm[0])[None, :], (128, D)))
    shared["g_post"] = f(np.broadcast_to(np.asarray(post_norm[0])[None, :], (128, D)))
    shared["mbias"] = f(np.asarray(merge_bias[0]).reshape(3, 8, 128).transpose(2, 0, 1).reshape(128, 24))
    shared["rpbg"] = gather_rpb(np.asarray(na_rpb[0], dtype=np.float32))
    maps = []
    for b in cores:
        m = dict(shared)
        m["x"] = f(x[b])
        m["mem"] = f(mem[b])
        maps.append(m)
    return maps


def kernel(x, mem, pre_norm, w_in, merge_bias, na_rpb, mem_norm, w_mem_kv, w_branch_a, w_branch_b, w_branch_c,
           w_out, post_norm):
    x = np.asarray(x)
    mem = np.asarray(mem)
    nc = build()
    maps = make_in_maps(x, mem, pre_norm, w_in, merge_bias, na_rpb, mem_norm, w_mem_kv, w_branch_a, w_branch_b,
                        w_branch_c, w_out, post_norm, list(range(8)))
    res = run_bass_kernel_spmd(nc, maps, core_ids=list(range(8)))
    return np.stack([np.asarray(r["out"], dtype=np.float32) for r in res.results], axis=0)
```

```python
import numpy as np
from contextlib import ExitStack
import ml_dtypes
import concourse.bass as bass
import concourse.mybir as mybir
from concourse.bass_utils import run_bass_kernel_spmd

F32 = mybir.dt.float32
BF16 = mybir.dt.bfloat16
AF = mybir.ActivationFunctionType
ALU = mybir.AluOpType
AX = mybir.AxisListType

S = 2048
D = 1024
NIN = 11264
DILS = (1, 4, 16)
OFF_B = 4608
OFF_CQ = 6144
OFF_GA = 6656
OFF_GB = 7168
OFF_GC = 7680
OFF_ML = 8192
EPS = 1e-6
NW = 6


class Sched:
    def __init__(self, nc):
        self.nc = nc
        self.eng = dict(pe=nc.tensor, act=nc.scalar, dve=nc.vector, pool=nc.gpsimd, sp=nc.sync)
        self.sem = {}
        self.cnt = {}
        for e in ("pe", "act", "dve", "pool"):
            self.sem[e] = nc.semaphore("s_" + e).__enter__()
            self.cnt[e] = 0
        self.waited = {e: {} for e in self.eng}
        self.state = {}
        self.dsem = {}

    def _wait(self, e, tok):
        if tok is None:
            return
        name, sem, val, src = tok
        if src == e and e == "pe":
            return
        w = self.waited[e]
        if w.get(name, 0) >= val:
            return
        self.eng[e].wait_ge(sem, val)
        w[name] = val

    def deps(self, e, reads, writes):
        for k in reads:
            st = self.state.get(k)
            if st:
                self._wait(e, st[0])
                if isinstance(k, tuple) and k[0] == "ps":
                    for src, t in st[1].items():
                        if src != e:
                            self._wait(e, t)
        for k in writes:
            st = self.state.get(k)
            if st:
                self._wait(e, st[0])
                for t in st[1].values():
                    self._wait(e, t)

    def commit(self, tok, reads, writes):
        src = tok[3]
        for k in writes:
            self.state[k] = [tok, {}]
        for k in reads:
            if k in writes:
                continue
            st = self.state.setdefault(k, [None, {}])
            st[1][src] = tok

    def op(self, e, fn, reads=(), writes=()):
        self.deps(e, reads, writes)
        ins = fn(self.eng[e])
        self.cnt[e] += 1
        ins.then_inc(self.sem[e], 1)
        tok = ("s_" + e, self.sem[e], self.cnt[e], e)
        self.commit(tok, reads, writes)
        return tok

    def dma(self, e, out, in_, reads=(), writes=(), semkey=None):
        self.deps(e, reads, writes)
        if semkey not in self.dsem:
            self.dsem[semkey] = [self.nc.semaphore("d_" + semkey).__enter__(), 0]
        d = self.dsem[semkey]
        d[1] += 16
        self.eng[e].dma_start(out=out, in_=in_).then_inc(d[0], 16)
        tok = ("d_" + semkey, d[0], d[1], "dma_" + semkey)
        self.commit(tok, reads, writes)
        return tok

    def barrier(self, pe=True):
        toks = [("s_" + e, self.sem[e], self.cnt[e], e) for e in self.sem if self.cnt[e] > 0]
        toks += [("d_" + k, d[0], d[1], "dma_" + k) for k, d in self.dsem.items()]
        for e in self.eng:
            if e == "pe" and not pe:
                continue
            for t in toks:
                if t[3] == e and e == "pe":
                    continue
                self._wait(e, t)


class WQ:
    def __init__(self, sch, ring, specs, first=2):
        self.sch = sch
        self.ring = ring
        self.specs = specs
        self.issued = 0
        self.released = set()
        self.limit = first
        self.extra = ()
        self.try_issue()

    def unlimit(self, extra_reads=()):
        self.limit = None
        self.extra = tuple(extra_reads)
        self.try_issue()
        self.extra = ()

    def try_issue(self):
        n = len(self.ring)
        while self.issued < len(self.specs):
            j = self.issued
            if self.limit is not None and j >= self.limit:
                break
            if j >= n and (j - n) not in self.released:
                break
            slot = j % n
            src, nch = self.specs[j]
            ncols = src.shape[1]
            dst = self.ring[slot][:, 0:nch * ncols].rearrange("p (c n) -> p c n", c=nch)
            self.sch.dma("pool", out=dst, in_=src.rearrange("(c p) n -> p c n", p=128),
                         reads=list(self.extra), writes=[("ring", slot)], semkey="ring%d" % slot)
            self.issued += 1

    def get(self, j):
        assert j < self.issued, (j, self.issued)
        slot = j % len(self.ring)
        src, nch = self.specs[j]
        ncols = src.shape[1]
        v = self.ring[slot][:, 0:nch * ncols].rearrange("p (c n) -> p c n", c=nch)
        return v, ("ring", slot)

    def release(self, j):
        self.released.add(j)
        self.try_issue()


def na_groups():
    groups = []
    tiles = [("e", t, 128 * t, 0, 3, 6 - 2 * t, 1) for t in range(4)]
    groups.append((0, 1, tiles))
    for r0 in (4, 5, 12, 13, 20, 21):
        tiles = []
        for m in range(7):
            kr0 = r0 - 4 + 2 * m
            g_lo, g_hi = max(0, m - 3), min(3, m)
            w0 = 10 - 2 * m + 2 * g_lo
            if r0 % 2 == 0:
                tiles.append(("e", kr0 // 2, 64 * kr0, g_lo, g_hi, w0, 2))
            else:
                tiles.append(("o", (kr0 - 1) // 2, 64 * kr0, g_lo, g_hi, w0, 2))
        groups.append((r0, 2, tiles))
    tiles = [("e", t, 128 * t, 0, 3, 34 - 2 * t, 1) for t in range(12, 16)]
    groups.append((28, 1, tiles))
    return groups


def dil_superblocks(g):
    dil = DILS[g]
    L = S // dil
    nb = L // 128
    sbs = []
    if nb == 1:
        for r0 in range(0, dil, 4):
            units = []
            for rr in range(4):
                r = r0 + rr
                units.append((rr * 128, 128, r * L, [(r * nb + 0, 2, 0)]))
            sbs.append((r0 * L, 512, units))
        return sbs
    for r in range(dil):
        ulist = []
        for qb in range(-1, nb):
            lo, hi = 0, 128
            if qb == -1:
                lo = 64
            if qb == nb - 1:
                hi = 64
            tl = []
            if qb >= 0:
                tl.append((r * nb + qb, 0, lo))
            if qb + 1 < nb:
                tl.append((r * nb + qb + 1, 1, lo))
            ulist.append((r * L + 128 * qb + 64 + lo, hi - lo, tl))
        csz = 5 if nb == 4 else 4
        for i in range(0, len(ulist), csz):
            ch = ulist[i:i + csz]
            pos0 = ch[0][0]
            units = [(u[0] - pos0, u[1], u[0], u[2]) for u in ch]
            width = ch[-1][0] + ch[-1][1] - pos0
            sbs.append((pos0, width, units))
    return sbs


def build(debug=False, phases=("A", "C", "B"), agroups=(0, 1, 2), astop=9):
    nc = bass.Bass("TRN2", target_bir_lowering=False)

    def din(name, shape, dt=F32):
        return nc.dram_tensor(name, list(shape), dt, kind="ExternalInput").ap()

    x = din("x", [S, D])
    mem = din("mem", [256, D])
    w_in = din("w_in", [D, NIN])
    w_mem = din("w_mem", [D, 1024])
    w_br = [din("w_br%d" % b, [512, D]) for b in range(3)]
    w_out = din("w_out", [D, D])
    g_pre = din("g_pre", [128, D])
    g_mem = din("g_mem", [128, D])
    g_post = din("g_post", [128, D])
    mbias = din("mbias", [128, 24])
    rpbg = din("rpbg", [128, 8, 14 * 64])
    c_ident = din("c_ident", [128, 128], BF16)
    c_perm = din("c_perm", [128, 128], BF16)
    c_ropeC = din("c_ropeC", [128, S])
    c_ropeS = din("c_ropeS", [128, S])
    c_mask = din("c_mask", [128, 3, 2, 2, 128], BF16)
    c_colmask = din("c_colmask", [128, 14 * 64], BF16)
    out = nc.dram_tensor("out", [S, D], F32, kind="ExternalOutput").ap()
    dbg = {}
    if debug:
        dbg["hT"] = nc.dram_tensor("dbg_hT", [128, 8, S], BF16, kind="ExternalOutput").ap()
        dbg["za"] = nc.dram_tensor("dbg_za", [128, 4, S], BF16, kind="ExternalOutput").ap()
        dbg["zb"] = nc.dram_tensor("dbg_zb", [128, 4, S], BF16, kind="ExternalOutput").ap()
        dbg["zc"] = nc.dram_tensor("dbg_zc", [128, 4, S], BF16, kind="ExternalOutput").ap()
        dbg["yT"] = nc.dram_tensor("dbg_yT", [128, 8, S], BF16, kind="ExternalOutput").ap()

    sch = Sched(nc)

    uniq = [0]

    def sb(name, shape, dt):
        uniq[0] += 1
        return nc.sbuf_tensor("%s_%d" % (name, uniq[0]), list(shape), dt)

    hT = sb("hT", [128, 8, S], BF16).__enter__()
    za = sb("za", [128, 4, S], BF16).__enter__()
    ring = [sb("ring%d" % i, [128, 4096], BF16).__enter__() for i in range(NW)]
    memT = sb("memT", [128, 8, 256], BF16).__enter__()
    ident = sb("ident", [128, 128], BF16).__enter__()
    ones = sb("ones", [128, 128], BF16).__enter__()
    zeros = sb("zeros", [128, 256], BF16).__enter__()
    epsT = sb("epsT", [128, 1], F32).__enter__()
    mb_sb = sb("mb_sb", [128, 24], F32).__enter__()
    psall = nc.psum_tensor("psall", [128, 8 * 512], F32).__enter__()

    class _Bank:
        def __init__(self, i):
            self.i = i

        def __getitem__(self, key):
            return psall[:, self.i * 512:(self.i + 1) * 512][key]
    banks = [_Bank(i) for i in range(8)]

    def bkey(i):
        return ("ps", i)

    specs = []
    idx = {}

    def add(name, ap, nch):
        idx[name] = len(specs)
        specs.append((ap, nch))

    if "A" in phases:
        for hf in range(2):
            for g in agroups:
                for s, sn in ((2, "v"), (0, "q"), (1, "k")):
                    c0 = 512 * (3 * g + s) + 256 * hf
                    add("A%d%d%s" % (hf, g, sn), w_in[:, c0:c0 + 256], 8)
            add("GA%d" % hf, w_in[:, OFF_GA + 256 * hf:OFF_GA + 256 * hf + 256], 8)
    if "C" in phases:
        add("KM", w_mem[:, 0:512], 8)
        add("VM", w_mem[:, 512:1024], 8)
        add("CQ", w_in[:, OFF_CQ:OFF_CQ + 512], 8)
        add("GC", w_in[:, OFF_GC:OFF_GC + 512], 8)
    if "B" in phases:
        for hf in range(2):
            for s, sn in ((2, "v"), (0, "q"), (1, "k")):
                c0 = OFF_B + 512 * s + 256 * hf
                add("B%d%s" % (hf, sn), w_in[:, c0:c0 + 256], 8)
            add("GB%d" % hf, w_in[:, OFF_GB + 256 * hf:OFF_GB + 256 * hf + 256], 8)
    for nh in range(2):
        for b in range(3):
            c0 = OFF_ML + 1024 * b + 512 * nh
            add("ML%d%d" % (nh, b), w_in[:, c0:c0 + 512], 8)
    for nh in range(2):
        add("WO%d" % nh, w_out[:, 512 * nh:512 * nh + 512], 8)

    sch.dma("sp", out=ident[:, :], in_=c_ident[:, :], writes=["ident"], semkey="const1")
    sch.dma("sp", out=mb_sb[:, :], in_=mbias[:, :], writes=["mb"], semkey="const2")
    sch.op("dve", lambda e: e.memset(ones[:, :], 1.0), writes=["ones"])
    sch.op("dve", lambda e: e.memset(zeros[:, :], 0.0), writes=["zeros"])
    sch.op("dve", lambda e: e.memset(epsT[:, :], EPS), writes=["eps"])

    wq = WQ(sch, ring, specs)

    def norm_tile(src_dram_rows, gbc, xs, sq, ss, rs, rstd, hb, skey, gkey="gbc", k2="", part=None):
        if part in (None, -1):
            sch.dma("sp", out=xs[:, :], in_=src_dram_rows, writes=[skey + "xs"], semkey=skey + "xs")
        if part in (None, 0, 0.1):
            sch.op("act", lambda e: e.activation(out=sq[:, :], in_=xs[:, :], func=AF.Square,
                                                 accum_out=ss[:, 0:1]),
                   reads=[skey + "xs"], writes=["sq" + k2, "ss" + k2])
        if part in (None, 1):
            sch.op("act", lambda e: e.activation(out=rs[:, 0:1], in_=ss[:, 0:1], func=AF.Sqrt,
                                                 bias=epsT[:, 0:1], scale=1.0 / D),
                   reads=["ss" + k2, "eps"], writes=["rs" + k2])
            sch.op("dve", lambda e: e.reciprocal(out=rstd[:, 0:1], in_=rs[:, 0:1]), reads=["rs" + k2],
                   writes=["rstd" + k2])
            sch.op("dve", lambda e: e.scalar_tensor_tensor(out=hb[:, :], in0=xs[:, :], scalar=rstd[:, 0:1],
                                                           in1=gbc[:, :], op0=ALU.mult, op1=ALU.mult),
                   reads=[skey + "xs", "rstd" + k2, gkey], writes=[skey + "hb"])

    def transpose8(hb, hbkey, dst, dkey, bank_i, evac="act"):
        bT = banks[bank_i][:, :].bitcast(BF16)

        def f(pe):
            for c in range(8):
                m = pe.transpose(out=bT[:, c * 128:(c + 1) * 128], in_=hb[:, c * 128:(c + 1) * 128],
                                 identity=ident[:, :])
            return m
        sch.op("pe", f, reads=[hbkey, "ident"], writes=[bkey(bank_i)])
        if evac == "act":
            sch.op("act", lambda e: e.activation(out=dst, in_=bT.rearrange("p (c t) -> p c t", c=8), func=AF.Copy),
                   reads=[bkey(bank_i)], writes=[dkey])
        else:
            sch.op("dve", lambda e: e.tensor_copy(out=dst, in_=bT.rearrange("p (c t) -> p c t", c=8)),
                   reads=[bkey(bank_i)], writes=[dkey])

    pring = [0]

    def next_bank(lst):
        b = lst[pring[0] % len(lst)]
        pring[0] += 1
        return b

    def proj_fm(wv, wkey, col0, rhs_fn, bank_i, nchunk=8, ncols=128, extra_reads=()):
        def f(pe):
            for c in range(nchunk):
                r = rhs_fn(c)
                n_ = int(np.prod(r.shape[1:]))
                m = pe.matmul(banks[bank_i][0:ncols, 0:n_],
                              lhsT=wv[:, c, col0:col0 + ncols], rhs=r, start=(c == 0), stop=(c == nchunk - 1))
            return m
        return sch.op("pe", f, reads=[wkey] + list(extra_reads), writes=[bkey(bank_i)])

    def pipeline(jobs, lag):
        pend = []
        for s1, s2 in jobs:
            s1()
            pend.append(s2)
            if len(pend) > lag:
                pend.pop(0)()
        for f in pend:
            f()

    def pipeline_n(jobs, lags):
        ns = len(jobs[0])
        offs = [0]
        for l in lags:
            offs.append(offs[-1] + l)
        n = len(jobs)
        for it in range(n + offs[-1]):
            for s in range(ns):
                j = it - offs[s]
                if 0 <= j < n:
                    jobs[j][s]()

    vA_cm = sb("vA", [128, 16, 256], BF16)
    vA = vA_cm.__enter__()
    with ExitStack() as es_:
        gbc = es_.enter_context(sb("gbc", [128, D], F32))
        gbm = es_.enter_context(sb("gbm", [128, D], F32))
        NX = 6
        xss = [es_.enter_context(sb("xs", [128, D], F32)) for _ in range(NX)]
        sqs = [es_.enter_context(sb("sq", [128, D], F32)) for _ in range(3)]
        sss = [es_.enter_context(sb("ss", [128, 1], F32)) for _ in range(3)]
        rss = [es_.enter_context(sb("rs", [128, 1], F32)) for _ in range(3)]
        rstds = [es_.enter_context(sb("rstd", [128, 1], F32)) for _ in range(3)]
        hbs = [es_.enter_context(sb("hb", [128, D], BF16)) for _ in range(6)]

        early_v = ("A" in phases) and agroups[0] == 0

        def early_v_proj(kp):
            wvv, wvk = wq.get(idx["A00v"])
            bi = (0, 1, 2, 3)[kp % 4]

            def fv(pe):
                for u in range(2):
                    kt = 2 * kp + u
                    for c in range(8):
                        m = pe.matmul(banks[bi][:, u * 256:(u + 1) * 256], lhsT=hT[:, c, kt * 128:(kt + 1) * 128],
                                      rhs=wvv[:, c, :], start=(c == 0), stop=(c == 7))
                return m
            sch.op("pe", fv, reads=[wvk, ("hT", 2 * kp), ("hT", 2 * kp + 1)], writes=[bkey(bi)])
            sch.op("pool" if False else "dve", lambda e: e.tensor_copy(
                out=vA[:, 2 * kp:2 * kp + 2, :], in_=banks[bi][:, :].rearrange("p (u n) -> p u n", u=2)),
                reads=[bkey(bi)], writes=[("vA", kp)])

        def p0_job(tt):
            i2 = tt % 2
            i3 = tt % 3
            if tt < 16:
                src, g_, gk, dst, dk = x[tt * 128:(tt + 1) * 128, :], gbc, "gbc", hT[:, :, tt * 128:(tt + 1) * 128], ("hT", tt)
            else:
                mt = tt - 16
                src, g_, gk, dst, dk = mem[mt * 128:(mt + 1) * 128, :], gbm, "gbm", memT[:, :, mt * 128:(mt + 1) * 128], ("memT", mt)

            ix = tt % NX
            args = (src, g_, xss[ix], sqs[i3], sss[i3], rss[i3], rstds[i3], hbs[ix], "p0_%d" % ix)

            def sl():
                norm_tile(*args, gkey=gk, k2="_%d" % i3, part=-1)
                if tt == 1:
                    sch.dma("sp", out=gbc[:, :], in_=g_pre[:, :], writes=["gbc"], semkey="const3")
                if tt == 12:
                    sch.dma("sp", out=gbm[:, :], in_=g_mem[:, :], writes=["gbm"], semkey="const8")
                if tt == 15:
                    wq.unlimit(extra_reads=["p0_%dxs" % ix])

            def s0a():
                norm_tile(*args, gkey=gk, k2="_%d" % i3, part=0.1)

            def s0b():
                norm_tile(*args, gkey=gk, k2="_%d" % i3, part=0.2)

            def s1():
                norm_tile(*args, gkey=gk, k2="_%d" % i3, part=1)

            def s2():
                transpose8(hbs[ix], "p0_%dhb" % ix, dst, dk, 6 + i2, evac=("act" if tt % 2 == 0 else "dve"))
                if early_v and 3 <= tt < 18 and tt % 2 == 1:
                    early_v_proj((tt - 3) // 2)
            return sl, s0a, s0b, s1, s2
        pipeline_n([p0_job(tt) for tt in range(18)], [2, 1, 1, 1])
        sch.barrier()
    if debug:
        sch.dma("sp", out=dbg["hT"][:, :, :], in_=hT[:, :, :], semkey="dbg")
        sch.barrier()

    if "A" in phases:
        with ExitStack() as es_:
            accn = es_.enter_context(sb("accn", [128, 2, S], F32))
            accd = es_.enter_context(sb("accd", [128, 2, S], F32))
            qT = es_.enter_context(sb("qT", [128, 2, S], BF16))
            kT = es_.enter_context(sb("kT", [128, 2, S], BF16))
            ropeC = es_.enter_context(sb("ropeC", [128, S], F32))
            ropeS = es_.enter_context(sb("ropeS", [128, S], F32))
            perm = es_.enter_context(sb("perm", [128, 128], BF16))
            maskA = es_.enter_context(sb("maskA", [128, 3 * 512], BF16))
            NQB = 3
            qbs = [es_.enter_context(sb("qb", [128, 512], BF16)) for _ in range(NQB)]
            t1s = [es_.enter_context(sb("t1", [128, 512], F32)) for _ in range(3)]
            t2s = [es_.enter_context(sb("t2", [128, 512], F32)) for _ in range(3)]
            NPB = 8
            pAs = [es_.enter_context(sb("pA", [128, 512], BF16)) for _ in range(NPB)]
            pms = [es_.enter_context(sb("pm", [128, 512], BF16)) for _ in range(NPB)]
            sch.dma("sp", out=ropeC[:, :], in_=c_ropeC[:, :], writes=["ropeC"], semkey="const4")
            sch.dma("sp", out=ropeS[:, :], in_=c_ropeS[:, :], writes=["ropeS"], semkey="const5")
            sch.dma("sp", out=perm[:, :], in_=c_perm[:, :], writes=["perm"], semkey="const6")
            sch.dma("sp", out=maskA[:, :], in_=c_mask.rearrange("k a h t q -> k (a h t q)"), writes=["maskA"],
                    semkey="const7")
            WIDE = [0, 1, 2, 3, 4, 5]
            cq = [0]
            ct = [0]
            cu = [0]
            csb = [0]
            accq = []
            spc = [0]
            rc = [0]
            ROPEB = [2, 3, 4, 5, 6, 7, 0, 1]

            def rope_job(wv_, wk_, dstT, dn, jj, tb, dil):
                iq = cq[0] % NQB
                cq[0] += 1
                it = ct[0] % 3
                ct[0] += 1
                qb = qbs[iq]
                t1 = t1s[it]
                t2 = t2s[it]
                st = {}

                def s1():
                    bx = ROPEB[rc[0] % 8]
                    rc[0] += 1
                    st["bx"] = bx
                    proj_fm(wv_, wk_, jj * 128, lambda c: hT[:, c, tb * 512:(tb + 1) * 512], bx)
                    sch.op("act", lambda e: e.activation(out=qb[:, :], in_=banks[bx][:, :], func=AF.Copy),
                           reads=[bkey(bx)], writes=[("qb", iq)])

                def s2():
                    bx = st["bx"]
                    by = ROPEB[rc[0] % 8]
                    rc[0] += 1
                    sch.op("pe", lambda pe: pe.matmul(banks[by][:, :], lhsT=perm[:, :], rhs=qb[:, :],
                                                      start=True, stop=True),
                           reads=[("qb", iq), "perm"], writes=[bkey(by)])
                    sch.op("dve", lambda e: e.tensor_tensor(out=t1[:, :], in0=banks[bx][:, :],
                                                            in1=ropeC[:, tb * 512:(tb + 1) * 512], op=ALU.mult),
                           reads=[bkey(bx), "ropeC"], writes=[("t1", it)])
                    sch.op("dve", lambda e: e.tensor_tensor(out=t2[:, :], in0=banks[by][:, :],
                                                            in1=ropeS[:, tb * 512:(tb + 1) * 512], op=ALU.mult),
                           reads=[bkey(by), "ropeS"], writes=[("t2", it)])
                    npb = 512 // dil
                    dst = dstT[:, jj, tb * 512:(tb + 1) * 512]
                    i0, i1 = t1[:, :], t2[:, :]
                    sch.op(("pool" if (tb % 2 == 0) else "dve"),
                           lambda e: e.tensor_tensor(out=dst, in0=i0, in1=i1, op=ALU.add),
                           reads=[("t1", it), ("t2", it)], writes=[(dn, jj, tb)])
                return s1, s2

            def unit_job(g, jj, unit, acc_after, nb_, db_):
                (colofs, nq, qpos, tl) = unit
                ip = cu[0] % NPB
                if g == 0:
                    meng = "dve" if (cu[0] % 4) != 3 else "pool"
                else:
                    meng = "dve" if (cu[0] % 2) == 0 else "pool"
                cu[0] += 1
                pA = pAs[ip]
                pm = pms[ip]
                lo = tl[0][2]
                nt = len(tl)
                dil_ = DILS[g]
                L_ = S // dil_
                npb_ = 512 // dil_

                def tok_of(p):
                    return dil_ * (p % L_) + p // L_

                def tbs_of(p0, n):
                    t0_ = tok_of(p0)
                    return range(t0_ // 512, (t0_ + dil_ * (n - 1)) // 512 + 1)
                q0 = tok_of(qpos)
                qkeys = [("qT", jj, t) for t in tbs_of(qpos, nq)]
                for (kt_, mk_, lo__) in tl:
                    qkeys += [("kT", jj, t) for t in tbs_of(kt_ * 128, 128)]
                vkeys_ = sorted(set(("vA", kt_ // 2) for (kt_, mk_, lo__) in tl))

                def s1():
                    bs = 2 if spc[0] < 2 else (4 if spc[0] % 2 == 0 else 2)
                    spc[0] += 1

                    def fs(pe):
                        for ti, (kt, mk, lo_) in enumerate(tl):
                            for h in range(2):
                                kt0 = tok_of(kt * 128)
                                m = pe.matmul(banks[bs + h][:, ti * 128:ti * 128 + nq],
                                              lhsT=kT[h * 64:(h + 1) * 64, jj, kt0:kt0 + dil_ * 127 + 1:dil_],
                                              rhs=qT[h * 64:(h + 1) * 64, jj, q0:q0 + dil_ * (nq - 1) + 1:dil_],
                                              start=True, stop=True)
                        return m
                    sch.op("pe", fs, reads=qkeys, writes=[bkey(bs), bkey(bs + 1)])
                    sv = psall[:, bs * 512:(bs + 2) * 512].rearrange("p (h x) -> p h x", h=2)[:, :, 0:256]
                    sv = sv.rearrange("p h (t q) -> p h t q", t=2)[:, :, 0:nt, 0:nq]
                    pv = pA[:, :].rearrange("p (h t q) -> p h t q", h=2, t=2)[:, :, 0:nt, 0:nq]
                    pmv = pm[:, :].rearrange("p (h t q) -> p h t q", h=2, t=2)[:, :, 0:nt, 0:nq]
                    sch.op("act", lambda e: e.activation(out=pv, in_=sv, func=AF.Exp, scale=0.125),
                           reads=[bkey(bs), bkey(bs + 1)], writes=[("pA", ip)])
                    mk0 = tl[0][1]
                    mv = maskA[:, mk0 * 512:(mk0 + 1) * 512].rearrange("p (h t q) -> p h t q", h=2, t=2)
                    if mk0 == 1:
                        mv = mv[:, :, 1:1 + nt, lo:lo + nq]
                    else:
                        mv = mv[:, :, 0:nt, lo:lo + nq]
                    sch.op(meng, lambda e: e.tensor_tensor(out=pmv, in0=pv, in1=mv, op=ALU.mult),
                           reads=[("pA", ip), "maskA"], writes=[("pm", ip)])

                def s2():
                    def fpv(pe):
                        for ti, (kt, mk, lo_) in enumerate(tl):
                            for h in range(2):
                                pe.matmul(banks[nb_][h * 64:(h + 1) * 64, colofs:colofs + nq],
                                          lhsT=vA[:, kt, (2 * jj + h) * 64:(2 * jj + h + 1) * 64],
                                          rhs=pm[:, (h * 2 + ti) * 128:(h * 2 + ti) * 128 + nq],
                                          start=(ti == 0), stop=(ti == nt - 1))
                        for ti, (kt, mk, lo_) in enumerate(tl):
                            for h in range(2):
                                m = pe.matmul(banks[db_][h * 64:(h + 1) * 64, colofs:colofs + nq],
                                              lhsT=ones[:, 0:64],
                                              rhs=pm[:, (h * 2 + ti) * 128:(h * 2 + ti) * 128 + nq],
                                              start=(ti == 0), stop=(ti == nt - 1))
                        return m
                    keep = []
                    for ent in accq:
                        ent[0] += 1
                        if ent[0] >= 2 or ent[1] == nb_:
                            ent[2]()
                        else:
                            keep.append(ent)
                    accq[:] = keep
                    sch.op("pe", fpv, reads=[("pm", ip), "ones"] + vkeys_,
                           writes=[bkey(nb_), bkey(db_)])
                    if acc_after is not None:
                        accq.append([0, nb_, acc_after])
                return s1, s2

            def make_acc(g, jj, pos0, width, first, nb_, db_):
                dil = DILS[g]
                L = S // dil

                def f():
                    if dil == 1:
                        an = accn[:, jj, pos0:pos0 + width]
                        ad = accd[:, jj, pos0:pos0 + width]
                        bn = banks[nb_][:, 0:width]
                        bd = banks[db_][:, 0:width]
                    elif dil == 4:
                        r = pos0 // L
                        an = accn[:, jj, r::4]
                        ad = accd[:, jj, r::4]
                        bn = banks[nb_][:, 0:512]
                        bd = banks[db_][:, 0:512]
                    else:
                        r0 = pos0 // L
                        an = accn[:, jj, :].rearrange("p (m r) -> p r m", r=16)[:, r0:r0 + 4, :]
                        ad = accd[:, jj, :].rearrange("p (m r) -> p r m", r=16)[:, r0:r0 + 4, :]
                        bn = banks[nb_][:, :].rearrange("p (r m) -> p r m", r=4)
                        bd = banks[db_][:, :].rearrange("p (r m) -> p r m", r=4)
                    if first:
                        sch.op("act", lambda e: e.activation(out=an, in_=bn, func=AF.Copy),
                               reads=[bkey(nb_)], writes=[("accn", jj)])
                        sch.op("act", lambda e: e.activation(out=ad, in_=bd, func=AF.Copy),
                               reads=[bkey(db_)], writes=[("accd", jj)])
                    else:
                        sch.op("dve", lambda e: e.tensor_tensor(out=an, in0=bn, in1=an, op=ALU.add),
                               reads=[bkey(nb_), ("accn", jj)], writes=[("accn", jj)])
                        sch.op("dve", lambda e: e.tensor_tensor(out=ad, in0=bd, in1=ad, op=ALU.add),
                               reads=[bkey(db_), ("accd", jj)], writes=[("accd", jj)])
                return f

            def finalize_pair(hf, jj, wgv, wgk, tbs=(0, 1, 2, 3)):
                for ent in accq:
                    ent[2]()
                accq[:] = []
                for tb in tbs:
                    it = ct[0] % 3
                    ct[0] += 1
                    t1 = t1s[it]
                    t2 = t2s[it]
                    bg = next_bank([2, 3, 4, 5])
                    proj_fm(wgv, wgk, jj * 128, lambda c: hT[:, c, tb * 512:(tb + 1) * 512], bg)
                    sl = slice(tb * 512, (tb + 1) * 512)
                    sch.op("dve", lambda e: e.reciprocal(out=t1[:, :], in_=accd[:, jj, sl]),
                           reads=[("accd", jj), ("t1", it)], writes=[("t1", it)])
                    sch.op("pool", lambda e: e.tensor_tensor(out=t1[:, :], in0=t1[:, :], in1=accn[:, jj, sl],
                                                             op=ALU.mult),
                           reads=[("accn", jj), ("t1", it)], writes=[("t1", it)])
                    sch.op("act", lambda e: e.activation(out=t2[:, :], in_=banks[bg][:, :], func=AF.Silu),
                           reads=[bkey(bg), ("t2", it)], writes=[("t2", it)])
                    sch.op("pool", lambda e: e.tensor_tensor(out=za[:, 2 * hf + jj, sl], in0=t1[:, :],
                                                             in1=t2[:, :], op=ALU.mult),
                           reads=[("t1", it), ("t2", it)], writes=[("za", 2 * hf + jj, tb)])

            for hf in range(2):
                for g in agroups:
                    dil = DILS[g]
                    L = S // dil
                    wqv, wqk = wq.get(idx["A%d%dq" % (hf, g)])
                    wkv, wkk = wq.get(idx["A%d%dk" % (hf, g)])
                    wvv, wvk = wq.get(idx["A%d%dv" % (hf, g)])
                    nbt = L // 128
                    skipv = (hf == 0 and g == 0 and agroups[0] == 0)
                    for kp in range(0 if skipv else 8):
                        bi = next_bank(WIDE)

                        def fv(pe):
                            for u in range(2):
                                kt = 2 * kp + u
                                r, mb = kt // nbt, kt % nbt
                                st_ = dil * 128 * mb + r
                                for c in range(8):
                                    m = pe.matmul(banks[bi][:, u * 256:(u + 1) * 256],
                                                  lhsT=hT[:, c, st_:st_ + dil * 127 + 1:dil], rhs=wvv[:, c, :],
                                                  start=(c == 0), stop=(c == 7))
                            return m
                        sch.op("pe", fv, reads=[wvk], writes=[bkey(bi)])
                        sch.op("act", lambda e: e.activation(
                            out=vA[:, 2 * kp:2 * kp + 2, :], in_=banks[bi][:, :].rearrange("p (u n) -> p u n", u=2),
                            func=AF.Copy), reads=[bkey(bi)], writes=[("vA", kp)])
                    wq.release(idx["A%d%dv" % (hf, g)])
                    jobs = []
                    rc[0] = 0
                    for jj in range(2 if astop >= 2 else 0):
                        for (wv_, wk_, dstT, dn) in ((wqv, wqk, qT, "qT"), (wkv, wkk, kT, "kT")):
                            for tb in range(4):
                                jobs.append(rope_job(wv_, wk_, dstT, dn, jj, tb, dil))
                    jobs_r = jobs
                    csb[0] = 0
                    spc[0] = 0
                    sbs = dil_superblocks(g)
                    lastg = (g == agroups[-1])
                    fin_tb = {0: set(), 1: set()}
                    if lastg:
                        wgv, wgk = wq.get(idx["GA%d" % hf])
                    jobs = []
                    for jj in range(2 if astop >= 3 else 0):
                        for (pos0, width, units) in sbs:
                            nb_, db_ = ((6, 7), (0, 1))[csb[0] % 2]
                            csb[0] += 1
                            for ui, unit in enumerate(units):
                                acc = None
                                if ui == len(units) - 1:
                                    acc = make_acc(g, jj, pos0, width, g == agroups[0], nb_, db_)
                                jobs.append(unit_job(g, jj, unit, acc, nb_, db_))
                            if lastg and astop >= 4 and dil == 1:
                                done_hi = pos0 + width
                                tbs = tuple(t for t in range(4) if (t + 1) * 512 <= done_hi and t not in fin_tb[jj])
                                if tbs:
                                    fin_tb[jj].update(tbs)
                                    jobs.append((lambda: None, (lambda hf=hf, jj=jj, wgv=wgv, wgk=wgk, tbs=tbs:
                                                                finalize_pair(hf, jj, wgv, wgk, tbs))))
                        if lastg and astop >= 4 and dil != 1:
                            jobs.append((lambda: None, (lambda hf=hf, jj=jj, wgv=wgv, wgk=wgk: finalize_pair(hf, jj, wgv, wgk))))
                    jobs_u = jobs

                    def rel_qk(hf=hf, g=g):
                        wq.release(idx["A%d%dq" % (hf, g)])
                        wq.release(idx["A%d%dk" % (hf, g)])
                    nr = len(jobs_r)
                    if nr >= 4 and len(jobs_u) >= 8:
                        for i in range(nr - 2):
                            jobs_r[i][0]()
                            if i > 0:
                                jobs_r[i - 1][1]()
                        jobs_r[nr - 2][0]()
                        jobs_r[nr - 3][1]()
                        jobs_r[nr - 1][0]()
                        rel_qk()
                        jobs_u[0][0]()
                        jobs_r[nr - 2][1]()
                        jobs_u[1][0]()
                        jobs_r[nr - 1][1]()
                        jobs_u[2][0]()
                        pend = [jobs_u[0][1], jobs_u[1][1], jobs_u[2][1]]
                        for j_ in jobs_u[3:]:
                            j_[0]()
                            pend.append(j_[1])
                            if len(pend) > 6:
                                pend.pop(0)()
                        for f_ in pend:
                            f_()
                    else:
                        if jobs_r:
                            pipeline(jobs_r, 1)
                        rel_qk()
                        if jobs_u:
                            pipeline(jobs_u, 3)
                    for ent in accq:
                        ent[2]()
                    accq[:] = []
                wq.release(idx["GA%d" % hf])
            sch.barrier(pe=False)
    else:
        sch.op("dve", lambda e: e.memset(za[:, :, :], 0.0), writes=["za"])
        sch.barrier()

    vA_cm.__exit__(None, None, None)
    zb = sb("zb", [128, 4, S], BF16).__enter__()
    zc = sb("zc", [128, 4, S], BF16).__enter__()

    ebr_cm = sb("ebr", [128, 8, 14 * 64], BF16)
    ebr = ebr_cm.__enter__()
    if "C" in phases:
        with ExitStack() as es_:
            kmT = es_.enter_context(sb("kmT", [128, 4, 256], BF16))
            vm = es_.enter_context(sb("vm", [128, 2, 512], BF16))
            qcT = es_.enter_context(sb("qcT", [128, 4, S], BF16))
            sgc = es_.enter_context(sb("sgc", [128, 4, S], BF16))
            NPC = 5
            pcs = [es_.enter_context(sb("pc", [128, 1024], BF16)) for _ in range(NPC)]
            t1s = [es_.enter_context(sb("t1", [128, 512], F32)) for _ in range(2)]
            colm = es_.enter_context(sb("colm", [128, 14 * 64], BF16))
            rps = [es_.enter_context(sb("rp", [128, 14 * 64], F32)) for _ in range(2)]
            WIDE = [0, 1, 2, 3, 4, 5]
            kmv, kmk = wq.get(idx["KM"])
            for hp in range(2):
                bi = (0, 1)[hp]

                def fk(pe):
                    for u in range(2):
                        h = 2 * hp + u
                        for c in range(8):
                            m = pe.matmul(banks[bi][:, u * 256:(u + 1) * 256], lhsT=kmv[:, c, h * 128:(h + 1) * 128],
                                          rhs=memT[:, c, :], start=(c == 0), stop=(c == 7))
                    return m
                sch.op("pe", fk, reads=[kmk], writes=[bkey(bi)])
                sch.op("act", lambda e: e.activation(out=kmT[:, 2 * hp:2 * hp + 2, :],
                                                     in_=banks[bi][:, :].rearrange("p (u n) -> p u n", u=2),
                                                     func=AF.Copy), reads=[bkey(bi)], writes=[("kmT", hp)])
            wq.release(idx["KM"])
            vmv, vmk = wq.get(idx["VM"])
            for mt in range(2):
                bi = (6, 7)[mt]

                def fvm(pe):
                    for c in range(8):
                        m = pe.matmul(banks[bi][:, :], lhsT=memT[:, c, mt * 128:(mt + 1) * 128], rhs=vmv[:, c, :],
                                      start=(c == 0), stop=(c == 7))
                    return m
                sch.op("pe", fvm, reads=[vmk], writes=[bkey(bi)])
                sch.op("act", lambda e: e.activation(out=vm[:, mt, :], in_=banks[bi][:, :], func=AF.Copy),
                       reads=[bkey(bi)], writes=[("vm", mt)])
            wq.release(idx["VM"])
            def ebr_step(kk):
                if not ("B" in phases):
                    return
                if kk == 0:
                    sch.dma("sp", out=colm[:, :], in_=c_colmask[:, :], writes=["colm"], semkey="const9")
                if kk < 8:
                    rp = rps[kk % 2]
                    sch.dma("sp", out=rp[:, :], in_=rpbg[:, kk, :], writes=[("rp", kk % 2)], semkey="rp%d" % (kk % 2))
                if 1 <= kk <= 8:
                    h = kk - 1
                    rp = rps[h % 2]
                    sch.op("act", lambda e: e.activation(out=rp[:, :], in_=rp[:, :], func=AF.Exp),
                           reads=[("rp", h % 2)], writes=[("rp", h % 2)])
                    sch.op("pool", lambda e: e.tensor_tensor(
                        out=ebr[:, h, :].rearrange("p (w q) -> p w q", w=14),
                        in0=rp[:, :].rearrange("p (w q) -> p w q", w=14),
                        in1=colm[:, :].rearrange("p (w q) -> p w q", w=14), op=ALU.mult),
                        reads=[("rp", h % 2), "colm"], writes=[("ebr", h)])
            cqv, cqk = wq.get(idx["CQ"])
            gcv, gck = wq.get(idx["GC"])
            sc = float(128 ** -0.5)
            cu = [0]

            def c_job(h, tb):
                ip = cu[0] % NPC
                it = cu[0] % 2
                nb_, db_ = ((6, 7), (0, 1))[cu[0] % 2]
                cu[0] += 1
                pc = pcs[ip]
                t1 = t1s[it]
                sl = slice(tb * 512, (tb + 1) * 512)

                jidx = 4 * h + tb

                def s0():
                    if jidx >= 2 and jidx - 2 <= 8:
                        ebr_step(jidx - 2)
                    bi = 2 if jidx == 1 else 4
                    proj_fm(cqv, cqk, h * 128, lambda c: hT[:, c, sl], bi)
                    sch.op("act", lambda e: e.activation(out=qcT[:, h, sl], in_=banks[bi][:, :], func=AF.Copy),
                           reads=[bkey(bi)], writes=[("qcT", h, tb)])
                    bi2 = 3 if jidx == 1 else 5
                    proj_fm(gcv, gck, h * 128, lambda c: hT[:, c, sl], bi2)
                    sch.op("act", lambda e: e.activation(out=sgc[:, h, sl], in_=banks[bi2][:, :], func=AF.Silu),
                           reads=[bkey(bi2)], writes=[("sgc", h, tb)])

                def s1():
                    bs = 2

                    def fs(pe):
                        for mt in range(2):
                            m = pe.matmul(banks[bs + mt][:, :], lhsT=kmT[:, h, mt * 128:(mt + 1) * 128],
                                          rhs=qcT[:, h, sl], start=True, stop=True)
                        return m
                    sch.op("pe", fs, reads=[("kmT", h // 2), ("qcT", h, tb)], writes=[bkey(bs), bkey(bs + 1)])
                    sch.op("act", lambda e: e.activation(out=pc[:, :], in_=psall[:, bs * 512:(bs + 2) * 512],
                                                         func=AF.Exp, scale=sc),
                           reads=[bkey(bs), bkey(bs + 1)], writes=[("pc", ip)])

                def s2():
                    def fpv(pe):
                        for mt in range(2):
                            pe.matmul(banks[nb_][:, :], lhsT=vm[:, mt, h * 128:(h + 1) * 128],
                                      rhs=pc[:, mt * 512:(mt + 1) * 512], start=(mt == 0), stop=(mt == 1))
                        for mt in range(2):
                            m = pe.matmul(banks[db_][:, :], lhsT=ones[:, :], rhs=pc[:, mt * 512:(mt + 1) * 512],
                                          start=(mt == 0), stop=(mt == 1))
                        return m
                    sch.op("pe", fpv, reads=[("pc", ip), ("vm", 0), ("vm", 1), "ones"],
                           writes=[bkey(nb_), bkey(db_)])
                    sch.op("dve", lambda e: e.reciprocal(out=t1[:, :], in_=banks[db_][:, :]),
                           reads=[bkey(db_), ("t1", it)], writes=[("t1", it)])
                    sch.op("dve", lambda e: e.tensor_tensor(out=t1[:, :], in0=banks[nb_][:, :], in1=t1[:, :],
                                                            op=ALU.mult),
                           reads=[bkey(nb_), ("t1", it)], writes=[("t1", it)])
                    sch.op("pool", lambda e: e.tensor_tensor(out=zc[:, h, sl], in0=t1[:, :], in1=sgc[:, h, sl],
                                                             op=ALU.mult),
                           reads=[("t1", it), ("sgc", h, tb)], writes=[("zc", h, tb)])
                return s0, s1, s2
            pipeline_n([c_job(h, tb) for h in range(4) for tb in range(4)], [1, 3])
            wq.release(idx["CQ"])
            wq.release(idx["GC"])
            sch.barrier(pe=False)
    else:
        sch.op("dve", lambda e: e.memset(zc[:, :, :], 0.0), writes=["zc"])
        sch.barrier()

    if "B" in phases:
        with ExitStack() as es_:
            qT = es_.enter_context(sb("qT", [128, 2, S], BF16))
            kT = es_.enter_context(sb("kT", [128, 2, S], BF16))
            vE = es_.enter_context(sb("vE", [128, 16, 256], BF16))
            vO = es_.enter_context(sb("vO", [128, 16, 256], BF16))
            sgb = es_.enter_context(sb("sgb", [128, 2, S], BF16))
            NPB = 8
            pBs = [es_.enter_context(sb("pB", [128, 512], BF16)) for _ in range(NPB)]
            pns = [es_.enter_context(sb("pn", [128, 512], BF16)) for _ in range(NPB)]
            t1s = [es_.enter_context(sb("t1", [128, 256], F32)) for _ in range(2)]
            groups = na_groups()
            rcb = [0]

            def nextb():
                b_ = [2, 3, 4, 5, 6, 7, 0, 1][rcb[0] % 8]
                rcb[0] += 1
                return b_
            cu = [0]
            cg = [0]
            sc = 0.125
            vkeys = [("v", p, k_) for p in range(2) for k_ in range(8)]

            def tile_job(jj, pair, r0, step, tile, first, last, nb_, db_, fin, tidx=0):
                (par, vidx, ktok, g_lo, g_hi, w0, wstep) = tile
                ip = cu[0] % NPB
                meng = "pool" if tidx in (1, 2) else "dve"
                cu[0] += 1
                pB = pBs[ip]
                pn = pns[ip]
                ng = g_hi - g_lo + 1
                qkeys = [("qT", jj, t) for t in range(4)] + [("kT", jj, t) for t in range(4)]

                def s1():
                    bs = next_bank([2, 4])

                    def fs(pe):
                        for h in range(2):
                            qv = qT[h * 64:(h + 1) * 64, jj, :].rearrange("p (r q) -> p r q", q=64)
                            rr0 = r0 + step * g_lo
                            qv = qv[:, rr0:rr0 + step * (ng - 1) + 1:step, :]
                            m = pe.matmul(banks[bs + h][:, 64 * g_lo:64 * (g_hi + 1)],
                                          lhsT=kT[h * 64:(h + 1) * 64, jj, ktok:ktok + 128], rhs=qv,
                                          start=True, stop=True)
                        return m
                    sch.op("pe", fs, reads=qkeys, writes=[bkey(bs), bkey(bs + 1)])
                    sv = psall[:, bs * 512:(bs + 2) * 512].rearrange("p (h x) -> p h x", h=2)[:, :, 0:256]
                    sv = sv.rearrange("p h (g q) -> p h g q", g=4)[:, :, g_lo:g_hi + 1, :]
                    pv = pB[:, :].rearrange("p (h g q) -> p h g q", h=2, g=4)[:, :, g_lo:g_hi + 1, :]
                    pnv = pn[:, :].rearrange("p (h g q) -> p h g q", h=2, g=4)[:, :, g_lo:g_hi + 1, :]
                    sch.op("act", lambda e: e.activation(out=pv, in_=sv, func=AF.Exp, scale=sc),
                           reads=[bkey(bs), bkey(bs + 1)], writes=[("pB", ip)])
                    ev = ebr[:, 2 * pair:2 * pair + 2, :].rearrange("p h (w q) -> p h w q", w=14)
                    ev = ev[:, :, w0:w0 + wstep * (ng - 1) + 1:wstep, :]
                    sch.op(meng, lambda e: e.tensor_tensor(out=pnv, in0=pv, in1=ev, op=ALU.mult),
                           reads=[("pB", ip), ("ebr", 2 * pair), ("ebr", 2 * pair + 1)], writes=[("pn", ip)])

                def s2():
                    vt = vE if par == "e" else vO
                    def fpv(pe):
                        cs = slice(64 * g_lo, 64 * (g_hi + 1))
                        for h in range(2):
                            ps_ = slice(h * 256 + 64 * g_lo, h * 256 + 64 * (g_hi + 1))
                            pe.matmul(banks[nb_][h * 64:(h + 1) * 64, cs],
                                      lhsT=vt[:, vidx, (2 * jj + h) * 64:(2 * jj + h + 1) * 64], rhs=pn[:, ps_],
                                      start=first, stop=last)
                        for h in range(2):
                            ps_ = slice(h * 256 + 64 * g_lo, h * 256 + 64 * (g_hi + 1))
                            m = pe.matmul(banks[db_][h * 64:(h + 1) * 64, cs], lhsT=ones[:, 0:64], rhs=pn[:, ps_],
                                          start=first, stop=last)
                        return m
                    sch.op("pe", fpv, reads=[("pn", ip), "ones"] + vkeys, writes=[bkey(nb_), bkey(db_)])
                    if last:
                        fin()
                return s1, s2

            def make_fin(jj, pair, r0, step, nb_, db_):
                def f():
                    it = cg[0] % 2
                    cg[0] += 1
                    t1 = t1s[it]
                    sch.op("dve", lambda e: e.reciprocal(out=t1[:, :], in_=banks[db_][:, 0:256]),
                           reads=[bkey(db_), ("t1", it)], writes=[("t1", it)])
                    sch.op("dve", lambda e: e.tensor_tensor(out=t1[:, :], in0=banks[nb_][:, 0:256], in1=t1[:, :],
                                                            op=ALU.mult),
                           reads=[bkey(nb_), ("t1", it)], writes=[("t1", it)])
                    zv = zb[:, pair, :].rearrange("p (r q) -> p r q", q=64)[:, r0:r0 + 3 * step + 1:step, :]
                    gv = sgb[:, jj, :].rearrange("p (r q) -> p r q", q=64)[:, r0:r0 + 3 * step + 1:step, :]
                    sch.op("pool", lambda e: e.tensor_tensor(
                        out=zv, in0=t1[:, :].rearrange("p (g q) -> p g q", g=4), in1=gv, op=ALU.mult),
                        reads=[("t1", it)] + [("sgb", jj, t) for t in range(4)], writes=[("zb", pair, r0)])
                return f

            for hf in range(2):
                wqv, wqk = wq.get(idx["B%dq" % hf])
                wkv, wkk = wq.get(idx["B%dk" % hf])
                wvv, wvk = wq.get(idx["B%dv" % hf])
                for (vt, par) in ((vE, 0),):
                    for kp in range(8):
                        bi = nextb()
                        nu = 2 if (par == 0 or kp < 7) else 1

                        def fv(pe):
                            for u in range(nu):
                                t = 2 * kp + u
                                st_ = 128 * t + 64 * par
                                for c in range(8):
                                    m = pe.matmul(banks[bi][:, u * 256:(u + 1) * 256], lhsT=hT[:, c, st_:st_ + 128],
                                                  rhs=wvv[:, c, :], start=(c == 0), stop=(c == 7))
                            return m
                        sch.op("pe", fv, reads=[wvk], writes=[bkey(bi)])
                        sch.op("act", lambda e: e.activation(
                            out=vt[:, 2 * kp:2 * kp + nu, :],
                            in_=banks[bi][:, 0:nu * 256].rearrange("p (u n) -> p u n", u=nu), func=AF.Copy),
                            reads=[bkey(bi)], writes=[("v", par, kp)])
                wq.release(idx["B%dv" % hf])
                ev_keys = [("v", 0, kp_) for kp_ in range(8)]
                od_keys = [("v", 1, kp_) for kp_ in range(8)]
                sch.dma("sp", out=vO[0:64, 0:15, :], in_=vE[64:128, 0:15, :], reads=ev_keys, writes=od_keys,
                        semkey="vsh0")
                sch.dma("sp", out=vO[64:128, 0:15, :], in_=vE[0:64, 1:16, :], reads=ev_keys, writes=od_keys,
                        semkey="vsh1")
                for jj in range(2):
                    for (wv_, wk_, dstT, dn) in ((wqv, wqk, qT, "qT"), (wkv, wkk, kT, "kT")):
                        for tb in range(4):
                            bx = nextb()
                            proj_fm(wv_, wk_, jj * 128, lambda c: hT[:, c, tb * 512:(tb + 1) * 512], bx)
                            sch.op("act", lambda e: e.activation(out=dstT[:, jj, tb * 512:(tb + 1) * 512],
                                                                 in_=banks[bx][:, :], func=AF.Copy),
                                   reads=[bkey(bx)], writes=[(dn, jj, tb)])
                wq.release(idx["B%dq" % hf])
                wq.release(idx["B%dk" % hf])
                wgv, wgk = wq.get(idx["GB%d" % hf])
                for jj in range(2):
                    for tb in range(4):
                        bx = nextb()
                        proj_fm(wgv, wgk, jj * 128, lambda c: hT[:, c, tb * 512:(tb + 1) * 512], bx)
                        sch.op("act", lambda e: e.activation(out=sgb[:, jj, tb * 512:(tb + 1) * 512],
                                                             in_=banks[bx][:, :], func=AF.Silu),
                               reads=[bkey(bx)], writes=[("sgb", jj, tb)])
                wq.release(idx["GB%d" % hf])
                jobs = []
                for jj in range(2):
                    pair = 2 * hf + jj
                    for (r0, step, tiles) in groups:
                        nb_, db_ = ((6, 7), (0, 1))[cg[0] % 2] if False else ((6, 7), (0, 1))[len(jobs) % 2]
                        fin = make_fin(jj, pair, r0, step, nb_, db_)
                        gj = []
                        full = [t for t in tiles if t[3] == 0 and t[4] == 3]
                        tiles_o = [full[0]] + [t for t in tiles if t is not full[0]]
                        for ti, tile in enumerate(tiles_o):
                            gj.append(tile_job(jj, pair, r0, step, tile, ti == 0, ti == len(tiles_o) - 1, nb_, db_, fin, tidx=ti))
                        jobs.append(gj)
                flat = []
                for gi, gj in enumerate(jobs):
                    flat.extend(gj)
                pipeline(flat, 6)
            sch.barrier(pe=False)
    else:
        sch.op("dve", lambda e: e.memset(zb[:, :, :], 0.0), writes=["zb"])
        sch.barrier()

    ebr_cm.__exit__(None, None, None)
    if debug:
        sch.dma("sp", out=dbg["za"][:, :, :], in_=za[:, :, :], semkey="dbg")
        sch.dma("sp", out=dbg["zb"][:, :, :], in_=zb[:, :, :], semkey="dbg")
        sch.dma("sp", out=dbg["zc"][:, :, :], in_=zc[:, :, :], semkey="dbg")
        sch.barrier()

    yT = sb("yT", [128, 8, S], BF16).__enter__()
    zs = (za, zb, zc)
    with ExitStack() as es_:
        wbs = [es_.enter_context(sb("wbr", [128, 4, D], BF16)) for _ in range(3)]
        for b in range(3):
            for nh_ in range(2):
                sch.dma("pool", out=wbs[b][:, :, nh_ * 512:(nh_ + 1) * 512],
                        in_=w_br[b][:, nh_ * 512:(nh_ + 1) * 512].rearrange("(c p) n -> p c n", p=128),
                        writes=[("wbr", b, nh_)], semkey="wbr%d%d" % (b, nh_))
        sg0 = es_.enter_context(sb("sg0", [128, 512], F32))
        sg1 = es_.enter_context(sb("sg1", [128, 512], F32))
        sg2 = es_.enter_context(sb("sg2", [128, 512], F32))
        tt0 = es_.enter_context(sb("tt0", [128, 512], F32))
        tt1 = es_.enter_context(sb("tt1", [128, 512], F32))
        tt2 = es_.enter_context(sb("tt2", [128, 512], F32))
        uu = es_.enter_context(sb("uu", [128, 512], F32))
        sgs = (sg0, sg1, sg2)
        tts = (tt0, tt1, tt2)
        for nh in range(2):
            mlw = [wq.get(idx["ML%d%d" % (nh, b)]) for b in range(3)]
            wbw = [(wbs[b][:, :, nh * 512:(nh + 1) * 512], ("wbr", b, nh)) for b in range(3)]
            for nc4 in range(4):
                nch = 4 * nh + nc4
                for tb in range(4):
                    sl = slice(tb * 512, (tb + 1) * 512)
                    for b in range(3):
                        proj_fm(mlw[b][0], mlw[b][1], nc4 * 128, lambda c: hT[:, c, sl], b)
                        sch.op("act", lambda e: e.activation(out=sgs[b][:, :], in_=banks[b][:, :], func=AF.Sigmoid,
                                                             bias=mb_sb[:, b * 8 + nch:b * 8 + nch + 1]),
                               reads=[bkey(b), "mb"], writes=[("sg", b)])
                        if b == 0:
                            zk = [("za", c_, tb) for c_ in range(4)]
                        elif b == 2:
                            zk = [("zc", c_, tb) for c_ in range(4)]
                        else:
                            zk = [("zb", c_, r0_) for c_ in range(4) for r0_ in (0, 4, 5, 12, 13, 20, 21, 28)]
                        proj_fm(wbw[b][0], wbw[b][1], nc4 * 128, lambda c: zs[b][:, c, sl], 3 + b, nchunk=4,
                                extra_reads=zk)
                        sch.op("dve", lambda e: e.tensor_tensor(out=tts[b][:, :], in0=banks[3 + b][:, :],
                                                                in1=sgs[b][:, :], op=ALU.mult),
                               reads=[bkey(3 + b), ("sg", b)], writes=[("tt", b)])
                    sch.op("pool", lambda e: e.tensor_tensor(out=uu[:, :], in0=tt0[:, :], in1=tt1[:, :], op=ALU.add),
                           reads=[("tt", 0), ("tt", 1)], writes=["uu"])
                    sch.op("pool", lambda e: e.tensor_tensor(out=yT[:, nch, sl], in0=uu[:, :], in1=tt2[:, :],
                                                             op=ALU.add),
                           reads=["uu", ("tt", 2)], writes=[("yT", nch, tb)])
            for b in range(3):
                wq.release(idx["ML%d%d" % (nh, b)])
        sch.barrier(pe=False)
    if debug:
        sch.dma("sp", out=dbg["yT"][:, :, :], in_=yT[:, :, :], semkey="dbg")
        sch.barrier()

    with ExitStack() as es_:
        gbc = es_.enter_context(sb("gbc", [128, D], F32))
        xss = [es_.enter_context(sb("xs", [128, D], F32)) for _ in range(2)]
        osbs = [es_.enter_context(sb("osb", [128, D], F32)) for _ in range(2)]
        sq = es_.enter_context(sb("sq", [128, D], F32))
        sss = [es_.enter_context(sb("ss", [128, 1], F32)) for _ in range(2)]
        rss = [es_.enter_context(sb("rs", [128, 1], F32)) for _ in range(2)]
        rstds = [es_.enter_context(sb("rstd", [128, 1], F32)) for _ in range(2)]
        ons = [es_.enter_context(sb("on", [128, D], F32)) for _ in range(2)]
        ots = [es_.enter_context(sb("ot", [128, D], F32)) for _ in range(2)]
        sch.dma("sp", out=gbc[:, :], in_=g_post[:, :], writes=["gbc"], semkey="const10")
        wo = [wq.get(idx["WO%d" % nh]) for nh in range(2)]

        def o_job(tt):
            i = tt % 2
            xs, osb, ss, rs, rstd, on, ot = xss[i], osbs[i], sss[i], rss[i], rstds[i], ons[i], ots[i]

            def s1():
                sch.dma("sp", out=xs[:, :], in_=x[tt * 128:(tt + 1) * 128, :], writes=[("xs", i)],
                        semkey="fx%d" % i)
                for nh in range(2):
                    bi = 2 * (tt % 4) + nh

                    def fo(pe):
                        for c in range(8):
                            m = pe.matmul(banks[bi][:, :], lhsT=yT[:, c, tt * 128:(tt + 1) * 128],
                                          rhs=wo[nh][0][:, c, :], start=(c == 0), stop=(c == 7))
                        return m
                    sch.op("pe", fo, reads=[wo[nh][1]] + [("yT", c_, tt // 4) for c_ in range(8)], writes=[bkey(bi)])
                    sch.op("act", lambda e: e.activation(out=osb[:, nh * 512:(nh + 1) * 512], in_=banks[bi][:, :],
                                                         func=AF.Copy), reads=[bkey(bi)], writes=[("osb", i, nh)])
                sch.op("act", lambda e: e.activation(out=sq[:, :], in_=osb[:, :], func=AF.Square,
                                                     accum_out=ss[:, 0:1]),
                       reads=[("osb", i, 0), ("osb", i, 1)], writes=["sq", ("ss", i)])
                sch.op("act", lambda e: e.activation(out=rs[:, 0:1], in_=ss[:, 0:1], func=AF.Sqrt,
                                                     bias=epsT[:, 0:1], scale=1.0 / D),
                       reads=[("ss", i)], writes=[("rs", i)])
                sch.op("dve", lambda e: e.reciprocal(out=rstd[:, 0:1], in_=rs[:, 0:1]), reads=[("rs", i)],
                       writes=[("rstd", i)])

            def s2():
                sch.op("dve", lambda e: e.scalar_tensor_tensor(out=on[:, :], in0=osb[:, :], scalar=rstd[:, 0:1],
                                                               in1=gbc[:, :], op0=ALU.mult, op1=ALU.mult),
                       reads=[("osb", i, 0), ("osb", i, 1), ("rstd", i), "gbc"], writes=[("on", i)])
                sch.op("dve", lambda e: e.tensor_tensor(out=ot[:, :], in0=on[:, :], in1=xs[:, :], op=ALU.add),
                       reads=[("on", i), ("xs", i)], writes=[("ot", i)])
                sch.dma("sp", out=out[tt * 128:(tt + 1) * 128, :], in_=ot[:, :], reads=[("ot", i)],
                        semkey="out%d" % i)
            return s1, s2
        pipeline([o_job(tt) for tt in range(16)], 1)
        sch.barrier()
    return nc


def host_consts():
    bf = ml_dtypes.bfloat16
    c = {}
    c["c_ident"] = np.eye(128, dtype=np.float32).astype(bf)
    perm = np.zeros((128, 128), np.float32)
    for m in range(128):
        d = m % 64
        if d < 8:
            perm[m + 8, m] = 1.0
        elif d < 16:
            perm[m - 8, m] = 1.0
    c["c_perm"] = perm.astype(bf)
    pos = np.arange(S, dtype=np.float32)
    half = 8
    inv = (np.float32(500000.0) ** (-np.arange(half, dtype=np.float32) * np.float32(2.0) / np.float32(16))).astype(np.float32)
    ang = pos[None, :] * inv[:, None]
    cos, sin = np.cos(ang).astype(np.float32), np.sin(ang).astype(np.float32)
    C = np.ones((128, S), np.float32)
    Sg = np.zeros((128, S), np.float32)
    for p in range(128):
        d = p % 64
        if d < 8:
            C[p] = cos[d]
            Sg[p] = -sin[d]
        elif d < 16:
            C[p] = cos[d - 8]
            Sg[p] = sin[d - 8]
    c["c_ropeC"] = C
    c["c_ropeS"] = Sg
    k = np.arange(128)[:, None]
    q = np.arange(128)[None, :]
    mU = (k >= q).astype(np.float32)
    mL = (k <= q).astype(np.float32)
    mB = (np.abs(k - q) <= 64).astype(np.float32)
    mask = np.zeros((128, 3, 2, 2, 128), np.float32)
    for h in range(2):
        mask[:, 0, h, 0] = mU
        mask[:, 0, h, 1] = mL
        mask[:, 1, h, 0] = mU
        mask[:, 1, h, 1] = mL
        mask[:, 2, h, 0] = mB
        mask[:, 2, h, 1] = mB
    c["c_mask"] = mask.astype(bf)
    kc = np.arange(64)
    qc = np.arange(64)
    cstart = np.clip(qc - 8, 0, 48)
    cm = ((kc[:, None] >= cstart[None, :]) & (kc[:, None] < cstart[None, :] + 16)).astype(np.float32)
    c["c_colmask"] = np.tile(np.concatenate([cm, cm], axis=0), (1, 14)).astype(bf)
    return c


def gather_rpb(rpb):
    kc = np.arange(64)
    qc = np.arange(64)
    dc = np.clip(kc[:, None] - qc[None, :], -15, 15) + 15
    outp = np.zeros((2, 64, 8, 14, 64), np.float32)
    for w in range(14):
        d0 = 6 - w
        for i in range(2):
            outp[i, :, :, w, :] = np.transpose(rpb[:, d0 + i + 7][:, dc], (1, 0, 2))
    return np.ascontiguousarray(outp.reshape(128, 8, 14 * 64))


_NC_CACHE = {}


def make_in_maps(x, mem, pre_norm, w_in, merge_bias, na_rpb, mem_norm, w_mem_kv, w_branch_a, w_branch_b,
                 w_branch_c, w_out, post_norm, cores):
    f = lambda a: np.ascontiguousarray(np.asarray(a, dtype=np.float32))
    c = host_consts()
    shared = dict(c)
    shared["w_in"] = f(w_in[0])
    shared["w_mem"] = f(w_mem_kv[0])
    shared["w_br0"] = f(w_branch_a[0])
    shared["w_br1"] = f(w_branch_b[0])
    shared["w_br2"] = f(w_branch_c[0])
    shared["w_out"] = f(w_out[0])
    shared["g_pre"] = f(np.broadcast_to(np.asarray(pre_norm[0])[None, :], (128, D)))
    shared["g_mem"] = f(np.broadcast_to(np.asarray(mem_norm[0])[None, :], (128, D)))
    shared["g_post"] = f(np.broadcast_to(np.asarray(post_norm[0])[None, :], (128, D)))
    shared["mbias"] = f(np.asarray(merge_bias[0]).reshape(3, 8, 128).transpose(2, 0, 1).reshape(128, 24))
    shared["rpbg"] = gather_rpb(np.asarray(na_rpb[0], dtype=np.float32))
    maps = []
    for b in cores:
        m = dict(shared)
        m["x"] = f(x[b])
        m["mem"] = f(mem[b])
        maps.append(m)
    return maps


def kernel(x, mem, pre_norm, w_in, merge_bias, na_rpb, mem_norm, w_mem_kv, w_branch_a, w_branch_b, w_branch_c,
           w_out, post_norm):
    x = np.asarray(x)
    mem = np.asarray(mem)
    nc = build()
    maps = make_in_maps(x, mem, pre_norm, w_in, merge_bias, na_rpb, mem_norm, w_mem_kv, w_branch_a, w_branch_b,
                        w_branch_c, w_out, post_norm, list(range(8)))
    res = run_bass_kernel_spmd(nc, maps, core_ids=list(range(8)))
    return np.stack([np.asarray(r["out"], dtype=np.float32) for r in res.results], axis=0)
```

```python
import numpy as np
from contextlib import ExitStack
import ml_dtypes
import concourse.bass as bass
import concourse.mybir as mybir
from concourse.bass_utils import run_bass_kernel_spmd

F32 = mybir.dt.float32
BF16 = mybir.dt.bfloat16
AF = mybir.ActivationFunctionType
ALU = mybir.AluOpType
AX = mybir.AxisListType

S = 2048
D = 1024
NIN = 11264
DILS = (1, 4, 16)
OFF_B = 4608
OFF_CQ = 6144
OFF_GA = 6656
OFF_GB = 7168
OFF_GC = 7680
OFF_ML = 8192
EPS = 1e-6
NW = 6


class Sched:
    def __init__(self, nc):
        self.nc = nc
        self.eng = dict(pe=nc.tensor, act=nc.scalar, dve=nc.vector, pool=nc.gpsimd, sp=nc.sync)
        self.sem = {}
        self.cnt = {}
        for e in ("pe", "act", "dve", "pool"):
            self.sem[e] = nc.semaphore("s_" + e).__enter__()
            self.cnt[e] = 0
        self.waited = {e: {} for e in self.eng}
        self.state = {}
        self.dsem = {}

    def _wait(self, e, tok):
        if tok is None:
            return
        name, sem, val, src = tok
        if src == e and e == "pe":
            return
        w = self.waited[e]
        if w.get(name, 0) >= val:
            return
        self.eng[e].wait_ge(sem, val)
        w[name] = val

    def deps(self, e, reads, writes):
        for k in reads:
            st = self.state.get(k)
            if st:
                self._wait(e, st[0])
                if isinstance(k, tuple) and k[0] == "ps":
                    for src, t in st[1].items():
                        if src != e:
                            self._wait(e, t)
        for k in writes:
            st = self.state.get(k)
            if st:
                self._wait(e, st[0])
                for t in st[1].values():
                    self._wait(e, t)

    def commit(self, tok, reads, writes):
        src = tok[3]
        for k in writes:
            self.state[k] = [tok, {}]
        for k in reads:
            if k in writes:
                continue
            st = self.state.setdefault(k, [None, {}])
            st[1][src] = tok

    def op(self, e, fn, reads=(), writes=()):
        self.deps(e, reads, writes)
        ins = fn(self.eng[e])
        self.cnt[e] += 1
        ins.then_inc(self.sem[e], 1)
        tok = ("s_" + e, self.sem[e], self.cnt[e], e)
        self.commit(tok, reads, writes)
        return tok

    def dma(self, e, out, in_, reads=(), writes=(), semkey=None):
        self.deps(e, reads, writes)
        if semkey not in self.dsem:
            self.dsem[semkey] = [self.nc.semaphore("d_" + semkey).__enter__(), 0]
        d = self.dsem[semkey]
        d[1] += 16
        self.eng[e].dma_start(out=out, in_=in_).then_inc(d[0], 16)
        tok = ("d_" + semkey, d[0], d[1], "dma_" + semkey)
        self.commit(tok, reads, writes)
        return tok

    def barrier(self, pe=True):
        toks = [("s_" + e, self.sem[e], self.cnt[e], e) for e in self.sem if self.cnt[e] > 0]
        toks += [("d_" + k, d[0], d[1], "dma_" + k) for k, d in self.dsem.items()]
        for e in self.eng:
            if e == "pe" and not pe:
                continue
            for t in toks:
                if t[3] == e and e == "pe":
                    continue
                self._wait(e, t)


class WQ:
    def __init__(self, sch, ring, specs, first=2):
        self.sch = sch
        self.ring = ring
        self.specs = specs
        self.issued = 0
        self.released = set()
        self.limit = first
        self.extra = ()
        self.try_issue()

    def unlimit(self, extra_reads=()):
        self.limit = None
        self.extra = tuple(extra_reads)
        self.try_issue()
        self.extra = ()

    def try_issue(self):
        n = len(self.ring)
        while self.issued < len(self.specs):
            j = self.issued
            if self.limit is not None and j >= self.limit:
                break
            if j >= n and (j - n) not in self.released:
                break
            slot = j % n
            src, nch = self.specs[j]
            ncols = src.shape[1]
            dst = self.ring[slot][:, 0:nch * ncols].rearrange("p (c n) -> p c n", c=nch)
            self.sch.dma("pool", out=dst, in_=src.rearrange("(c p) n -> p c n", p=128),
                         reads=list(self.extra), writes=[("ring", slot)], semkey="ring%d" % slot)
            self.issued += 1

    def get(self, j):
        assert j < self.issued, (j, self.issued)
        slot = j % len(self.ring)
        src, nch = self.specs[j]
        ncols = src.shape[1]
        v = self.ring[slot][:, 0:nch * ncols].rearrange("p (c n) -> p c n", c=nch)
        return v, ("ring", slot)

    def release(self, j):
        self.released.add(j)
        self.try_issue()


def na_groups():
    groups = []
    tiles = [("e", t, 128 * t, 0, 3, 6 - 2 * t, 1) for t in range(4)]
    groups.append((0, 1, tiles))
    for r0 in (4, 5, 12, 13, 20, 21):
        tiles = []
        for m in range(7):
            kr0 = r0 - 4 + 2 * m
            g_lo, g_hi = max(0, m - 3), min(3, m)
            w0 = 10 - 2 * m + 2 * g_lo
            if r0 % 2 == 0:
                tiles.append(("e", kr0 // 2, 64 * kr0, g_lo, g_hi, w0, 2))
            else:
                tiles.append(("o", (kr0 - 1) // 2, 64 * kr0, g_lo, g_hi, w0, 2))
        groups.append((r0, 2, tiles))
    tiles = [("e", t, 128 * t, 0, 3, 34 - 2 * t, 1) for t in range(12, 16)]
    groups.append((28, 1, tiles))
    return groups


def dil_superblocks(g):
    dil = DILS[g]
    L = S // dil
    nb = L // 128
    sbs = []
    if nb == 1:
        for r0 in range(0, dil, 4):
            units = []
            for rr in range(4):
                r = r0 + rr
                units.append((rr * 128, 128, r * L, [(r * nb + 0, 2, 0)]))
            sbs.append((r0 * L, 512, units))
        return sbs
    for r in range(dil):
        ulist = []
        for qb in range(-1, nb):
            lo, hi = 0, 128
            if qb == -1:
                lo = 64
            if qb == nb - 1:
                hi = 64
            tl = []
            if qb >= 0:
                tl.append((r * nb + qb, 0, lo))
            if qb + 1 < nb:
                tl.append((r * nb + qb + 1, 1, lo))
            ulist.append((r * L + 128 * qb + 64 + lo, hi - lo, tl))
        csz = 5 if nb == 4 else 4
        for i in range(0, len(ulist), csz):
            ch = ulist[i:i + csz]
            pos0 = ch[0][0]
            units = [(u[0] - pos0, u[1], u[0], u[2]) for u in ch]
            width = ch[-1][0] + ch[-1][1] - pos0
            sbs.append((pos0, width, units))
    return sbs


def build(debug=False, phases=("A", "C", "B"), agroups=(0, 1, 2), astop=9):
    nc = bass.Bass("TRN2", target_bir_lowering=False)

    def din(name, shape, dt=F32):
        return nc.dram_tensor(name, list(shape), dt, kind="ExternalInput").ap()

    x = din("x", [S, D])
    mem = din("mem", [256, D])
    w_in = din("w_in", [D, NIN])
    w_mem = din("w_mem", [D, 1024])
    w_br = [din("w_br%d" % b, [512, D]) for b in range(3)]
    w_out = din("w_out", [D, D])
    g_pre = din("g_pre", [128, D])
    g_mem = din("g_mem", [128, D])
    g_post = din("g_post", [128, D])
    mbias = din("mbias", [128, 24])
    rpbg = din("rpbg", [128, 8, 14 * 64])
    c_ident = din("c_ident", [128, 128], BF16)
    c_perm = din("c_perm", [128, 128], BF16)
    c_ropeC = din("c_ropeC", [128, S])
    c_ropeS = din("c_ropeS", [128, S])
    c_mask = din("c_mask", [128, 3, 2, 2, 128], BF16)
    c_colmask = din("c_colmask", [128, 14 * 64], BF16)
    out = nc.dram_tensor("out", [S, D], F32, kind="ExternalOutput").ap()
    dbg = {}
    if debug:
        dbg["hT"] = nc.dram_tensor("dbg_hT", [128, 8, S], BF16, kind="ExternalOutput").ap()
        dbg["za"] = nc.dram_tensor("dbg_za", [128, 4, S], BF16, kind="ExternalOutput").ap()
        dbg["zb"] = nc.dram_tensor("dbg_zb", [128, 4, S], BF16, kind="ExternalOutput").ap()
        dbg["zc"] = nc.dram_tensor("dbg_zc", [128, 4, S], BF16, kind="ExternalOutput").ap()
        dbg["yT"] = nc.dram_tensor("dbg_yT", [128, 8, S], BF16, kind="ExternalOutput").ap()

    sch = Sched(nc)

    uniq = [0]

    def sb(name, shape, dt):
        uniq[0] += 1
        return nc.sbuf_tensor("%s_%d" % (name, uniq[0]), list(shape), dt)

    hT = sb("hT", [128, 8, S], BF16).__enter__()
    za = sb("za", [128, 4, S], BF16).__enter__()
    ring = [sb("ring%d" % i, [128, 4096], BF16).__enter__() for i in range(NW)]
    memT = sb("memT", [128, 8, 256], BF16).__enter__()
    ident = sb("ident", [128, 128], BF16).__enter__()
    ones = sb("ones", [128, 128], BF16).__enter__()
    zeros = sb("zeros", [128, 256], BF16).__enter__()
    epsT = sb("epsT", [128, 1], F32).__enter__()
    mb_sb = sb("mb_sb", [128, 24], F32).__enter__()
    psall = nc.psum_tensor("psall", [128, 8 * 512], F32).__enter__()

    class _Bank:
        def __init__(self, i):
            self.i = i

        def __getitem__(self, key):
            return psall[:, self.i * 512:(self.i + 1) * 512][key]
    banks = [_Bank(i) for i in range(8)]

    def bkey(i):
        return ("ps", i)

    specs = []
    idx = {}

    def add(name, ap, nch):
        idx[name] = len(specs)
        specs.append((ap, nch))

    if "A" in phases:
        for hf in range(2):
            for g in agroups:
                for s, sn in ((2, "v"), (0, "q"), (1, "k")):
                    c0 = 512 * (3 * g + s) + 256 * hf
                    add("A%d%d%s" % (hf, g, sn), w_in[:, c0:c0 + 256], 8)
            add("GA%d" % hf, w_in[:, OFF_GA + 256 * hf:OFF_GA + 256 * hf + 256], 8)
    if "C" in phases:
        add("KM", w_mem[:, 0:512], 8)
        add("VM", w_mem[:, 512:1024], 8)
        add("CQ", w_in[:, OFF_CQ:OFF_CQ + 512], 8)
        add("GC", w_in[:, OFF_GC:OFF_GC + 512], 8)
    if "B" in phases:
        for hf in range(2):
            for s, sn in ((2, "v"), (0, "q"), (1, "k")):
                c0 = OFF_B + 512 * s + 256 * hf
                add("B%d%s" % (hf, sn), w_in[:, c0:c0 + 256], 8)
            add("GB%d" % hf, w_in[:, OFF_GB + 256 * hf:OFF_GB + 256 * hf + 256], 8)
    for nh in range(2):
        for b in range(3):
            c0 = OFF_ML + 1024 * b + 512 * nh
            add("ML%d%d" % (nh, b), w_in[:, c0:c0 + 512], 8)
    for nh in range(2):
        add("WO%d" % nh, w_out[:, 512 * nh:512 * nh + 512], 8)

    sch.dma("sp", out=ident[:, :], in_=c_ident[:, :], writes=["ident"], semkey="const1")
    sch.dma("sp", out=mb_sb[:, :], in_=mbias[:, :], writes=["mb"], semkey="const2")
    sch.op("dve", lambda e: e.memset(ones[:, :], 1.0), writes=["ones"])
    sch.op("dve", lambda e: e.memset(zeros[:, :], 0.0), writes=["zeros"])
    sch.op("dve", lambda e: e.memset(epsT[:, :], EPS), writes=["eps"])

    wq = WQ(sch, ring, specs)

    def norm_tile(src_dram_rows, gbc, xs, sq, ss, rs, rstd, hb, skey, gkey="gbc", k2="", part=None):
        if part in (None, -1):
            sch.dma("sp", out=xs[:, :], in_=src_dram_rows, writes=[skey + "xs"], semkey=skey + "xs")
        if part in (None, 0, 0.1):
            sch.op("act", lambda e: e.activation(out=sq[:, :], in_=xs[:, :], func=AF.Square,
                                                 accum_out=ss[:, 0:1]),
                   reads=[skey + "xs"], writes=["sq" + k2, "ss" + k2])
        if part in (None, 1):
            sch.op("act", lambda e: e.activation(out=rs[:, 0:1], in_=ss[:, 0:1], func=AF.Sqrt,
                                                 bias=epsT[:, 0:1], scale=1.0 / D),
                   reads=["ss" + k2, "eps"], writes=["rs" + k2])
            sch.op("dve", lambda e: e.reciprocal(out=rstd[:, 0:1], in_=rs[:, 0:1]), reads=["rs" + k2],
                   writes=["rstd" + k2])
            sch.op("dve", lambda e: e.scalar_tensor_tensor(out=hb[:, :], in0=xs[:, :], scalar=rstd[:, 0:1],
                                                           in1=gbc[:, :], op0=ALU.mult, op1=ALU.mult),
                   reads=[skey + "xs", "rstd" + k2, gkey], writes=[skey + "hb"])

    def transpose8(hb, hbkey, dst, dkey, bank_i, evac="act"):
        bT = banks[bank_i][:, :].bitcast(BF16)

        def f(pe):
            for c in range(8):
                m = pe.transpose(out=bT[:, c * 128:(c + 1) * 128], in_=hb[:, c * 128:(c + 1) * 128],
                                 identity=ident[:, :])
            return m
        sch.op("pe", f, reads=[hbkey, "ident"], writes=[bkey(bank_i)])
        if evac == "act":
            sch.op("act", lambda e: e.activation(out=dst, in_=bT.rearrange("p (c t) -> p c t", c=8), func=AF.Copy),
                   reads=[bkey(bank_i)], writes=[dkey])
        else:
            sch.op("dve", lambda e: e.tensor_copy(out=dst, in_=bT.rearrange("p (c t) -> p c t", c=8)),
                   reads=[bkey(bank_i)], writes=[dkey])

    pring = [0]

    def next_bank(lst):
        b = lst[pring[0] % len(lst)]
        pring[0] += 1
        return b

    def proj_fm(wv, wkey, col0, rhs_fn, bank_i, nchunk=8, ncols=128, extra_reads=()):
        def f(pe):
            for c in range(nchunk):
                r = rhs_fn(c)
                n_ = int(np.prod(r.shape[1:]))
                m = pe.matmul(banks[bank_i][0:ncols, 0:n_],
                              lhsT=wv[:, c, col0:col0 + ncols], rhs=r, start=(c == 0), stop=(c == nchunk - 1))
            return m
        return sch.op("pe", f, reads=[wkey] + list(extra_reads), writes=[bkey(bank_i)])

    def pipeline(jobs, lag):
        pend = []
        for s1, s2 in jobs:
            s1()
            pend.append(s2)
            if len(pend) > lag:
                pend.pop(0)()
        for f in pend:
            f()

    def pipeline_n(jobs, lags):
        ns = len(jobs[0])
        offs = [0]
        for l in lags:
            offs.append(offs[-1] + l)
        n = len(jobs)
        for it in range(n + offs[-1]):
            for s in range(ns):
                j = it - offs[s]
                if 0 <= j < n:
                    jobs[j][s]()

    vA_cm = sb("vA", [128, 16, 256], BF16)
    vA = vA_cm.__enter__()
    with ExitStack() as es_:
        gbc = es_.enter_context(sb("gbc", [128, D], F32))
        gbm = es_.enter_context(sb("gbm", [128, D], F32))
        NX = 6
        xss = [es_.enter_context(sb("xs", [128, D], F32)) for _ in range(NX)]
        sqs = [es_.enter_context(sb("sq", [128, D], F32)) for _ in range(3)]
        sss = [es_.enter_context(sb("ss", [128, 1], F32)) for _ in range(3)]
        rss = [es_.enter_context(sb("rs", [128, 1], F32)) for _ in range(3)]
        rstds = [es_.enter_context(sb("rstd", [128, 1], F32)) for _ in range(3)]
        hbs = [es_.enter_context(sb("hb", [128, D], BF16)) for _ in range(6)]

        early_v = ("A" in phases) and agroups[0] == 0

        def early_v_proj(kp):
            wvv, wvk = wq.get(idx["A00v"])
            bi = (0, 1, 2, 3)[kp % 4]

            def fv(pe):
                for u in range(2):
                    kt = 2 * kp + u
                    for c in range(8):
                        m = pe.matmul(banks[bi][:, u * 256:(u + 1) * 256], lhsT=hT[:, c, kt * 128:(kt + 1) * 128],
                                      rhs=wvv[:, c, :], start=(c == 0), stop=(c == 7))
                return m
            sch.op("pe", fv, reads=[wvk, ("hT", 2 * kp), ("hT", 2 * kp + 1)], writes=[bkey(bi)])
            sch.op("pool" if False else "dve", lambda e: e.tensor_copy(
                out=vA[:, 2 * kp:2 * kp + 2, :], in_=banks[bi][:, :].rearrange("p (u n) -> p u n", u=2)),
                reads=[bkey(bi)], writes=[("vA", kp)])

        def p0_job(tt):
            i2 = tt % 2
            i3 = tt % 3
            if tt < 16:
                src, g_, gk, dst, dk = x[tt * 128:(tt + 1) * 128, :], gbc, "gbc", hT[:, :, tt * 128:(tt + 1) * 128], ("hT", tt)
            else:
                mt = tt - 16
                src, g_, gk, dst, dk = mem[mt * 128:(mt + 1) * 128, :], gbm, "gbm", memT[:, :, mt * 128:(mt + 1) * 128], ("memT", mt)

            ix = tt % NX
            args = (src, g_, xss[ix], sqs[i3], sss[i3], rss[i3], rstds[i3], hbs[ix], "p0_%d" % ix)

            def sl():
                norm_tile(*args, gkey=gk, k2="_%d" % i3, part=-1)
                if tt == 1:
                    sch.dma("sp", out=gbc[:, :], in_=g_pre[:, :], writes=["gbc"], semkey="const3")
                if tt == 12:
                    sch.dma("sp", out=gbm[:, :], in_=g_mem[:, :], writes=["gbm"], semkey="const8")
                if tt == 15:
                    wq.unlimit(extra_reads=["p0_%dxs" % ix])

            def s0a():
                norm_tile(*args, gkey=gk, k2="_%d" % i3, part=0.1)

            def s0b():
                norm_tile(*args, gkey=gk, k2="_%d" % i3, part=0.2)

            def s1():
                norm_tile(*args, gkey=gk, k2="_%d" % i3, part=1)

            def s2():
                transpose8(hbs[ix], "p0_%dhb" % ix, dst, dk, 6 + i2, evac=("act" if tt % 2 == 0 else "dve"))
                if early_v and 3 <= tt < 18 and tt % 2 == 1:
                    early_v_proj((tt - 3) // 2)
            return sl, s0a, s0b, s1, s2
        pipeline_n([p0_job(tt) for tt in range(18)], [2, 1, 1, 1])
        sch.barrier()
    if debug:
        sch.dma("sp", out=dbg["hT"][:, :, :], in_=hT[:, :, :], semkey="dbg")
        sch.barrier()

    if "A" in phases:
        with ExitStack() as es_:
            accn = es_.enter_context(sb("accn", [128, 2, S], F32))
            accd = es_.enter_context(sb("accd", [128, 2, S], F32))
            qT = es_.enter_context(sb("qT", [128, 2, S], BF16))
            kT = es_.enter_context(sb("kT", [128, 2, S], BF16))
            ropeC = es_.enter_context(sb("ropeC", [128, S], F32))
            ropeS = es_.enter_context(sb("ropeS", [128, S], F32))
            perm = es_.enter_context(sb("perm", [128, 128], BF16))
            maskA = es_.enter_context(sb("maskA", [128, 3 * 512], BF16))
            NQB = 3
            qbs = [es_.enter_context(sb("qb", [128, 512], BF16)) for _ in range(NQB)]
            t1s = [es_.enter_context(sb("t1", [128, 512], F32)) for _ in range(3)]
            t2s = [es_.enter_context(sb("t2", [128, 512], F32)) for _ in range(3)]
            NPB = 8
            pAs = [es_.enter_context(sb("pA", [128, 512], BF16)) for _ in range(NPB)]
            pms = [es_.enter_context(sb("pm", [128, 512], BF16)) for _ in range(NPB)]
            sch.dma("sp", out=ropeC[:, :], in_=c_ropeC[:, :], writes=["ropeC"], semkey="const4")
            sch.dma("sp", out=ropeS[:, :], in_=c_ropeS[:, :], writes=["ropeS"], semkey="const5")
            sch.dma("sp", out=perm[:, :], in_=c_perm[:, :], writes=["perm"], semkey="const6")
            sch.dma("sp", out=maskA[:, :], in_=c_mask.rearrange("k a h t q -> k (a h t q)"), writes=["maskA"],
                    semkey="const7")
            WIDE = [0, 1, 2, 3, 4, 5]
            cq = [0]
            ct = [0]
            cu = [0]
            csb = [0]
            accq = []
            spc = [0]
            rc = [0]
            ROPEB = [2, 3, 4, 5, 6, 7, 0, 1]

            def rope_job(wv_, wk_, dstT, dn, jj, tb, dil):
                iq = cq[0] % NQB
                cq[0] += 1
                it = ct[0] % 3
                ct[0] += 1
                qb = qbs[iq]
                t1 = t1s[it]
                t2 = t2s[it]
                st = {}

                def s1():
                    bx = ROPEB[rc[0] % 8]
                    rc[0] += 1
                    st["bx"] = bx
                    proj_fm(wv_, wk_, jj * 128, lambda c: hT[:, c, tb * 512:(tb + 1) * 512], bx)
                    sch.op("act", lambda e: e.activation(out=qb[:, :], in_=banks[bx][:, :], func=AF.Copy),
                           reads=[bkey(bx)], writes=[("qb", iq)])

                def s2():
                    bx = st["bx"]
                    by = ROPEB[rc[0] % 8]
                    rc[0] += 1
                    sch.op("pe", lambda pe: pe.matmul(banks[by][:, :], lhsT=perm[:, :], rhs=qb[:, :],
                                                      start=True, stop=True),
                           reads=[("qb", iq), "perm"], writes=[bkey(by)])
                    sch.op("dve", lambda e: e.tensor_tensor(out=t1[:, :], in0=banks[bx][:, :],
                                                            in1=ropeC[:, tb * 512:(tb + 1) * 512], op=ALU.mult),
                           reads=[bkey(bx), "ropeC"], writes=[("t1", it)])
                    sch.op("dve", lambda e: e.tensor_tensor(out=t2[:, :], in0=banks[by][:, :],
                                                            in1=ropeS[:, tb * 512:(tb + 1) * 512], op=ALU.mult),
                           reads=[bkey(by), "ropeS"], writes=[("t2", it)])
                    npb = 512 // dil
                    dst = dstT[:, jj, tb * 512:(tb + 1) * 512]
                    i0, i1 = t1[:, :], t2[:, :]
                    sch.op(("pool" if (tb % 2 == 0) else "dve"),
                           lambda e: e.tensor_tensor(out=dst, in0=i0, in1=i1, op=ALU.add),
                           reads=[("t1", it), ("t2", it)], writes=[(dn, jj, tb)])
                return s1, s2

            def unit_job(g, jj, unit, acc_after, nb_, db_):
                (colofs, nq, qpos, tl) = unit
                ip = cu[0] % NPB
                if g == 0:
                    meng = "dve" if (cu[0] % 4) != 3 else "pool"
                else:
                    meng = "dve" if (cu[0] % 2) == 0 else "pool"
                cu[0] += 1
                pA = pAs[ip]
                pm = pms[ip]
                lo = tl[0][2]
                nt = len(tl)
                dil_ = DILS[g]
                L_ = S // dil_
                npb_ = 512 // dil_

                def tok_of(p):
                    return dil_ * (p % L_) + p // L_

                def tbs_of(p0, n):
                    t0_ = tok_of(p0)
                    return range(t0_ // 512, (t0_ + dil_ * (n - 1)) // 512 + 1)
                q0 = tok_of(qpos)
                qkeys = [("qT", jj, t) for t in tbs_of(qpos, nq)]
                for (kt_, mk_, lo__) in tl:
                    qkeys += [("kT", jj, t) for t in tbs_of(kt_ * 128, 128)]
                vkeys_ = sorted(set(("vA", kt_ // 2) for (kt_, mk_, lo__) in tl))

                def s1():
                    bs = 2 if spc[0] < 2 else (4 if spc[0] % 2 == 0 else 2)
                    spc[0] += 1

                    def fs(pe):
                        for ti, (kt, mk, lo_) in enumerate(tl):
                            for h in range(2):
                                kt0 = tok_of(kt * 128)
                                m = pe.matmul(banks[bs + h][:, ti * 128:ti * 128 + nq],
                                              lhsT=kT[h * 64:(h + 1) * 64, jj, kt0:kt0 + dil_ * 127 + 1:dil_],
                                              rhs=qT[h * 64:(h + 1) * 64, jj, q0:q0 + dil_ * (nq - 1) + 1:dil_],
                                              start=True, stop=True)
                        return m
                    sch.op("pe", fs, reads=qkeys, writes=[bkey(bs), bkey(bs + 1)])
                    sv = psall[:, bs * 512:(bs + 2) * 512].rearrange("p (h x) -> p h x", h=2)[:, :, 0:256]
                    sv = sv.rearrange("p h (t q) -> p h t q", t=2)[:, :, 0:nt, 0:nq]
                    pv = pA[:, :].rearrange("p (h t q) -> p h t q", h=2, t=2)[:, :, 0:nt, 0:nq]
                    pmv = pm[:, :].rearrange("p (h t q) -> p h t q", h=2, t=2)[:, :, 0:nt, 0:nq]
                    sch.op("act", lambda e: e.activation(out=pv, in_=sv, func=AF.Exp, scale=0.125),
                           reads=[bkey(bs), bkey(bs + 1)], writes=[("pA", ip)])
                    mk0 = tl[0][1]
                    mv = maskA[:, mk0 * 512:(mk0 + 1) * 512].rearrange("p (h t q) -> p h t q", h=2, t=2)
                    if mk0 == 1:
                        mv = mv[:, :, 1:1 + nt, lo:lo + nq]
                    else:
                        mv = mv[:, :, 0:nt, lo:lo + nq]
                    sch.op(meng, lambda e: e.tensor_tensor(out=pmv, in0=pv, in1=mv, op=ALU.mult),
                           reads=[("pA", ip), "maskA"], writes=[("pm", ip)])

                def s2():
                    def fpv(pe):
                        for ti, (kt, mk, lo_) in enumerate(tl):
                            for h in range(2):
                                pe.matmul(banks[nb_][h * 64:(h + 1) * 64, colofs:colofs + nq],
                                          lhsT=vA[:, kt, (2 * jj + h) * 64:(2 * jj + h + 1) * 64],
                                          rhs=pm[:, (h * 2 + ti) * 128:(h * 2 + ti) * 128 + nq],
                                          start=(ti == 0), stop=(ti == nt - 1))
                        for ti, (kt, mk, lo_) in enumerate(tl):
                            for h in range(2):
                                m = pe.matmul(banks[db_][h * 64:(h + 1) * 64, colofs:colofs + nq],
                                              lhsT=ones[:, 0:64],
                                              rhs=pm[:, (h * 2 + ti) * 128:(h * 2 + ti) * 128 + nq],
                                              start=(ti == 0), stop=(ti == nt - 1))
                        return m
                    keep = []
                    for ent in accq:
                        ent[0] += 1
                        if ent[0] >= 2 or ent[1] == nb_:
                            ent[2]()
                        else:
                            keep.append(ent)
                    accq[:] = keep
                    sch.op("pe", fpv, reads=[("pm", ip), "ones"] + vkeys_,
                           writes=[bkey(nb_), bkey(db_)])
                    if acc_after is not None:
                        accq.append([0, nb_, acc_after])
                return s1, s2

            def make_acc(g, jj, pos0, width, first, nb_, db_):
                dil = DILS[g]
                L = S // dil

                def f():
                    if dil == 1:
                        an = accn[:, jj, pos0:pos0 + width]
                        ad = accd[:, jj, pos0:pos0 + width]
                        bn = banks[nb_][:, 0:width]
                        bd = banks[db_][:, 0:width]
                    elif dil == 4:
                        r = pos0 // L
                        an = accn[:, jj, r::4]
                        ad = accd[:, jj, r::4]
                        bn = banks[nb_][:, 0:512]
                        bd = banks[db_][:, 0:512]
                    else:
                        r0 = pos0 // L
                        an = accn[:, jj, :].rearrange("p (m r) -> p r m", r=16)[:, r0:r0 + 4, :]
                        ad = accd[:, jj, :].rearrange("p (m r) -> p r m", r=16)[:, r0:r0 + 4, :]
                        bn = banks[nb_][:, :].rearrange("p (r m) -> p r m", r=4)
                        bd = banks[db_][:, :].rearrange("p (r m) -> p r m", r=4)
                    if first:
                        sch.op("act", lambda e: e.activation(out=an, in_=bn, func=AF.Copy),
                               reads=[bkey(nb_)], writes=[("accn", jj)])
                        sch.op("act", lambda e: e.activation(out=ad, in_=bd, func=AF.Copy),
                               reads=[bkey(db_)], writes=[("accd", jj)])
                    else:
                        sch.op("dve", lambda e: e.tensor_tensor(out=an, in0=bn, in1=an, op=ALU.add),
                               reads=[bkey(nb_), ("accn", jj)], writes=[("accn", jj)])
                        sch.op("dve", lambda e: e.tensor_tensor(out=ad, in0=bd, in1=ad, op=ALU.add),
                               reads=[bkey(db_), ("accd", jj)], writes=[("accd", jj)])
                return f

            def finalize_pair(hf, jj, wgv, wgk, tbs=(0, 1, 2, 3)):
                for ent in accq:
                    ent[2]()
                accq[:] = []
                for tb in tbs:
                    it = ct[0] % 3
                    ct[0] += 1
                    t1 = t1s[it]
                    t2 = t2s[it]
                    bg = next_bank([2, 3, 4, 5])
                    proj_fm(wgv, wgk, jj * 128, lambda c: hT[:, c, tb * 512:(tb + 1) * 512], bg)
                    sl = slice(tb * 512, (tb + 1) * 512)
                    sch.op("dve", lambda e: e.reciprocal(out=t1[:, :], in_=accd[:, jj, sl]),
                           reads=[("accd", jj), ("t1", it)], writes=[("t1", it)])
                    sch.op("pool", lambda e: e.tensor_tensor(out=t1[:, :], in0=t1[:, :], in1=accn[:, jj, sl],
                                                             op=ALU.mult),
                           reads=[("accn", jj), ("t1", it)], writes=[("t1", it)])
                    sch.op("act", lambda e: e.activation(out=t2[:, :], in_=banks[bg][:, :], func=AF.Silu),
                           reads=[bkey(bg), ("t2", it)], writes=[("t2", it)])
                    sch.op("pool", lambda e: e.tensor_tensor(out=za[:, 2 * hf + jj, sl], in0=t1[:, :],
                                                             in1=t2[:, :], op=ALU.mult),
                           reads=[("t1", it), ("t2", it)], writes=[("za", 2 * hf + jj, tb)])

            for hf in range(2):
                for g in agroups:
                    dil = DILS[g]
                    L = S // dil
                    wqv, wqk = wq.get(idx["A%d%dq" % (hf, g)])
                    wkv, wkk = wq.get(idx["A%d%dk" % (hf, g)])
                    wvv, wvk = wq.get(idx["A%d%dv" % (hf, g)])
                    nbt = L // 128
                    skipv = (hf == 0 and g == 0 and agroups[0] == 0)
                    for kp in range(0 if skipv else 8):
                        bi = next_bank(WIDE)

                        def fv(pe):
                            for u in range(2):
                                kt = 2 * kp + u
                                r, mb = kt // nbt, kt % nbt
                                st_ = dil * 128 * mb + r
                                for c in range(8):
                                    m = pe.matmul(banks[bi][:, u * 256:(u + 1) * 256],
                                                  lhsT=hT[:, c, st_:st_ + dil * 127 + 1:dil], rhs=wvv[:, c, :],
                                                  start=(c == 0), stop=(c == 7))
                            return m
                        sch.op("pe", fv, reads=[wvk], writes=[bkey(bi)])
                        sch.op("act", lambda e: e.activation(
                            out=vA[:, 2 * kp:2 * kp + 2, :], in_=banks[bi][:, :].rearrange("p (u n) -> p u n", u=2),
                            func=AF.Copy), reads=[bkey(bi)], writes=[("vA", kp)])
                    wq.release(idx["A%d%dv" % (hf, g)])
                    jobs = []
                    rc[0] = 0
                    for jj in range(2 if astop >= 2 else 0):
                        for (wv_, wk_, dstT, dn) in ((wqv, wqk, qT, "qT"), (wkv, wkk, kT, "kT")):
                            for tb in range(4):
                                jobs.append(rope_job(wv_, wk_, dstT, dn, jj, tb, dil))
                    jobs_r = jobs
                    csb[0] = 0
                    spc[0] = 0
                    sbs = dil_superblocks(g)
                    lastg = (g == agroups[-1])
                    fin_tb = {0: set(), 1: set()}
                    if lastg:
                        wgv, wgk = wq.get(idx["GA%d" % hf])
                    jobs = []
                    for jj in range(2 if astop >= 3 else 0):
                        for (pos0, width, units) in sbs:
                            nb_, db_ = ((6, 7), (0, 1))[csb[0] % 2]
                            csb[0] += 1
                            for ui, unit in enumerate(units):
                                acc = None
                                if ui == len(units) - 1:
                                    acc = make_acc(g, jj, pos0, width, g == agroups[0], nb_, db_)
                                jobs.append(unit_job(g, jj, unit, acc, nb_, db_))
                            if lastg and astop >= 4 and dil == 1:
                                done_hi = pos0 + width
                                tbs = tuple(t for t in range(4) if (t + 1) * 512 <= done_hi and t not in fin_tb[jj])
                                if tbs:
                                    fin_tb[jj].update(tbs)
                                    jobs.append((lambda: None, (lambda hf=hf, jj=jj, wgv=wgv, wgk=wgk, tbs=tbs:
                                                                finalize_pair(hf, jj, wgv, wgk, tbs))))
                        if lastg and astop >= 4 and dil != 1:
                            jobs.append((lambda: None, (lambda hf=hf, jj=jj, wgv=wgv, wgk=wgk: finalize_pair(hf, jj, wgv, wgk))))
                    jobs_u = jobs

                    def rel_qk(hf=hf, g=g):
                        wq.release(idx["A%d%dq" % (hf, g)])
                        wq.release(idx["A%d%dk" % (hf, g)])
                    nr = len(jobs_r)
                    if nr >= 4 and len(jobs_u) >= 8:
                        for i in range(nr - 2):
                            jobs_r[i][0]()
                            if i > 0:
                                jobs_r[i - 1][1]()
                        jobs_r[nr - 2][0]()
                        jobs_r[nr - 3][1]()
                        jobs_r[nr - 1][0]()
                        rel_qk()
                        jobs_u[0][0]()
                        jobs_r[nr - 2][1]()
                        jobs_u[1][0]()
                        jobs_r[nr - 1][1]()
                        jobs_u[2][0]()
                        pend = [jobs_u[0][1], jobs_u[1][1], jobs_u[2][1]]
                        for j_ in jobs_u[3:]:
                            j_[0]()
                            pend.append(j_[1])
                            if len(pend) > 6:
                                pend.pop(0)()
                        for f_ in pend:
                            f_()
                    else:
                        if jobs_r:
                            pipeline(jobs_r, 1)
                        rel_qk()
                        if jobs_u:
                            pipeline(jobs_u, 3)
                    for ent in accq:
                        ent[2]()
                    accq[:] = []
                wq.release(idx["GA%d" % hf])
            sch.barrier(pe=False)
    else:
        sch.op("dve", lambda e: e.memset(za[:, :, :], 0.0), writes=["za"])
        sch.barrier()

    vA_cm.__exit__(None, None, None)
    zb = sb("zb", [128, 4, S], BF16).__enter__()
    zc = sb("zc", [128, 4, S], BF16).__enter__()

    ebr_cm = sb("ebr", [128, 8, 14 * 64], BF16)
    ebr = ebr_cm.__enter__()
    if "C" in phases:
        with ExitStack() as es_:
            kmT = es_.enter_context(sb("kmT", [128, 4, 256], BF16))
            vm = es_.enter_context(sb("vm", [128, 2, 512], BF16))
            qcT = es_.enter_context(sb("qcT", [128, 4, S], BF16))
            sgc = es_.enter_context(sb("sgc", [128, 4, S], BF16))
            NPC = 5
            pcs = [es_.enter_context(sb("pc", [128, 1024], BF16)) for _ in range(NPC)]
            t1s = [es_.enter_context(sb("t1", [128, 512], F32)) for _ in range(2)]
            colm = es_.enter_context(sb("colm", [128, 14 * 64], BF16))
            rps = [es_.enter_context(sb("rp", [128, 14 * 64], F32)) for _ in range(2)]
            WIDE = [0, 1, 2, 3, 4, 5]
            kmv, kmk = wq.get(idx["KM"])
            for hp in range(2):
                bi = next_bank(WIDE)

                def fk(pe):
                    for u in range(2):
                        h = 2 * hp + u
                        for c in range(8):
                            m = pe.matmul(banks[bi][:, u * 256:(u + 1) * 256], lhsT=kmv[:, c, h * 128:(h + 1) * 128],
                                          rhs=memT[:, c, :], start=(c == 0), stop=(c == 7))
                    return m
                sch.op("pe", fk, reads=[kmk], writes=[bkey(bi)])
                sch.op("act", lambda e: e.activation(out=kmT[:, 2 * hp:2 * hp + 2, :],
                                                     in_=banks[bi][:, :].rearrange("p (u n) -> p u n", u=2),
                                                     func=AF.Copy), reads=[bkey(bi)], writes=[("kmT", hp)])
            wq.release(idx["KM"])
            vmv, vmk = wq.get(idx["VM"])
            for mt in range(2):
                bi = next_bank(WIDE)

                def fvm(pe):
                    for c in range(8):
                        m = pe.matmul(banks[bi][:, :], lhsT=memT[:, c, mt * 128:(mt + 1) * 128], rhs=vmv[:, c, :],
                                      start=(c == 0), stop=(c == 7))
                    return m
                sch.op("pe", fvm, reads=[vmk], writes=[bkey(bi)])
                sch.op("act", lambda e: e.activation(out=vm[:, mt, :], in_=banks[bi][:, :], func=AF.Copy),
                       reads=[bkey(bi)], writes=[("vm", mt)])
            wq.release(idx["VM"])
            def ebr_step(kk):
                if not ("B" in phases):
                    return
                if kk == 0:
                    sch.dma("sp", out=colm[:, :], in_=c_colmask[:, :], writes=["colm"], semkey="const9")
                if kk < 8:
                    rp = rps[kk % 2]
                    sch.dma("sp", out=rp[:, :], in_=rpbg[:, kk, :], writes=[("rp", kk % 2)], semkey="rp%d" % (kk % 2))
                if 1 <= kk <= 8:
                    h = kk - 1
                    rp = rps[h % 2]
                    sch.op("act", lambda e: e.activation(out=rp[:, :], in_=rp[:, :], func=AF.Exp),
                           reads=[("rp", h % 2)], writes=[("rp", h % 2)])
                    sch.op("pool", lambda e: e.tensor_tensor(
                        out=ebr[:, h, :].rearrange("p (w q) -> p w q", w=14),
                        in0=rp[:, :].rearrange("p (w q) -> p w q", w=14),
                        in1=colm[:, :].rearrange("p (w q) -> p w q", w=14), op=ALU.mult),
                        reads=[("rp", h % 2), "colm"], writes=[("ebr", h)])
            cqv, cqk = wq.get(idx["CQ"])
            gcv, gck = wq.get(idx["GC"])
            sc = float(128 ** -0.5)
            cu = [0]

            def c_job(h, tb):
                ip = cu[0] % NPC
                it = cu[0] % 2
                nb_, db_ = ((6, 7), (0, 1))[cu[0] % 2]
                cu[0] += 1
                pc = pcs[ip]
                t1 = t1s[it]
                sl = slice(tb * 512, (tb + 1) * 512)

                jidx = 4 * h + tb

                def s0():
                    if jidx >= 2 and jidx - 2 <= 8:
                        ebr_step(jidx - 2)
                    bi = 4
                    proj_fm(cqv, cqk, h * 128, lambda c: hT[:, c, sl], bi)
                    sch.op("act", lambda e: e.activation(out=qcT[:, h, sl], in_=banks[bi][:, :], func=AF.Copy),
                           reads=[bkey(bi)], writes=[("qcT", h, tb)])
                    bi2 = 5
                    proj_fm(gcv, gck, h * 128, lambda c: hT[:, c, sl], bi2)
                    sch.op("act", lambda e: e.activation(out=sgc[:, h, sl], in_=banks[bi2][:, :], func=AF.Silu),
                           reads=[bkey(bi2)], writes=[("sgc", h, tb)])

                def s1():
                    bs = 2

                    def fs(pe):
                        for mt in range(2):
                            m = pe.matmul(banks[bs + mt][:, :], lhsT=kmT[:, h, mt * 128:(mt + 1) * 128],
                                          rhs=qcT[:, h, sl], start=True, stop=True)
                        return m
                    sch.op("pe", fs, reads=[("kmT", h // 2), ("qcT", h, tb)], writes=[bkey(bs), bkey(bs + 1)])
                    sch.op("act", lambda e: e.activation(out=pc[:, :], in_=psall[:, bs * 512:(bs + 2) * 512],
                                                         func=AF.Exp, scale=sc),
                           reads=[bkey(bs), bkey(bs + 1)], writes=[("pc", ip)])

                def s2():
                    def fpv(pe):
                        for mt in range(2):
                            pe.matmul(banks[nb_][:, :], lhsT=vm[:, mt, h * 128:(h + 1) * 128],
                                      rhs=pc[:, mt * 512:(mt + 1) * 512], start=(mt == 0), stop=(mt == 1))
                        for mt in range(2):
                            m = pe.matmul(banks[db_][:, :], lhsT=ones[:, :], rhs=pc[:, mt * 512:(mt + 1) * 512],
                                          start=(mt == 0), stop=(mt == 1))
                        return m
                    sch.op("pe", fpv, reads=[("pc", ip), ("vm", 0), ("vm", 1), "ones"],
                           writes=[bkey(nb_), bkey(db_)])
                    sch.op("dve", lambda e: e.reciprocal(out=t1[:, :], in_=banks[db_][:, :]),
                           reads=[bkey(db_), ("t1", it)], writes=[("t1", it)])
                    sch.op("dve", lambda e: e.tensor_tensor(out=t1[:, :], in0=banks[nb_][:, :], in1=t1[:, :],
                                                            op=ALU.mult),
                           reads=[bkey(nb_), ("t1", it)], writes=[("t1", it)])
                    sch.op("pool", lambda e: e.tensor_tensor(out=zc[:, h, sl], in0=t1[:, :], in1=sgc[:, h, sl],
                                                             op=ALU.mult),
                           reads=[("t1", it), ("sgc", h, tb)], writes=[("zc", h, tb)])
                return s0, s1, s2
            pipeline_n([c_job(h, tb) for h in range(4) for tb in range(4)], [1, 3])
            wq.release(idx["CQ"])
            wq.release(idx["GC"])
            sch.barrier(pe=False)
    else:
        sch.op("dve", lambda e: e.memset(zc[:, :, :], 0.0), writes=["zc"])
        sch.barrier()

    if "B" in phases:
        with ExitStack() as es_:
            qT = es_.enter_context(sb("qT", [128, 2, S], BF16))
            kT = es_.enter_context(sb("kT", [128, 2, S], BF16))
            vE = es_.enter_context(sb("vE", [128, 16, 256], BF16))
            vO = es_.enter_context(sb("vO", [128, 16, 256], BF16))
            sgb = es_.enter_context(sb("sgb", [128, 2, S], BF16))
            NPB = 8
            pBs = [es_.enter_context(sb("pB", [128, 512], BF16)) for _ in range(NPB)]
            pns = [es_.enter_context(sb("pn", [128, 512], BF16)) for _ in range(NPB)]
            t1s = [es_.enter_context(sb("t1", [128, 256], F32)) for _ in range(2)]
            groups = na_groups()
            rcb = [0]

            def nextb():
                b_ = [2, 3, 4, 5, 6, 7, 0, 1][rcb[0] % 8]
                rcb[0] += 1
                return b_
            cu = [0]
            cg = [0]
            sc = 0.125
            vkeys = [("v", p, k_) for p in range(2) for k_ in range(8)]

            def tile_job(jj, pair, r0, step, tile, first, last, nb_, db_, fin, tidx=0):
                (par, vidx, ktok, g_lo, g_hi, w0, wstep) = tile
                ip = cu[0] % NPB
                meng = "pool" if tidx in (1, 2) else "dve"
                cu[0] += 1
                pB = pBs[ip]
                pn = pns[ip]
                ng = g_hi - g_lo + 1
                qkeys = [("qT", jj, t) for t in range(4)] + [("kT", jj, t) for t in range(4)]

                def s1():
                    bs = next_bank([2, 4])

                    def fs(pe):
                        for h in range(2):
                            qv = qT[h * 64:(h + 1) * 64, jj, :].rearrange("p (r q) -> p r q", q=64)
                            rr0 = r0 + step * g_lo
                            qv = qv[:, rr0:rr0 + step * (ng - 1) + 1:step, :]
                            m = pe.matmul(banks[bs + h][:, 64 * g_lo:64 * (g_hi + 1)],
                                          lhsT=kT[h * 64:(h + 1) * 64, jj, ktok:ktok + 128], rhs=qv,
                                          start=True, stop=True)
                        return m
                    sch.op("pe", fs, reads=qkeys, writes=[bkey(bs), bkey(bs + 1)])
                    sv = psall[:, bs * 512:(bs + 2) * 512].rearrange("p (h x) -> p h x", h=2)[:, :, 0:256]
                    sv = sv.rearrange("p h (g q) -> p h g q", g=4)[:, :, g_lo:g_hi + 1, :]
                    pv = pB[:, :].rearrange("p (h g q) -> p h g q", h=2, g=4)[:, :, g_lo:g_hi + 1, :]
                    pnv = pn[:, :].rearrange("p (h g q) -> p h g q", h=2, g=4)[:, :, g_lo:g_hi + 1, :]
                    sch.op("act", lambda e: e.activation(out=pv, in_=sv, func=AF.Exp, scale=sc),
                           reads=[bkey(bs), bkey(bs + 1)], writes=[("pB", ip)])
                    ev = ebr[:, 2 * pair:2 * pair + 2, :].rearrange("p h (w q) -> p h w q", w=14)
                    ev = ev[:, :, w0:w0 + wstep * (ng - 1) + 1:wstep, :]
                    sch.op(meng, lambda e: e.tensor_tensor(out=pnv, in0=pv, in1=ev, op=ALU.mult),
                           reads=[("pB", ip), ("ebr", 2 * pair), ("ebr", 2 * pair + 1)], writes=[("pn", ip)])

                def s2():
                    vt = vE if par == "e" else vO
                    def fpv(pe):
                        cs = slice(64 * g_lo, 64 * (g_hi + 1))
                        for h in range(2):
                            ps_ = slice(h * 256 + 64 * g_lo, h * 256 + 64 * (g_hi + 1))
                            pe.matmul(banks[nb_][h * 64:(h + 1) * 64, cs],
                                      lhsT=vt[:, vidx, (2 * jj + h) * 64:(2 * jj + h + 1) * 64], rhs=pn[:, ps_],
                                      start=first, stop=last)
                        for h in range(2):
                            ps_ = slice(h * 256 + 64 * g_lo, h * 256 + 64 * (g_hi + 1))
                            m = pe.matmul(banks[db_][h * 64:(h + 1) * 64, cs], lhsT=ones[:, 0:64], rhs=pn[:, ps_],
                                          start=first, stop=last)
                        return m
                    sch.op("pe", fpv, reads=[("pn", ip), "ones"] + vkeys, writes=[bkey(nb_), bkey(db_)])
                    if last:
                        fin()
                return s1, s2

            def make_fin(jj, pair, r0, step, nb_, db_):
                def f():
                    it = cg[0] % 2
                    cg[0] += 1
                    t1 = t1s[it]
                    sch.op("dve", lambda e: e.reciprocal(out=t1[:, :], in_=banks[db_][:, 0:256]),
                           reads=[bkey(db_), ("t1", it)], writes=[("t1", it)])
                    sch.op("dve", lambda e: e.tensor_tensor(out=t1[:, :], in0=banks[nb_][:, 0:256], in1=t1[:, :],
                                                            op=ALU.mult),
                           reads=[bkey(nb_), ("t1", it)], writes=[("t1", it)])
                    zv = zb[:, pair, :].rearrange("p (r q) -> p r q", q=64)[:, r0:r0 + 3 * step + 1:step, :]
                    gv = sgb[:, jj, :].rearrange("p (r q) -> p r q", q=64)[:, r0:r0 + 3 * step + 1:step, :]
                    sch.op("pool", lambda e: e.tensor_tensor(
                        out=zv, in0=t1[:, :].rearrange("p (g q) -> p g q", g=4), in1=gv, op=ALU.mult),
                        reads=[("t1", it)] + [("sgb", jj, t) for t in range(4)], writes=[("zb", pair, r0)])
                return f

            for hf in range(2):
                wqv, wqk = wq.get(idx["B%dq" % hf])
                wkv, wkk = wq.get(idx["B%dk" % hf])
                wvv, wvk = wq.get(idx["B%dv" % hf])
                for (vt, par) in ((vE, 0),):
                    for kp in range(8):
                        bi = nextb()
                        nu = 2 if (par == 0 or kp < 7) else 1

                        def fv(pe):
                            for u in range(nu):
                                t = 2 * kp + u
                                st_ = 128 * t + 64 * par
                                for c in range(8):
                                    m = pe.matmul(banks[bi][:, u * 256:(u + 1) * 256], lhsT=hT[:, c, st_:st_ + 128],
                                                  rhs=wvv[:, c, :], start=(c == 0), stop=(c == 7))
                            return m
                        sch.op("pe", fv, reads=[wvk], writes=[bkey(bi)])
                        sch.op("act", lambda e: e.activation(
                            out=vt[:, 2 * kp:2 * kp + nu, :],
                            in_=banks[bi][:, 0:nu * 256].rearrange("p (u n) -> p u n", u=nu), func=AF.Copy),
                            reads=[bkey(bi)], writes=[("v", par, kp)])
                wq.release(idx["B%dv" % hf])
                ev_keys = [("v", 0, kp_) for kp_ in range(8)]
                od_keys = [("v", 1, kp_) for kp_ in range(8)]
                sch.dma("sp", out=vO[0:64, 0:15, :], in_=vE[64:128, 0:15, :], reads=ev_keys, writes=od_keys,
                        semkey="vsh0")
                sch.dma("sp", out=vO[64:128, 0:15, :], in_=vE[0:64, 1:16, :], reads=ev_keys, writes=od_keys,
                        semkey="vsh1")
                for jj in range(2):
                    for (wv_, wk_, dstT, dn) in ((wqv, wqk, qT, "qT"), (wkv, wkk, kT, "kT")):
                        for tb in range(4):
                            bx = nextb()
                            proj_fm(wv_, wk_, jj * 128, lambda c: hT[:, c, tb * 512:(tb + 1) * 512], bx)
                            sch.op("act", lambda e: e.activation(out=dstT[:, jj, tb * 512:(tb + 1) * 512],
                                                                 in_=banks[bx][:, :], func=AF.Copy),
                                   reads=[bkey(bx)], writes=[(dn, jj, tb)])
                wq.release(idx["B%dq" % hf])
                wq.release(idx["B%dk" % hf])
                wgv, wgk = wq.get(idx["GB%d" % hf])
                for jj in range(2):
                    for tb in range(4):
                        bx = nextb()
                        proj_fm(wgv, wgk, jj * 128, lambda c: hT[:, c, tb * 512:(tb + 1) * 512], bx)
                        sch.op("act", lambda e: e.activation(out=sgb[:, jj, tb * 512:(tb + 1) * 512],
                                                             in_=banks[bx][:, :], func=AF.Silu),
                               reads=[bkey(bx)], writes=[("sgb", jj, tb)])
                wq.release(idx["GB%d" % hf])
                jobs = []
                for jj in range(2):
                    pair = 2 * hf + jj
                    for (r0, step, tiles) in groups:
                        nb_, db_ = ((6, 7), (0, 1))[cg[0] % 2] if False else ((6, 7), (0, 1))[len(jobs) % 2]
                        fin = make_fin(jj, pair, r0, step, nb_, db_)
                        gj = []
                        full = [t for t in tiles if t[3] == 0 and t[4] == 3]
                        tiles_o = [full[0]] + [t for t in tiles if t is not full[0]]
                        for ti, tile in enumerate(tiles_o):
                            gj.append(tile_job(jj, pair, r0, step, tile, ti == 0, ti == len(tiles_o) - 1, nb_, db_, fin, tidx=ti))
                        jobs.append(gj)
                flat = []
                for gi, gj in enumerate(jobs):
                    flat.extend(gj)
                pipeline(flat, 6)
            sch.barrier(pe=False)
    else:
        sch.op("dve", lambda e: e.memset(zb[:, :, :], 0.0), writes=["zb"])
        sch.barrier()

    ebr_cm.__exit__(None, None, None)
    if debug:
        sch.dma("sp", out=dbg["za"][:, :, :], in_=za[:, :, :], semkey="dbg")
        sch.dma("sp", out=dbg["zb"][:, :, :], in_=zb[:, :, :], semkey="dbg")
        sch.dma("sp", out=dbg["zc"][:, :, :], in_=zc[:, :, :], semkey="dbg")
        sch.barrier()

    yT = sb("yT", [128, 8, S], BF16).__enter__()
    zs = (za, zb, zc)
    with ExitStack() as es_:
        wbs = [es_.enter_context(sb("wbr", [128, 4, D], BF16)) for _ in range(3)]
        for b in range(3):
            for nh_ in range(2):
                sch.dma("pool", out=wbs[b][:, :, nh_ * 512:(nh_ + 1) * 512],
                        in_=w_br[b][:, nh_ * 512:(nh_ + 1) * 512].rearrange("(c p) n -> p c n", p=128),
                        writes=[("wbr", b, nh_)], semkey="wbr%d%d" % (b, nh_))
        sg0 = es_.enter_context(sb("sg0", [128, 512], F32))
        sg1 = es_.enter_context(sb("sg1", [128, 512], F32))
        sg2 = es_.enter_context(sb("sg2", [128, 512], F32))
        tt0 = es_.enter_context(sb("tt0", [128, 512], F32))
        tt1 = es_.enter_context(sb("tt1", [128, 512], F32))
        tt2 = es_.enter_context(sb("tt2", [128, 512], F32))
        uu = es_.enter_context(sb("uu", [128, 512], F32))
        sgs = (sg0, sg1, sg2)
        tts = (tt0, tt1, tt2)
        for nh in range(2):
            mlw = [wq.get(idx["ML%d%d" % (nh, b)]) for b in range(3)]
            wbw = [(wbs[b][:, :, nh * 512:(nh + 1) * 512], ("wbr", b, nh)) for b in range(3)]
            for nc4 in range(4):
                nch = 4 * nh + nc4
                for tb in range(4):
                    sl = slice(tb * 512, (tb + 1) * 512)
                    for b in range(3):
                        proj_fm(mlw[b][0], mlw[b][1], nc4 * 128, lambda c: hT[:, c, sl], b)
                        sch.op("act", lambda e: e.activation(out=sgs[b][:, :], in_=banks[b][:, :], func=AF.Sigmoid,
                                                             bias=mb_sb[:, b * 8 + nch:b * 8 + nch + 1]),
                               reads=[bkey(b), "mb"], writes=[("sg", b)])
                        if b == 0:
                            zk = [("za", c_, tb) for c_ in range(4)]
                        elif b == 2:
                            zk = [("zc", c_, tb) for c_ in range(4)]
                        else:
                            zk = [("zb", c_, r0_) for c_ in range(4) for r0_ in (0, 4, 5, 12, 13, 20, 21, 28)]
                        proj_fm(wbw[b][0], wbw[b][1], nc4 * 128, lambda c: zs[b][:, c, sl], 3 + b, nchunk=4,
                                extra_reads=zk)
                        sch.op("dve", lambda e: e.tensor_tensor(out=tts[b][:, :], in0=banks[3 + b][:, :],
                                                                in1=sgs[b][:, :], op=ALU.mult),
                               reads=[bkey(3 + b), ("sg", b)], writes=[("tt", b)])
                    sch.op("pool", lambda e: e.tensor_tensor(out=uu[:, :], in0=tt0[:, :], in1=tt1[:, :], op=ALU.add),
                           reads=[("tt", 0), ("tt", 1)], writes=["uu"])
                    sch.op("pool", lambda e: e.tensor_tensor(out=yT[:, nch, sl], in0=uu[:, :], in1=tt2[:, :],
                                                             op=ALU.add),
                           reads=["uu", ("tt", 2)], writes=[("yT", nch, tb)])
            for b in range(3):
                wq.release(idx["ML%d%d" % (nh, b)])
        sch.barrier(pe=False)
    if debug:
        sch.dma("sp", out=dbg["yT"][:, :, :], in_=yT[:, :, :], semkey="dbg")
        sch.barrier()

    with ExitStack() as es_:
        gbc = es_.enter_context(sb("gbc", [128, D], F32))
        xss = [es_.enter_context(sb("xs", [128, D], F32)) for _ in range(3)]
        osbs = [es_.enter_context(sb("osb", [128, D], F32)) for _ in range(2)]
        sq = es_.enter_context(sb("sq", [128, D], F32))
        sss = [es_.enter_context(sb("ss", [128, 1], F32)) for _ in range(2)]
        rss = [es_.enter_context(sb("rs", [128, 1], F32)) for _ in range(2)]
        rstds = [es_.enter_context(sb("rstd", [128, 1], F32)) for _ in range(2)]
        ons = [es_.enter_context(sb("on", [128, D], F32)) for _ in range(2)]
        sch.dma("sp", out=gbc[:, :], in_=g_post[:, :], writes=["gbc"], semkey="const10")
        wo = [wq.get(idx["WO%d" % nh]) for nh in range(2)]

        def o_job(tt):
            i = tt % 2
            ix = tt % 3
            xs, osb, ss, rs, rstd, on = xss[ix], osbs[i], sss[i], rss[i], rstds[i], ons[i]

            def s1():
                sch.dma("sp", out=xs[:, :], in_=x[tt * 128:(tt + 1) * 128, :], writes=[("xs", ix)],
                        semkey="fx%d" % ix)
                for nh in range(2):
                    bi = 2 * (tt % 3) + nh

                    def fo(pe):
                        for c in range(8):
                            m = pe.matmul(banks[bi][:, :], lhsT=yT[:, c, tt * 128:(tt + 1) * 128],
                                          rhs=wo[nh][0][:, c, :], start=(c == 0), stop=(c == 7))
                        return m
                    sch.op("pe", fo, reads=[wo[nh][1]] + [("yT", c_, tt // 4) for c_ in range(8)], writes=[bkey(bi)])
                    sch.op("act", lambda e: e.activation(out=osb[:, nh * 512:(nh + 1) * 512], in_=banks[bi][:, :],
                                                         func=AF.Copy), reads=[bkey(bi)], writes=[("osb", i, nh)])
                sch.op("act", lambda e: e.activation(out=sq[:, :], in_=osb[:, :], func=AF.Square,
                                                     accum_out=ss[:, 0:1]),
                       reads=[("osb", i, 0), ("osb", i, 1)], writes=["sq", ("ss", i)])
                sch.op("act", lambda e: e.activation(out=rs[:, 0:1], in_=ss[:, 0:1], func=AF.Sqrt,
                                                     bias=epsT[:, 0:1], scale=1.0 / D),
                       reads=[("ss", i)], writes=[("rs", i)])

            def s2():
                sch.op("dve", lambda e: e.reciprocal(out=rstd[:, 0:1], in_=rs[:, 0:1]), reads=[("rs", i)],
                       writes=[("rstd", i)])
                sch.op("dve", lambda e: e.scalar_tensor_tensor(out=on[:, :], in0=osb[:, :], scalar=rstd[:, 0:1],
                                                               in1=gbc[:, :], op0=ALU.mult, op1=ALU.mult),
                       reads=[("osb", i, 0), ("osb", i, 1), ("rstd", i), "gbc"], writes=[("on", i)])
                sch.op("dve", lambda e: e.tensor_tensor(out=on[:, :], in0=on[:, :], in1=xs[:, :], op=ALU.add),
                       reads=[("on", i), ("xs", ix)], writes=[("on", i)])
                sch.dma("sp", out=out[tt * 128:(tt + 1) * 128, :], in_=on[:, :], reads=[("on", i)],
                        semkey="out%d" % i)
            return s1, s2
        pipeline([o_job(tt) for tt in range(16)], 1)
        sch.barrier()
    return nc


def host_consts():
    bf = ml_dtypes.bfloat16
    c = {}
    c["c_ident"] = np.eye(128, dtype=np.float32).astype(bf)
    perm = np.zeros((128, 128), np.float32)
    for m in range(128):
        d = m % 64
        if d < 8:
            perm[m + 8, m] = 1.0
        elif d < 16:
            perm[m - 8, m] = 1.0
    c["c_perm"] = perm.astype(bf)
    pos = np.arange(S, dtype=np.float32)
    half = 8
    inv = (np.float32(500000.0) ** (-np.arange(half, dtype=np.float32) * np.float32(2.0) / np.float32(16))).astype(np.float32)
    ang = pos[None, :] * inv[:, None]
    cos, sin = np.cos(ang).astype(np.float32), np.sin(ang).astype(np.float32)
    C = np.ones((128, S), np.float32)
    Sg = np.zeros((128, S), np.float32)
    for p in range(128):
        d = p % 64
        if d < 8:
            C[p] = cos[d]
            Sg[p] = -sin[d]
        elif d < 16:
            C[p] = cos[d - 8]
            Sg[p] = sin[d - 8]
    c["c_ropeC"] = C
    c["c_ropeS"] = Sg
    k = np.arange(128)[:, None]
    q = np.arange(128)[None, :]
    mU = (k >= q).astype(np.float32)
    mL = (k <= q).astype(np.float32)
    mB = (np.abs(k - q) <= 64).astype(np.float32)
    mask = np.zeros((128, 3, 2, 2, 128), np.float32)
    for h in range(2):
        mask[:, 0, h, 0] = mU
        mask[:, 0, h, 1] = mL
        mask[:, 1, h, 0] = mU
        mask[:, 1, h, 1] = mL
        mask[:, 2, h, 0] = mB
        mask[:, 2, h, 1] = mB
    c["c_mask"] = mask.astype(bf)
    kc = np.arange(64)
    qc = np.arange(64)
    cstart = np.clip(qc - 8, 0, 48)
    cm = ((kc[:, None] >= cstart[None, :]) & (kc[:, None] < cstart[None, :] + 16)).astype(np.float32)
    c["c_colmask"] = np.tile(np.concatenate([cm, cm], axis=0), (1, 14)).astype(bf)
    return c


def gather_rpb(rpb):
    kc = np.arange(64)
    qc = np.arange(64)
    dc = np.clip(kc[:, None] - qc[None, :], -15, 15) + 15
    outp = np.zeros((2, 64, 8, 14, 64), np.float32)
    for w in range(14):
        d0 = 6 - w
        for i in range(2):
            outp[i, :, :, w, :] = np.transpose(rpb[:, d0 + i + 7][:, dc], (1, 0, 2))
    return np.ascontiguousarray(outp.reshape(128, 8, 14 * 64))


_NC_CACHE = {}


def make_in_maps(x, mem, pre_norm, w_in, merge_bias, na_rpb, mem_norm, w_mem_kv, w_branch_a, w_branch_b,
                 w_branch_c, w_out, post_norm, cores):
    f = lambda a: np.ascontiguousarray(np.asarray(a, dtype=np.float32))
    c = host_consts()
    shared = dict(c)
    shared["w_in"] = f(w_in[0])
    shared["w_mem"] = f(w_mem_kv[0])
    shared["w_br0"] = f(w_branch_a[0])
    shared["w_br1"] = f(w_branch_b[0])
    shared["w_br2"] = f(w_branch_c[0])
    shared["w_out"] = f(w_out[0])
    shared["g_pre"] = f(np.broadcast_to(np.asarray(pre_norm[0])[None, :], (128, D)))
    shared["g_mem"] = f(np.broadcast_to(np.asarray(mem_norm[0])[None, :], (128, D)))
    shared["g_post"] = f(np.broadcast_to(np.asarray(post_norm[0])[None, :], (128, D)))
    shared["mbias"] = f(np.asarray(merge_bias[0]).reshape(3, 8, 128).transpose(2, 0, 1).reshape(128, 24))
    shared["rpbg"] = gather_rpb(np.asarray(na_rpb[0], dtype=np.float32))
    maps = []
    for b in cores:
        m = dict(shared)
        m["x"] = f(x[b])
        m["mem"] = f(mem[b])
        maps.append(m)
    return maps


def kernel(x, mem, pre_norm, w_in, merge_bias, na_rpb, mem_norm, w_mem_kv, w_branch_a, w_branch_b, w_branch_c,
           w_out, post_norm):
    x = np.asarray(x)
    mem = np.asarray(mem)
    nc = build()
    maps = make_in_maps(x, mem, pre_norm, w_in, merge_bias, na_rpb, mem_norm, w_mem_kv, w_branch_a, w_branch_b,
                        w_branch_c, w_out, post_norm, list(range(8)))
    res = run_bass_kernel_spmd(nc, maps, core_ids=list(range(8)))
    return np.stack([np.asarray(r["out"], dtype=np.float32) for r in res.results], axis=0)
```
